# Optimizing a Trainium2 kernel written in Bass

```python
import jax, jax.numpy as jnp
from jax import lax
import numpy as np

D_MODEL = 2048
BATCH = 1
SEQ = 8192
DEPTH = 1

POOL_WIDTH = D_MODEL // 2
POOL_WINDOWS = (2, 4, 8, 16)
N_POOL_GROUPS = len(POOL_WINDOWS)
POOL_GROUP = POOL_WIDTH // N_POOL_GROUPS
N_Q_HEADS = 16
N_KV_HEADS = 4
GROUP = N_Q_HEADS // N_KV_HEADS
HEAD_DIM = D_MODEL // N_Q_HEADS
NSA_WIDTH = N_Q_HEADS * HEAD_DIM
KV_WIDTH = N_KV_HEADS * HEAD_DIM
CMP_BLOCK = 32
CMP_STRIDE = 16
CMP_HIDDEN = 2 * HEAD_DIM
SEL_BLOCK = 64
N_SEL = 16
WINDOW = 512
Q_BLOCK = 128
ROPE_THETA = 10000.0
EPS = 1e-6
NEG = -1e30
FORCE = 1e6
N_BRANCHES = 2
IN_SPLITS = (POOL_WIDTH, POOL_WIDTH, NSA_WIDTH, KV_WIDTH, KV_WIDTH, KV_WIDTH, KV_WIDTH,
             KV_WIDTH, KV_WIDTH, NSA_WIDTH, 3 * N_Q_HEADS)
D_IN = sum(IN_SPLITS)

kernel_name = "hybrid_pool_nsa_gated_block"


def rmsnorm(x, w):
    xf = x.astype(jnp.float32)
    y = xf * lax.rsqrt(jnp.mean(xf * xf, axis=-1, keepdims=True) + EPS)
    return (y * w.astype(jnp.float32)).astype(x.dtype)


def rope(x, pos):
    half = HEAD_DIM // 2
    inv = ROPE_THETA ** (-jnp.arange(half, dtype=jnp.float32) / half)
    ang = pos.astype(jnp.float32)[:, None] * inv[None, :]
    cos = jnp.cos(ang)[None, :, None, :]
    sin = jnp.sin(ang)[None, :, None, :]
    xf = x.astype(jnp.float32)
    x1, x2 = xf[..., :half], xf[..., half:]
    return jnp.concatenate([x1 * cos - x2 * sin, x2 * cos + x1 * sin], axis=-1).astype(x.dtype)


def pool_mixer(u, mix, scale):
    b, s, _ = u.shape
    ug = u.reshape(b, s, N_POOL_GROUPS, POOL_GROUP)
    uf = ug.astype(jnp.float32)
    c = jnp.pad(jnp.cumsum(uf, axis=1), ((0, 0), (1, 0), (0, 0), (0, 0)))
    t = np.arange(s)
    win = np.array(POOL_WINDOWS)
    lo = np.maximum(t[:, None] + 1 - win[None, :], 0)
    cnt = jnp.asarray(np.minimum(t[:, None] + 1, win[None, :]).astype(np.float32))
    c_lo = c[:, lo, np.arange(N_POOL_GROUPS)[None, :]]
    mean = (c[:, 1:] - c_lo) / cnt[None, :, :, None]
    pooled = (mean - uf).astype(u.dtype)
    z = jnp.einsum("bsgc,gcd->bsgd", pooled, mix).reshape(b, s, POOL_WIDTH)
    return z * scale


def compress(kraw, pe, w1, w2):
    s = kraw.shape[1]
    n_cmp = (s - CMP_BLOCK) // CMP_STRIDE + 1
    idx = np.arange(n_cmp)[:, None] * CMP_STRIDE + np.arange(CMP_BLOCK)[None, :]
    kb = kraw[:, idx] + pe[None, None, :, None, :]
    hdn = jax.nn.silu(jnp.einsum("bnlhd,ldf->bnhf", kb, w1))
    return jnp.einsum("bnhf,fe->bnhe", hdn, w2)


def nsa_attention(q, kc, vc, ks, vs, kw, vw, gates):
    b, s = q.shape[0], q.shape[1]
    n_cmp = kc.shape[1]
    n_blk = s // SEL_BLOCK
    n_sel = min(N_SEL, n_blk)
    scale = HEAD_DIM ** -0.5
    cmp_end = jnp.arange(n_cmp) * CMP_STRIDE + CMP_BLOCK - 1
    ratio = SEL_BLOCK // CMP_STRIDE
    sub = CMP_BLOCK // CMP_STRIDE
    sel_idx = (np.arange(n_blk)[:, None, None] * ratio + np.arange(ratio)[None, :, None]
               - np.arange(sub)[None, None, :]).reshape(n_blk, ratio * sub)
    sel_valid = jnp.asarray((sel_idx >= 0) & (sel_idx < n_cmp))
    sel_idx = np.clip(sel_idx, 0, n_cmp - 1)
    ks_blk = ks.reshape(b, n_blk, SEL_BLOCK, N_KV_HEADS, HEAD_DIM).transpose(0, 3, 1, 2, 4)
    vs_blk = vs.reshape(b, n_blk, SEL_BLOCK, N_KV_HEADS, HEAD_DIM).transpose(0, 3, 1, 2, 4)
    kw_pad = jnp.pad(kw, ((0, 0), (WINDOW, 0), (0, 0), (0, 0)))
    vw_pad = jnp.pad(vw, ((0, 0), (WINDOW, 0), (0, 0), (0, 0)))
    gather = jax.vmap(jax.vmap(lambda kt, ix: kt[ix]))
    jblk = jnp.arange(n_blk)

    def block(i):
        s0 = i * Q_BLOCK
        qb = lax.dynamic_slice_in_dim(q, s0, Q_BLOCK, axis=1)
        gb = lax.dynamic_slice_in_dim(gates, s0, Q_BLOCK, axis=1)
        tpos = s0 + jnp.arange(Q_BLOCK)
        sc = jnp.einsum("bqhgd,bnhd->bhgqn", qb, kc, preferred_element_type=jnp.float32) * scale
        cvalid = cmp_end[None, :] <= tpos[:, None]
        p_c = jnp.where(cvalid, jax.nn.softmax(jnp.where(cvalid, sc, NEG), axis=-1), 0.0)
        o_c = jnp.einsum("bhgqn,bnhe->bqhge", p_c.astype(vc.dtype), vc)
        imp_c = p_c.sum(axis=2)
        imp = jnp.where(sel_valid, imp_c[..., sel_idx], 0.0).sum(-1)
        jt = (tpos // SEL_BLOCK)[:, None]
        forced = (jblk[None, :] == 0) | (jblk[None, :] == jt) | (jblk[None, :] == jt - 1)
        imp = jnp.where(forced, FORCE, imp)
        imp = jnp.where(jblk[None, :] > jt, NEG, imp)
        _, top = lax.top_k(imp, n_sel)
        kg = gather(ks_blk, top)
        vg = gather(vs_blk, top)
        kpos = top[..., None] * SEL_BLOCK + jnp.arange(SEL_BLOCK)
        svalid = kpos <= tpos[None, None, :, None, None]
        ss = jnp.einsum("bqhgd,bhqnld->bhgqnl", qb, kg, preferred_element_type=jnp.float32) * scale
        ss = jnp.where(svalid[:, :, None], ss, NEG)
        p_s = jax.nn.softmax(ss.reshape(ss.shape[:4] + (-1,)), axis=-1).reshape(ss.shape)
        o_s = jnp.einsum("bhgqnl,bhqnle->bqhge", p_s.astype(vg.dtype), vg)
        kwb = lax.dynamic_slice_in_dim(kw_pad, s0, WINDOW + Q_BLOCK, axis=1)
        vwb = lax.dynamic_slice_in_dim(vw_pad, s0, WINDOW + Q_BLOCK, axis=1)
        wpos = s0 - WINDOW + jnp.arange(WINDOW + Q_BLOCK)
        diff = tpos[:, None] - wpos[None, :]
        wvalid = (diff >= 0) & (diff < WINDOW) & (wpos[None, :] >= 0)
        sw = jnp.einsum("bqhgd,bkhd->bhgqk", qb, kwb, preferred_element_type=jnp.float32) * scale
        p_w = jax.nn.softmax(jnp.where(wvalid, sw, NEG), axis=-1)
        o_w = jnp.einsum("bhgqk,bkhe->bqhge", p_w.astype(vwb.dtype), vwb)
        return gb[..., 0:1] * o_c + gb[..., 1:2] * o_s + gb[..., 2:3] * o_w

    out = lax.map(block, jnp.arange(s // Q_BLOCK))
    return out.transpose(1, 0, 2, 3, 4, 5).reshape(b, s, NSA_WIDTH)


def setup_inputs(seed: int = 0) -> dict:
    key = jax.random.key(seed)
    ks = jax.random.split(key, 20)
    f32 = jnp.float32
    nrm = lambda k, shape, fan: jax.random.normal(k, shape, f32) * (fan ** -0.5)
    L = DEPTH
    return {
        "x": jax.random.normal(ks[0], (BATCH, SEQ, D_MODEL), f32),
        "norm_w": 1.0 + 0.05 * jax.random.normal(ks[1], (L, D_MODEL), f32),
        "w_in": nrm(ks[2], (L, D_MODEL, D_IN), D_MODEL),
        "pool_mix": nrm(ks[3], (L, N_POOL_GROUPS, POOL_GROUP, POOL_GROUP), POOL_GROUP),
        "pool_scale": 1.0 + 0.1 * jax.random.normal(ks[4], (L, POOL_WIDTH), f32),
        "cmp_pe_k": 0.1 * jax.random.normal(ks[5], (L, CMP_BLOCK, HEAD_DIM), f32),
        "cmp_w1_k": nrm(ks[6], (L, CMP_BLOCK, HEAD_DIM, CMP_HIDDEN), CMP_BLOCK * HEAD_DIM),
        "cmp_w2_k": nrm(ks[7], (L, CMP_HIDDEN, HEAD_DIM), CMP_HIDDEN),
        "cmp_pe_v": 0.1 * jax.random.normal(ks[8], (L, CMP_BLOCK, HEAD_DIM), f32),
        "cmp_w1_v": nrm(ks[9], (L, CMP_BLOCK, HEAD_DIM, CMP_HIDDEN), CMP_BLOCK * HEAD_DIM),
        "cmp_w2_v": nrm(ks[10], (L, CMP_HIDDEN, HEAD_DIM), CMP_HIDDEN),
        "w_pool_out": nrm(ks[11], (L, POOL_WIDTH, D_MODEL), POOL_WIDTH),
        "w_nsa_out": nrm(ks[12], (L, NSA_WIDTH, D_MODEL), NSA_WIDTH),
        "w_merge": nrm(ks[13], (L, D_MODEL, N_BRANCHES * D_MODEL), D_MODEL),
        "b_merge": 0.01 * jax.random.normal(ks[14], (L, N_BRANCHES * D_MODEL), f32),
        "w_out": nrm(ks[15], (L, D_MODEL, D_MODEL), D_MODEL),
        "final_norm_w": 1.0 + 0.05 * jax.random.normal(ks[16], (D_MODEL,), f32),
    }


def reference(x, norm_w, w_in, pool_mix, pool_scale, cmp_pe_k, cmp_w1_k, cmp_w2_k,
              cmp_pe_v, cmp_w1_v, cmp_w2_v, w_pool_out, w_nsa_out, w_merge, b_merge,
              w_out, final_norm_w):
    b, s, _ = x.shape
    pos = jnp.arange(s)
    n_cmp = (s - CMP_BLOCK) // CMP_STRIDE + 1
    cmp_pos = jnp.arange(n_cmp) * CMP_STRIDE + CMP_BLOCK - 1
    split_at = np.cumsum(IN_SPLITS)[:-1].tolist()
    for layer in range(DEPTH):
        h = rmsnorm(x, norm_w[layer])
        proj = h @ w_in[layer]
        (u_pool, g_pool, q, kc_raw, vc_raw, k_sel, v_sel, k_win, v_win,
         g_nsa, g_br) = jnp.split(proj, split_at, axis=-1)
        y_pool = pool_mixer(u_pool, pool_mix[layer], pool_scale[layer]) * jax.nn.silu(g_pool)
        y_a = y_pool @ w_pool_out[layer]
        heads = lambda t, n: t.reshape(b, s, n, HEAD_DIM)
        qh = rope(heads(q, N_Q_HEADS), pos).reshape(b, s, N_KV_HEADS, GROUP, HEAD_DIM)
        kc = rope(compress(heads(kc_raw, N_KV_HEADS), cmp_pe_k[layer], cmp_w1_k[layer],
                           cmp_w2_k[layer]), cmp_pos)
        vc = compress(heads(vc_raw, N_KV_HEADS), cmp_pe_v[layer], cmp_w1_v[layer], cmp_w2_v[layer])
        ksh = rope(heads(k_sel, N_KV_HEADS), pos)
        kwh = rope(heads(k_win, N_KV_HEADS), pos)
        gates = jax.nn.sigmoid(g_br).reshape(b, s, N_KV_HEADS, GROUP, 3)
        o_nsa = nsa_attention(qh, kc, vc, ksh, heads(v_sel, N_KV_HEADS), kwh,
                              heads(v_win, N_KV_HEADS), gates)
        y_b = (o_nsa * jax.nn.silu(g_nsa)) @ w_nsa_out[layer]
        gm = jax.nn.sigmoid(h @ w_merge[layer] + b_merge[layer]).reshape(b, s, N_BRANCHES, D_MODEL)
        merged = gm[:, :, 0] * y_a + gm[:, :, 1] * y_b
        x = x + merged @ w_out[layer]
    return rmsnorm(x, final_norm_w)
```

```python
import numpy as np
import ml_dtypes
from contextlib import ExitStack
import concourse.bass as bass
import concourse.mybir as mybir
from concourse.bass_utils import run_bass_kernel_spmd

F32 = mybir.dt.float32
BF16 = mybir.dt.bfloat16
ALU = mybir.AluOpType
AF = mybir.ActivationFunctionType

S = 8192
D = 2048
NC = 8
NS = 8
TW = 160
NT = NS * TW
DIN = 9264
NEGB = -30000.0
SCALE = 128 ** -0.5
EPS = 1e-6

ENGS = ["tensor", "vector", "scalar", "gpsimd", "sync"]


class Buf:
    __slots__ = ("w", "r", "name")

    def __init__(self, name=""):
        self.w = None
        self.r = []
        self.name = name


class Sched:
    def __init__(self, nc, stack):
        self.nc = nc
        self.stack = stack
        self.ops = {e: [] for e in ENGS}
        self.sem = {e: stack.enter_context(nc.semaphore("sem_" + e)) for e in ENGS}
        self.cnt = {e: 0 for e in ENGS}
        self.opidx = {e: 0 for e in ENGS}
        self.seen = {e: {} for e in ENGS}
        self.dsem = {}
        self.dcnt = {}
        self.widx = {}

    def _wait(self, e, ev):
        key, val = ev
        if self.seen[e].get(key, 0) >= val:
            return
        self.seen[e][key] = val
        self.ops[e].append(("wait", key, val))

    def _deps(self, e, reads, writes):
        for b in reads:
            if b.w is not None:
                if b.w[0] == ("E", e):
                    if e == "tensor":
                        continue
                    if self.opidx[e] - self.widx.get(id(b), -10) >= 3:
                        continue
                self._wait(e, b.w)
        for b in writes:
            if b.w is not None and b.w[0] != ("E", e):
                self._wait(e, b.w)
            for r in b.r:
                if r[0] != ("E", e):
                    self._wait(e, r)

    def op(self, e, fn, reads=(), writes=(), inc=True):
        if e != "tensor":
            pr = [b for b in reads if b.name.startswith("ps")]
            if pr:
                reads = [b for b in reads if not b.name.startswith("ps")]
                writes = list(writes) + pr
        self._deps(e, reads, writes)
        ev = (("E", e), self.cnt[e] + 1)
        if inc:
            self.cnt[e] += 1
        self.ops[e].append(("op", fn, inc, ev[1]))
        for b in reads:
            b.r.append(ev)
        for b in writes:
            b.w = ev
            b.r = []
            self.widx[id(b)] = self.opidx[e]
        self.opidx[e] += 1

    def dma(self, q, out, in_, reads=(), writes=(), slot=None, in_fn=None, **kw):
        if slot not in self.dsem:
            self.dsem[slot] = self.stack.enter_context(self.nc.semaphore("d_" + slot))
            self.dcnt[slot] = 0
        self._deps(q, reads, writes)
        self.dcnt[slot] += 16
        ev = (("D", slot), self.dcnt[slot])
        sem = self.dsem[slot]
        if in_fn is not None:
            self.ops[q].append(("raw", lambda eng, out=out, in_fn=in_fn, sem=sem, kw=kw: eng.dma_start(
                out=out, in_=in_fn(eng), **kw).then_inc(sem, 16)))
        else:
            self.ops[q].append(("raw", lambda eng, out=out, in_=in_, sem=sem, kw=kw: eng.dma_start(
                out=out, in_=in_, **kw).then_inc(sem, 16)))
        for b in reads:
            b.r.append(ev)
        for b in writes:
            b.w = ev
            b.r = []

    def raw(self, e, fn, reads=(), writes=(), ev=None):
        self._deps(e, reads, writes)
        self.ops[e].append(("raw", fn))
        for b in reads:
            b.r.append(ev)
        for b in writes:
            b.w = ev
            b.r = []

    def barrier(self, bufs=()):
        for e in ENGS:
            for o in ENGS:
                if o != e and o != "sync" and self.cnt[o] > 0:
                    self._wait(e, (("E", o), self.cnt[o]))
            for s, v in self.dcnt.items():
                if v > 0:
                    self._wait(e, (("D", s), v))

    def emit(self, block):
        ops = self.ops
        needed = {e: set() for e in ENGS}
        for e in ENGS:
            for rec in ops[e]:
                if rec[0] == "wait" and rec[1][0] == "E":
                    needed[rec[1][1]].add(rec[2])
        remap = {e: {v: i + 1 for i, v in enumerate(sorted(needed[e]))} for e in ENGS}
        sems, dsems = self.sem, self.dsem

        def run(e, eng):
            for rec in ops[e]:
                if rec[0] == "wait":
                    key, val = rec[1], rec[2]
                    if key[0] == "E":
                        eng.wait_ge(sems[key[1]], remap[key[1]][val])
                    else:
                        eng.wait_ge(dsems[key[1]], val)
                elif rec[0] == "op":
                    _, fn, inc, val = rec
                    if inc and val in needed[e]:
                        fn(eng).then_inc(sems[e], 1)
                    else:
                        fn(eng)
                else:
                    rec[1](eng)

        @block.tensor
        def _(eng):
            run("tensor", eng)

        @block.vector
        def _(eng):
            run("vector", eng)

        @block.scalar
        def _(eng):
            run("scalar", eng)

        @block.gpsimd
        def _(eng):
            run("gpsimd", eng)

        @block.sync
        def _(eng):
            run("sync", eng)


class Arena:
    def __init__(self, tensor, base, size):
        self.t = tensor
        self.base = base
        self.size = size
        self.off = 0

    def sub(self, size):
        a = Arena(self.t, self.base + self.off, size)
        self.off += size
        assert self.off <= self.size, (self.off, self.size)
        return a

    def reset(self):
        self.off = 0

    def alloc(self, shape, dtype):
        es = 2 if dtype == BF16 else 4
        n = int(np.prod(shape))
        nbytes = (n * es + 63) // 64 * 64
        o = self.base + self.off
        self.off += nbytes
        assert self.off <= self.size, ("arena overflow", self.off, self.size)
        ap = self.t[:, o // 2: o // 2 + n * es // 2]
        if dtype != BF16:
            ap = ap.bitcast(dtype)
        if len(shape) == 2:
            ap = ap.rearrange("p (a b) -> p a b", a=shape[0])
        elif len(shape) == 3:
            ap = ap.rearrange("p (a b c) -> p a b c", a=shape[0], b=shape[1])
        elif len(shape) == 4:
            ap = ap.rearrange("p (a b c d) -> p a b c d", a=shape[0], b=shape[1], c=shape[2])
        return ap


def build(stage=99, nh=4):
    nc = bass.Bass("TRN2", target_bir_lowering=False)
    dt = nc.dram_tensor

    def ext_in(name, shape, dtype=F32):
        return dt(name, list(shape), dtype, kind="ExternalInput").ap()

    def ext_out(name, shape, dtype=F32):
        return dt(name, list(shape), dtype, kind="ExternalOutput").ap()

    xT_d = ext_in("xT", [D, NT])
    xtok_d = ext_in("xtok", [NS * 128, D])
    normw_d = ext_in("normw", [128, 16])
    w_in_d = ext_in("w_in", [D, DIN])
    mix_d = ext_in("pool_mix", [4, 256, 256])
    pscale_d = ext_in("pscale", [128, 8])
    pek_d = ext_in("pe_kT", [128, 32])
    pev_d = ext_in("pe_vT", [128, 32])
    w1k_d = ext_in("w1k", [32, 128, 256])
    w1v_d = ext_in("w1v", [32, 128, 256])
    w2k_d = ext_in("w2k", [256, 128])
    w2v_d = ext_in("w2v", [256, 128])
    wpo_d = ext_in("w_pool_out", [1024, D])
    wno_d = ext_in("w_nsa_out", [D, D])
    wmg_d = ext_in("w_merge", [D, 2 * D])
    bmg_d = ext_in("bmerge", [128, 32])
    wout_d = ext_in("w_out", [D, D])
    fnw_d = ext_in("fnw", [128, D])
    ropecs_d = ext_in("ropecs", [128, NS * 2 * 64])
    ropecc_d = ext_in("ropecc", [128, 2 * 64])
    invc_d = ext_in("invc", [128, 4 * 16])
    ident_d = ext_in("ident", [128, 128], BF16)
    emat_d = ext_in("emat", [128, S], BF16)
    selcm_d = ext_in("selcm", [128, 8 * 512], BF16)
    wm_d = ext_in("wm", [128, 7 * 512], BF16)
    cmpm_d = ext_in("cmpm", [128, NS * 4 * 512], BF16)
    amat_d = ext_in("amat", [128, 4 * 128], BF16)
    fmm_d = ext_in("fmm", [128, NS * 2 * 128])
    out_d = ext_out("out", [NS * 128, D])

    kselT_loc = dt("kselT_loc", [128, 4096], BF16)
    kwinT_loc = dt("kwinT_loc", [NS * 128, 512], BF16)
    vsel_loc = dt("vsel_loc", [128, 4160], BF16)
    vwin_loc = dt("vwin_loc", [NS * 128, 512], BF16)
    kcT_loc = dt("kcT_loc", [128, 256], BF16)
    vc_loc = dt("vc_loc", [256, 128], BF16)
    kselT_all = dt("kselT_all", [NC * 128, 4096], BF16)
    kwinT_all = dt("kwinT_all", [(32 + NC * NS) * 128, 512], BF16)
    vsel_all = dt("vsel_all", [NC * 128, 4160], BF16)
    vwin_all = dt("vwin_all", [(32 + NC * NS) * 128, 512], BF16)
    kcT_all = dt("kcT_all", [NC * 128, 256], BF16)
    vc_all = dt("vc_all", [NC * 256, 128], BF16)
    hT_sp = dt("hT_sp", [128, 16 * NS * 128], BF16)
    yp_sp = dt("yp_sp", [128, 8 * 1024], BF16)

    dbg = {}
    if stage < 99:
        dbg["QT"] = ext_out("dbg_QT", [128, 16 * 1024], BF16)
        dbg["sgn"] = ext_out("dbg_sgn", [128, NS * 2048], BF16)
        dbg["gates"] = ext_out("dbg_gates", [128, NS * 48])
        dbg["ypool"] = ext_out("dbg_ypool", [128, 8 * 1024], BF16)
        dbg["kselT"] = ext_out("dbg_kselT", [128, 4096], BF16)
        dbg["kwinT"] = ext_out("dbg_kwinT", [NS * 128, 512], BF16)
        dbg["vsel"] = ext_out("dbg_vsel", [128, 4160], BF16)
        dbg["vwin"] = ext_out("dbg_vwin", [NS * 128, 512], BF16)
        dbg["kcT"] = ext_out("dbg_kcT", [128, 256], BF16)
        dbg["vc"] = ext_out("dbg_vc", [256, 128], BF16)
        dbg["og"] = ext_out("dbg_og", [128, NS * 2048], BF16)

    stack = ExitStack()
    with stack:
        ARENA_BYTES = 204 * 1024
        arena_t = stack.enter_context(nc.sbuf_tensor("arena", [128, ARENA_BYTES // 2], BF16))
        ps = [stack.enter_context(nc.psum_tensor("ps%d" % i, [128, 512], F32))[:] for i in range(8)]
        psb = [Buf("ps%d" % i) for i in range(8)]
        sc = Sched(nc, stack)
        block = stack.enter_context(nc.Block())

        top = Arena(arena_t, 0, ARENA_BYTES)
        ident = top.alloc([128], BF16)
        ones = top.alloc([128], BF16)
        normw = top.alloc([16], F32)
        pscale = top.alloc([8], F32)
        bmerge = top.alloc([32], F32)
        gates = top.alloc([NS, 48], F32)
        b_const = Buf("const")
        b_gates = Buf("gates")
        b_pay = Buf("pay")
        qs = top.sub(64 * 1024)
        ph = top.sub(ARENA_BYTES - top.off)

        QT = qs.alloc([16, 1024], BF16)
        sgn = qs.alloc([NS, 2048], BF16)
        b_QT = Buf("QT")
        b_sgn = [Buf("sgn%d" % k) for k in range(NS)]

        V = "vector"
        A = "scalar"
        G = "gpsimd"
        T = "tensor"
        SY = "sync"

        def mm(out, lhsT, rhs, start, stop, reads, writes, inc=None, sgc=False):
            if inc is None:
                inc = stop
            if sgc:
                sc.op(T, lambda e: e.matmul(out, lhsT, rhs, start=start, stop=stop, skip_group_check=True),
                      reads, writes, inc=inc)
            else:
                sc.op(T, lambda e: e.matmul(out, lhsT, rhs, start=start, stop=stop), reads, writes, inc=inc)

        def transp(out, in_, reads, writes):
            sc.op(T, lambda e: e.transpose(out, in_, ident), reads + [b_const], writes)

        sc.dma(SY, ident, ident_d, writes=[b_const], slot="const")
        sc.dma(SY, normw, normw_d, writes=[b_const], slot="const")
        sc.dma(SY, pscale, pscale_d, writes=[b_const], slot="const")
        sc.dma(SY, bmerge, bmg_d, writes=[b_const], slot="const")
        sc.op(V, lambda e: e.memset(ones, 1.0), writes=[b_const])

        hT = ph.alloc([16, NT], BF16)
        b_hT = Buf("hT")
        ypool = ph.alloc([8, 1024], BF16)
        b_ypool = Buf("ypool")
        NWB = 2
        wbuf = [ph.alloc([16, 512], BF16) for _ in range(NWB)]
        b_wbuf = [Buf("wbuf%d" % i) for i in range(NWB)]
        ropecs = ph.alloc([NS, 2, 64], F32)
        ropecc = ph.alloc([2, 64], F32)
        invc = ph.alloc([4, 16], F32)
        sc.dma(SY, ropecs, ropecs_d.rearrange("p (k a f) -> p k a f", k=NS, a=2), writes=[b_const], slot="const")
        sc.dma(SY, ropecc, ropecc_d.rearrange("p (a f) -> p a f", a=2), writes=[b_const], slot="const")
        sc.dma(SY, invc, invc_d.rearrange("p (g t) -> p g t", g=4), writes=[b_const], slot="const")
        mixw = ph.alloc([4, 2, 256], BF16)
        w2 = ph.alloc([2, 2, 128], BF16)
        p1mark = ph.off

        xbuf = [ph.alloc([NT], F32) for _ in range(2)]
        b_xbuf = [Buf("xbuf0"), Buf("xbuf1")]
        sq = [ph.alloc([NT], BF16) for _ in range(2)]
        b_sq = [Buf("sq0"), Buf("sq1")]
        rstd = ph.alloc([NT], F32)
        b_rstd = Buf("rstd")
        xT_v = xT_d.rearrange("(c p) n -> c p n", p=128)
        segs = [(0, 512), (512, 1024), (1024, 1280)]
        for c in range(16):
            j = c % 2
            sc.dma(SY, xbuf[j], xT_v[c], writes=[b_xbuf[j]], slot="xbuf%d" % j)
            sc.op(A, lambda e, j=j: e.activation(sq[j], xbuf[j], AF.Square), [b_xbuf[j]], [b_sq[j]])
            for si, (a, b) in enumerate(segs):
                mm(ps[si][:, 0:b - a], ones, sq[j][:, a:b], c == 0, c == 15, [b_sq[j], b_const], [psb[si]],
                   inc=(si == 2))
        for si, (a, b) in enumerate(segs):
            sc.op(A, lambda e, si=si, a=a, b=b: e.activation(rstd[:, a:b], ps[si][:, 0:b - a], AF.Sqrt,
                                                               bias=EPS, scale=1.0 / D), [psb[si]], [b_rstd])
        sc.op(V, lambda e: e.reciprocal(rstd, rstd), [b_rstd], [b_rstd])
        for c in range(16):
            j = c % 2
            sc.dma(SY, xbuf[j], xT_v[c], writes=[b_xbuf[j]], slot="xbuf%d" % j)
            sc.op(V, lambda e, c=c, j=j: e.scalar_tensor_tensor(hT[:, c, :], xbuf[j], normw[:, c:c + 1], rstd,
                                                                 ALU.mult, ALU.mult),
                  [b_xbuf[j], b_rstd, b_const], [b_hT])
        ph.off = p1mark

        wcount = [0]

        def load_w(src_ap_cols, kchunks=16, ncols=512):
            i = wcount[0] % NWB
            wcount[0] += 1
            dst = wbuf[i][:, 0:kchunks, 0:ncols]
            sc.dma(G, dst, src_ap_cols.rearrange("(c p) n -> p c n", p=128), writes=[b_wbuf[i]], slot="wbuf%d" % i)
            return wbuf[i], b_wbuf[i]

        qs.reset()
        sgpool = qs.alloc([8, 1024], BF16)
        b_sgpool = Buf("sgpool")
        kcraw = qs.alloc([4, NT], BF16)
        vcraw = qs.alloc([4, NT], BF16)
        b_raw = {"k": Buf("kcraw"), "v": Buf("vcraw")}
        uT = qs.alloc([NT], F32)
        sA = qs.alloc([NT], F32)
        sB = qs.alloc([NT], F32)
        b_uT, b_sA, b_sB = Buf("uT"), Buf("sA"), Buf("sB")
        pooled = qs.alloc([2, 1024], BF16)
        b_pooled = [Buf("pooled0"), Buf("pooled1")]
        pfix = qs.alloc([16], F32)
        b_pfix = Buf("pfix")
        b_mixw = Buf("mixw")
        hdn = qs.alloc([2, 256], BF16)
        b_hdn = Buf("hdn")
        cbias = qs.alloc([2], F32)
        b_cbias = Buf("cbias")
        peT = qs.alloc([2, 32], BF16)
        b_peT = Buf("peT")
        b_w2 = Buf("w2")
        kcr = qs.alloc([2, 128], BF16)
        b_kcr = Buf("kcr")
        ctmp = qs.alloc([4, 128], F32)
        b_ctmp = Buf("ctmp")
        kcTs = qs.alloc([256], BF16)
        b_kcTs = Buf("kcTs")
        vcs = qs.alloc([2, 128], BF16)
        b_vcs = Buf("vcs")

        sc.dma(G, mixw, mix_d.rearrange("g (cc p) d -> p g cc d", p=128), writes=[b_mixw], slot="mixw")
        sc.dma(G, peT[:, 0, :], pek_d, writes=[b_peT], slot="misc_pe")
        sc.dma(G, peT[:, 1, :], pev_d, writes=[b_peT], slot="misc_pe")
        sc.dma(G, w2[:, 0], w2k_d.rearrange("(fb p) e -> p fb e", p=128), writes=[b_w2], slot="misc_w2")
        sc.dma(G, w2[:, 1], w2v_d.rearrange("(fb p) e -> p fb e", p=128), writes=[b_w2], slot="misc_w2")

        def fm_proj(col0, evac):
            wb, bwb = load_w(w_in_d[:, col0:col0 + 512])
            for j in range(4):
                base = 3 * (fm_proj.n % 2)
                fm_proj.n += 1
                for c in range(16):
                    for si, (a, b) in enumerate(segs):
                        mm(ps[base + si][:, 0:b - a], wb[:, c, j * 128:(j + 1) * 128], hT[:, c, a:b],
                           c == 0, c == 15, [bwb, b_hT], [psb[base + si]], inc=(c == 15 and si == 2))
                evac(j, base)
        fm_proj.n = 0

        xs = ph.alloc([512], F32)
        b_xs = Buf("xs")
        rt = ph.alloc([4, 256], F32)
        b_rt = Buf("rt")
        qr = [ph.alloc([4, 128], BF16) for _ in range(2)]
        b_qr = [Buf("qr0"), Buf("qr1")]
        kst = ph.alloc([NS, 512], BF16)
        b_kst = Buf("kst")
        kstK = ph.alloc([4, NS, 128], BF16)
        b_kstK = Buf("kstK")
        kstV = ph.alloc([4, NS, 130], BF16)
        b_kstV = Buf("kstV")
        sc.op(V, lambda e: e.memset(kstV[:, :, :, 128:130], 1.0), writes=[b_kstV])
        tmcount = [0]

        def tm_proj(col0, ncols, evac):
            wb, bwb = load_w(w_in_d[:, col0:col0 + ncols], ncols=ncols)
            pending = None
            for k in range(NS):
                pb = tmcount[0] % 4
                tmcount[0] += 1
                t0 = k * TW + 16
                for c in range(16):
                    mm(ps[pb][:, 0:ncols], hT[:, c, t0:t0 + 128], wb[:, c, 0:ncols], c == 0, c == 15,
                       [bwb, b_hT], [psb[pb]])
                if pending is not None:
                    pending()
                pending = evac(k, pb)
            if pending is not None:
                pending()

        def rope_tm(pb, k, dst_fn, bdst):
            sc.op(A, lambda e: e.activation(xs, ps[pb], AF.Copy), [psb[pb]], [b_xs])
            x4 = xs.rearrange("p (h d) -> p h d", h=4)
            x1, x2 = x4[:, :, 0:64], x4[:, :, 64:128]
            cs = ropecs[:, k, 0, :].unsqueeze(1).broadcast_to([128, 4, 64])
            sn = ropecs[:, k, 1, :].unsqueeze(1).broadcast_to([128, 4, 64])
            r4 = rt.rearrange("p a (h d) -> p a h d", h=4)
            sc.op(V, lambda e: e.tensor_tensor(r4[:, 0], x1, cs, ALU.mult), [b_xs, b_const], [b_rt])
            sc.op(V, lambda e: e.tensor_tensor(r4[:, 1], x2, sn, ALU.mult), [b_xs, b_const], [b_rt])
            sc.op(V, lambda e: e.tensor_tensor(r4[:, 2], x2, cs, ALU.mult), [b_xs, b_const], [b_rt])
            sc.op(V, lambda e: e.tensor_tensor(r4[:, 3], x1, sn, ALU.mult), [b_xs, b_const], [b_rt])
            qi = rope_tm.n % 2
            qrb, bq = qr[qi], b_qr[qi]
            sc.op(V, lambda e: e.tensor_tensor(qrb[:, :, 0:64], r4[:, 0], r4[:, 1], ALU.subtract), [b_rt], [bq])
            sc.op(V, lambda e: e.tensor_tensor(qrb[:, :, 64:128], r4[:, 2], r4[:, 3], ALU.add), [b_rt], [bq])
            pt = 4 + (rope_tm.n % 2)
            rope_tm.n += 1
            ptv = ps[pt].bitcast(BF16)

            def part2():
                for hh in range(4):
                    transp(ptv[:, hh * 128:(hh + 1) * 128], qrb[:, hh, :], [bq], [psb[pt]])
                sc.op(A, lambda e, ptv=ptv: e.activation(dst_fn(), ptv[:, 0:512].rearrange("p (h t) -> p h t", h=4),
                                                         AF.Copy), [psb[pt]], [bdst])
            return part2
        rope_tm.n = 0

        def evac_raw(which):
            dst = kcraw if which == "k" else vcraw

            def f(j, base):
                for si, (a, b) in enumerate(segs):
                    sc.op(A if si != 1 else V, (lambda e, si=si, a=a, b=b: e.activation(
                        dst[:, j, a:b], ps[base + si][:, 0:b - a], AF.Copy)) if si != 1 else
                        (lambda e, si=si, a=a, b=b: e.tensor_copy(dst[:, j, a:b], ps[base + si][:, 0:b - a])),
                        [psb[base + si]], [b_raw[which]])
            return f

        fm_proj(4096, evac_raw("k"))
        fm_proj(4608, evac_raw("v"))

        for wi, which in enumerate(["k", "v"]):
            raw = (kcraw if which == "k" else vcraw).rearrange("p h (k t) -> p h k t", k=NS)
            w1d = w1k_d if which == "k" else w1v_d
            i = wcount[0] % NWB
            wcount[0] += 1
            w1 = wbuf[i].rearrange("p a b -> p (a b)").rearrange("p (l f) -> p l f", l=32)
            bw1 = b_wbuf[i]
            sc.dma(G, w1, w1d.rearrange("l d f -> d l f"), writes=[bw1], slot="wbuf%d" % i)
            for fb in range(2):
                pb = fb
                for l in range(32):
                    mm(ps[pb][:, 0:256].rearrange("p (h k n) -> p h k n", h=4, k=NS),
                       w1[:, l, fb * 128:(fb + 1) * 128], raw[:, :, :, l + 16:l + 16 + 113:16],
                       l == 0, l == 31, [bw1, b_raw[which]], [psb[pb]])
                for l in range(32):
                    mm(ps[pb][:, 256:257], w1[:, l, fb * 128:(fb + 1) * 128], peT[:, wi, l:l + 1],
                       False, l == 31, [bw1, b_peT], [psb[pb]], sgc=True)
                sc.op(V, lambda e, pb=pb, fb=fb: e.tensor_copy(cbias[:, fb:fb + 1], ps[pb][:, 256:257]),
                      [psb[pb]], [b_cbias])
                sc.op(A, lambda e, pb=pb, fb=fb: e.activation(hdn[:, fb, :], ps[pb][:, 0:256], AF.Silu,
                                                                bias=cbias[:, fb:fb + 1]),
                      [psb[pb], b_cbias], [b_hdn])
            for mt in range(2):
                pb = 2 + mt
                for fb in range(2):
                    mm(ps[pb][:, 0:128], hdn[:, fb, mt * 128:(mt + 1) * 128], w2[:, wi, fb, :], fb == 0, fb == 1,
                       [b_hdn, b_w2], [psb[pb]])
                if which == "k":
                    x1 = ps[pb][:, 0:64]
                    x2 = ps[pb][:, 64:128]
                    cs, sn = ropecc[:, 0, :], ropecc[:, 1, :]
                    sc.op(V, lambda e, x1=x1, cs=cs: e.tensor_tensor(ctmp[:, 0, 0:64], x1, cs, ALU.mult), [psb[pb], b_const], [b_ctmp])
                    sc.op(V, lambda e, x2=x2, sn=sn: e.tensor_tensor(ctmp[:, 1, 0:64], x2, sn, ALU.mult), [psb[pb], b_const], [b_ctmp])
                    sc.op(V, lambda e, x2=x2, cs=cs: e.tensor_tensor(ctmp[:, 2, 0:64], x2, cs, ALU.mult), [psb[pb], b_const], [b_ctmp])
                    sc.op(V, lambda e, x1=x1, sn=sn: e.tensor_tensor(ctmp[:, 3, 0:64], x1, sn, ALU.mult), [psb[pb], b_const], [b_ctmp])
                    sc.op(V, lambda e, mt=mt: e.tensor_tensor(kcr[:, mt, 0:64], ctmp[:, 0, 0:64], ctmp[:, 1, 0:64], ALU.subtract), [b_ctmp], [b_kcr])
                    sc.op(V, lambda e, mt=mt: e.tensor_tensor(kcr[:, mt, 64:128], ctmp[:, 2, 0:64], ctmp[:, 3, 0:64], ALU.add), [b_ctmp], [b_kcr])
                    pt = 4 + mt
                    ptv = ps[pt].bitcast(BF16)
                    transp(ptv[:, 0:128], kcr[:, mt, :], [b_kcr], [psb[pt]])
                    sc.op(V, lambda e, ptv=ptv, mt=mt: e.tensor_copy(kcTs[:, mt * 128:(mt + 1) * 128], ptv[:, 0:128]),
                          [psb[pt]], [b_kcTs])
                else:
                    sc.op(V, lambda e, pb=pb, mt=mt: e.tensor_copy(vcs[:, mt, :], ps[pb][:, 0:128]), [psb[pb]], [b_vcs])
            if which == "k":
                sc.dma(SY, kcT_loc.ap(), kcTs, reads=[b_kcTs], writes=[b_pay], slot="pay")
            else:
                sc.dma(SY, vc_loc.ap().rearrange("(mt p) e -> p mt e", p=128), vcs, reads=[b_vcs], writes=[b_pay], slot="pay")

        def kv_block(col0, dst_loc, is_k, sel):
            if is_k and sel:
                tm_proj(col0, 512, lambda k, pb: rope_tm(pb, k, lambda: kstK[:, :, k, :], b_kstK))
                sc.dma(SY, dst_loc.ap(), kstK.rearrange("p h k t -> p (h k t)"), reads=[b_kstK], writes=[b_pay],
                       slot="pay")
            elif sel:
                tm_proj(col0, 512, lambda k, pb: sc.op(
                    V, lambda e: e.tensor_copy(kstV[:, :, k, 0:128], ps[pb].rearrange("p (h e) -> p h e", h=4)),
                    [psb[pb]], [b_kstV]))
                sc.dma(SY, dst_loc.ap(), kstV.rearrange("p h k e -> p (h k e)"), reads=[b_kstV], writes=[b_pay],
                       slot="pay")
            elif is_k:
                tm_proj(col0, 512, lambda k, pb: rope_tm(
                    pb, k, lambda: kst[:, k, :].rearrange("p (h t) -> p h t", h=4), b_kst))
                sc.dma(SY, dst_loc.ap().rearrange("(k p) n -> p k n", p=128), kst, reads=[b_kst], writes=[b_pay], slot="pay")
            else:
                tm_proj(col0, 512, lambda k, pb: sc.op(
                    V, lambda e: e.tensor_copy(kst[:, k, :], ps[pb]), [psb[pb]], [b_kst]))
                sc.dma(SY, dst_loc.ap().rearrange("(k p) n -> p k n", p=128), kst, reads=[b_kst], writes=[b_pay], slot="pay")

        kv_block(5120, kselT_loc, True, True)
        kv_block(5632, vsel_loc, False, True)
        kv_block(6144, kwinT_loc, True, False)
        kv_block(6656, vwin_loc, False, False)

        cc_sem = stack.enter_context(nc.semaphore("cc"))
        sc.dsem["cc"] = cc_sem
        sc.dcnt["cc"] = 0
        b_all = Buf("all")
        pairs = [(kcT_loc, kcT_all), (vc_loc, vc_all), (kwinT_loc, kwinT_all), (vwin_loc, vwin_all),
                 (kselT_loc, kselT_all), (vsel_loc, vsel_all)]
        for src, dst in pairs:
            sc.dcnt["cc"] += 1
            dap = dst.ap()[4096:, :] if dst in (kwinT_all, vwin_all) else dst.ap()
            sc.raw(G, lambda e, src=src, dap=dap: e.collective_compute(
                "AllGather", ALU.bypass, replica_groups=[list(range(NC))],
                ins=[src.ap().opt()], outs=[dap.opt()]).then_inc(cc_sem, 1),
                reads=[b_pay], writes=[b_all], ev=(("D", "cc"), sc.dcnt["cc"]))

        wscrK = dt("wscrK", [5 * NS * 128, 512], BF16)
        wscrV = dt("wscrV", [5 * NS * 128, 512], BF16)
        b_wscr = Buf("wscr")
        wregs = {}

        for t_all in (kwinT_all, vwin_all):
            sc.dma(SY, t_all.ap()[0:4096, :], t_all.ap()[4096 + 31 * 128:4096 + 63 * 128, :], reads=[b_all],
                   writes=[b_all], slot="xcopy")

        def wsrc(eng, t_all):
            pid = eng.partition_id()
            return t_all.ap()[bass.ds(pid * 1024, 5 * 1024), :]
        sc.dma(SY, wscrK.ap(), None, reads=[b_all], writes=[b_wscr], slot="wscr",
               in_fn=lambda eng: wsrc(eng, kwinT_all))
        sc.dma(SY, wscrV.ap(), None, reads=[b_all], writes=[b_wscr], slot="wscr",
               in_fn=lambda eng: wsrc(eng, vwin_all))

        def own(ap_seg, a, b):
            return ap_seg

        def evac_gpool(blk0):
            def f(j, base):
                blk = blk0 + j
                for k in range(NS):
                    t0 = k * TW + 16
                    si = 0 if t0 + 128 <= 512 else (1 if t0 >= 512 and t0 + 128 <= 1024 else (2 if t0 >= 1024 else -1))
                    if si >= 0:
                        a = segs[si][0]
                        sc.op(A, lambda e, blk=blk, k=k, si=si, a=a, t0=t0: e.activation(
                            sgpool[:, blk, k * 128:(k + 1) * 128], ps[base + si][:, t0 - a:t0 - a + 128], AF.Silu),
                            [psb[base + si]], [b_sgpool])
                    else:
                        for si2, (a, b) in enumerate(segs):
                            lo, hi = max(t0, a), min(t0 + 128, b)
                            if lo < hi:
                                sc.op(A, lambda e, blk=blk, k=k, si2=si2, a=a, lo=lo, hi=hi, t0=t0: e.activation(
                                    sgpool[:, blk, k * 128 + lo - t0:k * 128 + hi - t0],
                                    ps[base + si2][:, lo - a:hi - a], AF.Silu), [psb[base + si2]], [b_sgpool])
            return f

        fm_proj(1024, evac_gpool(0))
        fm_proj(1536, evac_gpool(4))

        def evac_upool(blk0):
            def f(j, base):
                blk = blk0 + j
                g = blk // 2
                cc = blk % 2
                for si, (a, b) in enumerate(segs):
                    sc.op(A, lambda e, si=si, a=a, b=b: e.activation(uT[:, a:b], ps[base + si][:, 0:b - a], AF.Copy),
                          [psb[base + si]], [b_uT])
                u3 = uT.rearrange("p (k t) -> p k t", k=NS)
                a3 = sA.rearrange("p (k t) -> p k t", k=NS)
                b3 = sB.rearrange("p (k t) -> p k t", k=NS)
                cur, bcur = u3, b_uT
                tmp = [(a3, b_sA), (b3, b_sB)]
                sh = 1
                lvl = 0
                w = 2 << g
                while sh < w:
                    dst, bdst = tmp[lvl % 2]
                    sc.op(V, lambda e, dst=dst, cur=cur, sh=sh: e.tensor_tensor(
                        dst[:, :, 2 * sh - 1:144], cur[:, :, 2 * sh - 1:144], cur[:, :, sh - 1:144 - sh], ALU.add),
                        [bcur], [bdst])
                    cur, bcur = dst, bdst
                    sh *= 2
                    lvl += 1
                p3 = pooled[:, cc, :].rearrange("p (k t) -> p k t", k=NS)
                sc.op(V, lambda e, cur=cur, p3=p3, w=w: e.scalar_tensor_tensor(
                    p3, cur[:, :, 16:144], 1.0 / w, u3[:, :, 16:144], ALU.mult, ALU.subtract),
                    [bcur, b_uT], [b_pooled[cc]])
                sc.op(V, lambda e, cur=cur, g=g: e.tensor_tensor(pfix, cur[:, 0, 16:32], invc[:, g, :], ALU.mult),
                      [bcur, b_const], [b_pfix])
                sc.op(V, lambda e, cc=cc: e.tensor_tensor(pooled[:, cc, 0:16], pfix, uT[:, 16:32], ALU.subtract),
                      [b_pfix, b_uT], [b_pooled[cc]])
                if cc == 1:
                    for db in range(2):
                        oblk = g * 2 + db
                        for half in range(2):
                            pb = 6 + half
                            for c2 in range(2):
                                mm(ps[pb], mixw[:, g, c2, db * 128:(db + 1) * 128],
                                   pooled[:, c2, half * 512:(half + 1) * 512], c2 == 0, c2 == 1,
                                   [b_mixw, b_pooled[0], b_pooled[1]], [psb[pb]])
                            sc.op(V, lambda e, pb=pb, oblk=oblk, half=half: e.scalar_tensor_tensor(
                                ypool[:, oblk, half * 512:(half + 1) * 512], ps[pb], pscale[:, oblk:oblk + 1],
                                sgpool[:, oblk, half * 512:(half + 1) * 512], ALU.mult, ALU.mult),
                                [psb[pb], b_sgpool, b_const], [b_ypool])
            return f

        fm_proj(0, evac_upool(0))
        fm_proj(512, evac_upool(4))

        sc.barrier()
        qs.reset()
        QT = qs.alloc([16, 1024], BF16)
        sgn = qs.alloc([NS, 2048], BF16)

        for qb in range(4):
            tm_proj(2048 + qb * 512, 512, lambda k, pb, qb=qb: rope_tm(
                pb, k, lambda: QT[:, qb * 4:qb * 4 + 4, k * 128:(k + 1) * 128], b_QT))

        for gb in range(4):
            tm_proj(7168 + gb * 512, 512, lambda k, pb, gb=gb: sc.op(
                A, lambda e: e.activation(sgn[:, k, gb * 512:(gb + 1) * 512], ps[pb], AF.Silu), [psb[pb]], [b_sgn[k]]))
        tm_proj(9216, 48, lambda k, pb: sc.op(
            A, lambda e: e.activation(gates[:, k, :], ps[pb][:, 0:48], AF.Sigmoid), [psb[pb]], [b_gates]))

        hT4 = hT.rearrange("p c (k t) -> p c k t", k=NS)
        for c0 in range(0, 16, 4):
            sc.dma(SY, hT_sp.ap().rearrange("p (c k t) -> p c k t", c=16, k=NS)[:, c0:c0 + 4],
                   hT4[:, c0:c0 + 4, :, 16:144], reads=[b_hT], slot="spill")
        sc.dma(SY, yp_sp.ap().rearrange("p (b t) -> p b t", b=8), ypool, reads=[b_ypool], slot="spill")

        if stage == 1:
            sc.barrier()
            sc.dma(SY, dbg["QT"].rearrange("p (a b) -> p a b", a=16), QT, reads=[b_QT], slot="dbg")
            sc.dma(SY, dbg["sgn"].rearrange("p (a b) -> p a b", a=NS), sgn, reads=b_sgn, slot="dbg")
            sc.dma(SY, dbg["gates"].rearrange("p (a b) -> p a b", a=NS), gates, reads=[b_gates], slot="dbg")
            sc.dma(SY, dbg["ypool"].rearrange("p (a b) -> p a b", a=8), ypool, reads=[b_ypool], slot="dbg")
            for nm, loc in [("kselT", kselT_loc), ("kwinT", kwinT_loc), ("vsel", vsel_loc), ("vwin", vwin_loc),
                            ("kcT", kcT_loc), ("vc", vc_loc)]:
                sc.dma(SY, dbg[nm], loc.ap(), slot="dbg")
            sc.barrier()
            sc.emit(block)
            return nc

        sc.barrier()

        ph.reset()
        KselT2 = [ph.alloc([S], BF16) for _ in range(2)]
        Vsel2 = [ph.alloc([64, 130], BF16) for _ in range(2)]
        emat = ph.alloc([S], BF16)
        selcm = ph.alloc([8, 512], BF16)
        wmk = ph.alloc([7, 512], BF16)
        cmpm2 = [ph.alloc([4, 512], BF16) for _ in range(2)]
        b_cmpm = [Buf('cmpm0'), Buf('cmpm1')]
        amat = ph.alloc([4, 128], BF16)
        fmm = ph.alloc([NS, 2, 128], F32)
        kcT = ph.alloc([512], BF16)
        vcx = ph.alloc([4, 130], BF16)
        kwT = [ph.alloc([5, 128], BF16) for _ in range(2)]
        vwx = [ph.alloc([5, 130], BF16) for _ in range(2)]
        NPT = 4
        PT = [ph.alloc([512], BF16) for _ in range(NPT)]
        PTc = ph.alloc([4, 512], BF16)
        imp = ph.alloc([128], F32)
        imp2 = ph.alloc([128], F32)
        m8a = ph.alloc([8], F32)
        m8b = ph.alloc([8], F32)
        mb = ph.alloc([128], BF16)
        mbT = ph.alloc([4, 128], BF16)
        den = ph.alloc([3, 4], F32)
        wgt = ph.alloc([3, 4], F32)
        ocomb = ph.alloc([4, 128], F32)
        b_ksel2, b_vsel2 = [Buf("ksel0"), Buf("ksel1")], [Buf("vsel0"), Buf("vsel1")]
        b_p2c, b_kc, b_vc = Buf("p2c"), Buf("kc"), Buf("vc")
        b_kw = [Buf("kw0"), Buf("kw1")]
        b_vw = [Buf("vw0"), Buf("vw1")]
        b_PT = [Buf("PT%d" % i) for i in range(NPT)]
        b_PTc = [Buf("PTc%d" % i) for i in range(4)]
        b_imp, b_imp2, b_m8a, b_m8b, b_mb, b_mbT = (Buf("imp"), Buf("imp2"), Buf("m8a"), Buf("m8b"), Buf("mb"),
                                                     Buf("mbT"))
        b_den, b_wgt, b_ocomb = Buf("den"), Buf("wgt"), Buf("ocomb")

        sc.dma(SY, emat, emat_d, writes=[b_p2c], slot="p2c")
        sc.dma(SY, selcm, selcm_d.rearrange("p (a b) -> p a b", a=8), writes=[b_p2c], slot="p2c")
        sc.dma(SY, wmk, wm_d.rearrange("p (a b) -> p a b", a=7), writes=[b_p2c], slot="p2c")
        sc.dma(SY, amat, amat_d.rearrange("p (a b) -> p a b", a=4), writes=[b_p2c], slot="p2c")
        sc.dma(SY, fmm, fmm_d.rearrange("p (k a b) -> p k a b", k=NS, a=2), writes=[b_p2c], slot="p2c")
        sc.op(V, lambda e: e.memset(vcx[:, :, 128:130], 1.0), writes=[b_vc])
        for i in range(2):
            sc.op(V, lambda e, i=i: e.memset(vwx[i][:, :, 128:130], 1.0), writes=[b_vw[i]])

        kc_v = kcT_all.ap().rearrange("(r e) (h n) -> e h r n", r=NC, h=4)
        vc_v = vc_all.ap().rearrange("(r h n) e -> r h n e", r=NC, h=4)

        sbank = [0]
        SBANKS = [0, 1]
        ptc = [0]
        accn = [0]
        accsets = [(2, 3), (4, 5)]
        TB = 7
        wcnt = [0]

        wK_v = wscrK.ap().rearrange("(j k d) n -> d j k n", j=5, k=NS)
        wV_v = wscrV.ap().rearrange("(j k d) n -> d j k n", j=5, k=NS)

        def load_window(h, k):
            i = wcnt[0] % 2
            wcnt[0] += 1
            sc.dma(SY, kwT[i], wK_v[:, :, k, h * 128:(h + 1) * 128], reads=[b_wscr], writes=[b_kw[i]], slot="kw%d" % i)
            sc.dma(SY, vwx[i][:, :, 0:128], wV_v[:, :, k, h * 128:(h + 1) * 128], reads=[b_wscr], writes=[b_vw[i]],
                   slot="vw%d" % i)
            return i

        def branch(Qh, tiles, evac):
            n = len(tiles)
            aset = accsets[accn[0] % 2]
            accn[0] += 1
            state = {}

            def emit_S(t):
                kT, kreads, masks, vr, vreads, ptd = tiles[t]
                sb = SBANKS[sbank[0] % len(SBANKS)]
                sbank[0] += 1
                so = ps[sb].rearrange("p (g q) -> p g q", g=4)
                mm(so, kT, Qh, True, len(masks) == 0, kreads + [b_QT], [psb[sb]])
                for mi, (ml, mr, mreads) in enumerate(masks):
                    mm(ps[sb], ml, mr, False, mi == len(masks) - 1, mreads, [psb[sb]])
                if ptd is None:
                    pi = ptc[0] % NPT
                    ptc[0] += 1
                    dst, bd = PT[pi], b_PT[pi]
                else:
                    dst, bd = ptd
                sc.op(A, lambda e, dst=dst, sb=sb: e.activation(dst, ps[sb], AF.Exp, scale=SCALE), [psb[sb]], [bd])
                state[t] = (dst, bd)

            def emit_PV(t):
                kT, kreads, masks, vr, vreads, ptd = tiles[t]
                dst, bd = state[t]
                for g in range(4):
                    bank = aset[g // 2]
                    o = ps[bank][:, (g % 2) * 129:(g % 2) * 129 + 129]
                    mm(o, dst[:, g * 128:(g + 1) * 128], vr, (t == 0 and g % 2 == 0), t == n - 1,
                       [bd] + vreads, [psb[bank]], inc=(t == n - 1 and g % 2 == 1) or (g == 3), sgc=True)

            emit_S(0)
            for t in range(n):
                if t + 1 < n:
                    emit_S(t + 1)
                emit_PV(t)
            evac(aset)

        def combine(bi, aset, h, k, first, last):
            for half in range(2):
                bank = aset[half]
                dv = ps[bank][:, 128:258:129]
                sc.op(V, lambda e, dv=dv, half=half: e.tensor_scalar(den[:, bi, half * 2:half * 2 + 2], dv, 1e-30, None,
                                                                     ALU.max), [psb[bank]], [b_den])
            sc.op(V, lambda e: e.reciprocal(den[:, bi, :], den[:, bi, :]), [b_den], [b_den])
            gsl = gates[:, k, h * 12 + bi:h * 12 + 12:3]
            sc.op(V, lambda e, gsl=gsl: e.tensor_tensor(wgt[:, bi, :], den[:, bi, :], gsl, ALU.mult),
                  [b_den, b_gates], [b_wgt])
            for g in range(4):
                bank = aset[g // 2]
                o = ps[bank][:, (g % 2) * 129:(g % 2) * 129 + 128]
                if first:
                    sc.op(V, lambda e, o=o, g=g: e.tensor_scalar(ocomb[:, g, :], o, wgt[:, bi, g:g + 1], None, ALU.mult),
                          [psb[bank], b_wgt], [b_ocomb])
                else:
                    sc.op(V, lambda e, o=o, g=g: e.scalar_tensor_tensor(ocomb[:, g, :], o, wgt[:, bi, g:g + 1],
                                                                         ocomb[:, g, :], ALU.mult, ALU.add),
                          [psb[bank], b_wgt, b_ocomb], [b_ocomb])
            if last:
                sl = sgn[:, k, h * 512:(h + 1) * 512]
                sc.op(V, lambda e, sl=sl: e.tensor_tensor(sl, ocomb.rearrange("p g e -> p (g e)"), sl, ALU.mult),
                      [b_ocomb, b_sgn[k]], [b_sgn[k]])

        cmpm_v = cmpm_d.rearrange("p (k a b) -> p k a b", k=NS, a=4)

        def load_kv(h):
            hp = h % 2
            for r in range(NC):
                sc.dma(SY, KselT2[hp][:, r * 1024:(r + 1) * 1024],
                       kselT_all.ap()[r * 128:(r + 1) * 128, h * 1024:(h + 1) * 1024],
                       reads=[b_all], writes=[b_ksel2[hp]], slot="ksel%d" % hp)
                sc.dma(SY, Vsel2[hp][:, r * 8:(r + 1) * 8, :].rearrange("p k e -> p (k e)"),
                       vsel_all.ap()[r * 128:(r + 1) * 128, h * 1040:(h + 1) * 1040],
                       reads=[b_all], writes=[b_vsel2[hp]], slot="vsel%d" % hp)

        load_kv(0)
        stepn = [0]
        for h in range(nh):
            KselT, Vsel = KselT2[h % 2], Vsel2[h % 2]
            b_ksel, b_vsel = b_ksel2[h % 2], b_vsel2[h % 2]
            sc.dma(SY, kcT.rearrange("p (r n) -> p r n", r=NC), kc_v[:, h], reads=[b_all], writes=[b_kc], slot="kc")
            for tc in range(4):
                for r2 in range(2):
                    sc.dma(SY, vcx[r2 * 64:(r2 + 1) * 64, tc, 0:128], vc_v[2 * tc + r2, h], reads=[b_all],
                           writes=[b_vc], slot="vc")
            for k in range(NS):
                Qh = QT[:, 4 * h:4 * h + 4, k * 128:(k + 1) * 128]
                wi = load_window(h, k)
                ci = stepn[0] % 2
                stepn[0] += 1
                sc.dma(SY, cmpm2[ci], cmpm_v[:, k], writes=[b_cmpm[ci]], slot="cmpm%d" % ci)
                if k == 1 and h + 1 < nh:
                    load_kv(h + 1)
                tiles = []
                for tc in range(4):
                    tiles.append((kcT[:, tc * 128:(tc + 1) * 128], [b_kc],
                                  [(ident, cmpm2[ci][:, tc, :], [b_const, b_cmpm[ci]])],
                                  vcx[:, tc, 0:129], [b_vc], (PTc[:, tc, :], b_PTc[tc])))

                def evac_c(aset, h=h, k=k):
                    UB = 6
                    for tc in range(4):
                        for g in range(4):
                            mm(ps[UB][:, g * 128:(g + 1) * 128], PTc[:, tc, g * 128:(g + 1) * 128], amat[:, tc, :],
                               (tc == 0 and g == 0), tc == 3, [b_PTc[tc], b_p2c], [psb[UB]],
                               inc=(tc == 3 and g == 3), sgc=True)
                    combine(0, aset, h, k, True, False)
                    sc.op(V, lambda e: e.tensor_scalar(imp, ps[UB][:, 0:128], den[:, 0, 0:1], None, ALU.mult),
                          [psb[UB], b_den], [b_imp])
                    for g in range(1, 4):
                        sc.op(V, lambda e, g=g: e.scalar_tensor_tensor(imp, ps[UB][:, g * 128:(g + 1) * 128],
                                                                        den[:, 0, g:g + 1], imp, ALU.mult, ALU.add),
                              [psb[UB], b_den, b_imp], [b_imp])
                    sc.op(V, lambda e, k=k: e.tensor_tensor(imp, imp, fmm[:, k, 0, :], ALU.mult), [b_imp, b_p2c], [b_imp])
                    sc.op(V, lambda e, k=k: e.tensor_tensor(imp, imp, fmm[:, k, 1, :], ALU.add), [b_imp, b_p2c], [b_imp])
                    sc.op(V, lambda e: e.max(m8a, imp), [b_imp], [b_m8a])
                    sc.op(V, lambda e: e.match_replace(imp2, m8a, imp, -3.0e38), [b_m8a, b_imp], [b_imp2])
                    sc.op(V, lambda e: e.max(m8b, imp2), [b_imp2], [b_m8b])
                    sc.op(V, lambda e: e.tensor_scalar(mb, imp, m8b[:, 7:8], NEGB, ALU.is_lt, ALU.mult),
                          [b_imp, b_m8b], [b_mb])
                    tv = ps[TB].bitcast(BF16)
                    transp(tv[:, 0:128], mb, [b_mb], [psb[TB]])
                    sc.op(A, lambda e, tv=tv: e.activation(mbT, tv[:, 0:128].unsqueeze(1).broadcast_to([128, 4, 128]),
                                                           AF.Copy), [psb[TB]], [b_mbT])

                branch(Qh, tiles, evac_c)
                tiles = []
                for j in range(5):
                    if k == 0:
                        masks = [(ident, wmk[:, j, :], [b_const, b_p2c])]
                    elif j == 0:
                        masks = [(ident, wmk[:, 5, :], [b_const, b_p2c])]
                    elif j == 4:
                        masks = [(ident, wmk[:, 6, :], [b_const, b_p2c])]
                    else:
                        masks = []
                    tiles.append((kwT[wi][:, j, :], [b_kw[wi]], masks, vwx[wi][:, j, 0:129], [b_vw[wi]], None))
                branch(Qh, tiles, lambda aset, h=h, k=k: combine(2, aset, h, k, False, False))
                tiles = []
                mbT2 = mbT.rearrange("p g q -> p (g q)")
                for t in range(8 * k + 8):
                    masks = [(emat[:, t * 128:(t + 1) * 128], mbT2, [b_p2c, b_mbT])]
                    if t >= 8 * k:
                        masks.append((ident, selcm[:, t - 8 * k, :], [b_const, b_p2c]))
                    pos = (t % 8) * 8 + t // 8
                    tiles.append((KselT[:, pos * 128:(pos + 1) * 128], [b_ksel], masks, Vsel[:, pos, 0:129], [b_vsel],
                                  None))
                branch(Qh, tiles, lambda aset, h=h, k=k: combine(1, aset, h, k, False, True))

        if stage == 2:
            import os
            for _i in range(int(os.environ.get("K_DUMMY_MM", "0"))):
                mm(ps[6][:, 0:8], ident, ident[:, 0:8], True, True, [b_const], [psb[6]], inc=(_i % 64 == 63))
            sc.barrier()
            sc.dma(SY, dbg["og"].rearrange("p (a b) -> p a b", a=NS), sgn, reads=b_sgn, slot="dbg")
            sc.barrier()
            sc.emit(block)
            return nc

        sc.barrier()
        ph.reset()
        qs.reset()
        ogT = qs.alloc([16, 1024], BF16)
        sgn = qs.alloc([NS, 2048], BF16)
        b_ogT = Buf("ogT")
        hTo = ph.alloc([16, 1024], BF16)
        b_hTo = Buf("hTo")
        ypl = ph.alloc([8, 1024], BF16)
        b_ypl = Buf("ypl")
        NW3 = 2
        w3 = [ph.alloc([56, 256], BF16) for _ in range(NW3)]
        b_w3 = [Buf("w3_%d" % i) for i in range(NW3)]
        sg0 = ph.alloc([512], F32)
        sg1 = ph.alloc([512], F32)
        m0 = ph.alloc([512], F32)
        m1 = ph.alloc([512], F32)
        b_sg0, b_sg1, b_m0, b_m1 = Buf("sg0"), Buf("sg1"), Buf("m0"), Buf("m1")
        sc.dma(SY, hTo, hT_sp.ap().rearrange("p (c t) -> p c t", c=16), writes=[b_hTo], slot="p3a")
        sc.dma(SY, ypl, yp_sp.ap().rearrange("p (b t) -> p b t", b=8), writes=[b_ypl], slot="p3a")
        tcount = [0]
        for k in range(NS):
            for c4 in range(4):
                tb = 6 + (tcount[0] % 2)
                tcount[0] += 1
                tv = ps[tb].bitcast(BF16)
                for cc in range(4):
                    c = c4 * 4 + cc
                    transp(tv[:, cc * 128:(cc + 1) * 128], sgn[:, k, c * 128:(c + 1) * 128], [b_sgn[k]], [psb[tb]])
                sc.op(A if (tcount[0] % 2) else V,
                      (lambda e, tv=tv, c4=c4, k=k: e.activation(
                          ogT[:, c4 * 4:c4 * 4 + 4, k * 128:(k + 1) * 128],
                          tv[:, 0:512].rearrange("p (c t) -> p c t", c=4), AF.Copy)) if (tcount[0] % 2) else
                      (lambda e, tv=tv, c4=c4, k=k: e.tensor_copy(
                          ogT[:, c4 * 4:c4 * 4 + 4, k * 128:(k + 1) * 128],
                          tv[:, 0:512].rearrange("p (c t) -> p c t", c=4))),
                      [psb[tb]], [b_ogT])
        sc.barrier()
        mergedT = sgn.rearrange("p k n -> p (k n)").rearrange("p (c t) -> p c t", c=16)
        b_mg = Buf("merged")
        wpo_v = wpo_d.rearrange("(c p) n -> p c n", p=128)
        wno_v = wno_d.rearrange("(c p) n -> p c n", p=128)
        wmg_v = wmg_d.rearrange("(c p) n -> p c n", p=128)
        pcount = [0]
        for ob in range(16):
            i = (ob // 2) % NW3
            o2 = ob % 2
            if o2 == 0:
                cols = slice(ob * 128, ob * 128 + 256)
                sc.dma(G, w3[i][:, 0:8, :], wpo_v[:, :, cols], writes=[b_w3[i]], slot="w3_%d" % i)
                sc.dma(G, w3[i][:, 8:24, :], wno_v[:, :, cols], writes=[b_w3[i]], slot="w3_%d" % i)
                sc.dma(G, w3[i][:, 24:40, :], wmg_v[:, :, cols], writes=[b_w3[i]], slot="w3_%d" % i)
                sc.dma(G, w3[i][:, 40:56, :], wmg_v[:, :, D + ob * 128:D + ob * 128 + 256], writes=[b_w3[i]],
                       slot="w3_%d" % i)
            w3s = w3[i][:, :, o2 * 128:(o2 + 1) * 128]
            for tg in range(2):
                tsl = slice(tg * 512, (tg + 1) * 512)
                pa, pg0, pbk, pg1 = [(pcount[0] * 4 + x) % 6 for x in range(4)]
                pcount[0] += 1
                for c in range(8):
                    mm(ps[pa], w3s[:, c, :], ypl[:, c, tsl], c == 0, c == 7, [b_w3[i], b_ypl], [psb[pa]])
                for c in range(16):
                    mm(ps[pg0], w3s[:, 24 + c, :], hTo[:, c, tsl], c == 0, c == 15, [b_w3[i], b_hTo], [psb[pg0]])
                for c in range(16):
                    mm(ps[pbk], w3s[:, 8 + c, :], ogT[:, c, tsl], c == 0, c == 15, [b_w3[i], b_ogT], [psb[pbk]])
                for c in range(16):
                    mm(ps[pg1], w3s[:, 40 + c, :], hTo[:, c, tsl], c == 0, c == 15, [b_w3[i], b_hTo], [psb[pg1]])
                sc.op(A, lambda e, pg0=pg0, ob=ob: e.activation(sg0, ps[pg0], AF.Sigmoid, bias=bmerge[:, ob:ob + 1]),
                      [psb[pg0], b_const], [b_sg0])
                sc.op(A, lambda e, pg1=pg1, ob=ob: e.activation(sg1, ps[pg1], AF.Sigmoid,
                                                                 bias=bmerge[:, 16 + ob:17 + ob]),
                      [psb[pg1], b_const], [b_sg1])
                sc.op(V, lambda e, pa=pa: e.tensor_tensor(m0, ps[pa], sg0, ALU.mult), [psb[pa], b_sg0], [b_m0])
                sc.op(V, lambda e, pbk=pbk: e.tensor_tensor(m1, ps[pbk], sg1, ALU.mult), [psb[pbk], b_sg1], [b_m1])
                sc.op(V, lambda e, ob=ob, tsl=tsl: e.tensor_tensor(mergedT[:, ob, tsl], m0, m1, ALU.add),
                      [b_m0, b_m1], [b_mg])
        sc.barrier()
        ph.reset()
        wout = ph.alloc([16, D], BF16)
        b_wout = Buf("wout")
        fnw = ph.alloc([D], F32)
        xt = [ph.alloc([D], F32) for _ in range(2)]
        b_xt = [Buf("xt0"), Buf("xt1")]
        junk = ph.alloc([D], BF16)
        b_junk = Buf("junk")
        ssq = ph.alloc([2], F32)
        b_ssq = Buf("ssq")
        wout_v = wout_d.rearrange("(c p) n -> p c n", p=128)
        for cg in range(4):
            sc.dma(G, wout[:, :, cg * 512:(cg + 1) * 512], wout_v[:, :, cg * 512:(cg + 1) * 512], writes=[b_wout],
                   slot="wout")
        sc.dma(SY, fnw, fnw_d, writes=[b_const], slot="const")
        for k in range(NS):
            i = k % 2
            sc.dma(SY, xt[i], xtok_d[k * 128:(k + 1) * 128, :], writes=[b_xt[i]], slot="xt%d" % i)
            for cg in range(4):
                pb = (k * 4 + cg) % 6
                for c in range(16):
                    mm(ps[pb], mergedT[:, c, k * 128:(k + 1) * 128], wout[:, c, cg * 512:(cg + 1) * 512],
                       c == 0, c == 15, [b_mg, b_wout], [psb[pb]])
                sc.op(V, lambda e, i=i, pb=pb, cg=cg: e.tensor_tensor(xt[i][:, cg * 512:(cg + 1) * 512], ps[pb],
                                                                      xt[i][:, cg * 512:(cg + 1) * 512], ALU.add),
                      [psb[pb], b_xt[i]], [b_xt[i]])
            sc.op(V, lambda e, i=i: e.memset(ssq[:, i:i + 1], 0.0), [], [b_ssq])
            sc.op(A, lambda e, i=i: e.activation(junk, xt[i], AF.Square, accum_out=ssq[:, i:i + 1]),
                  [b_xt[i]], [b_junk, b_ssq])
            sc.op(A, lambda e, i=i: e.activation(ssq[:, i:i + 1], ssq[:, i:i + 1], AF.Sqrt, bias=EPS, scale=1.0 / D),
                  [b_ssq], [b_ssq])
            sc.op(V, lambda e, i=i: e.reciprocal(ssq[:, i:i + 1], ssq[:, i:i + 1]), [b_ssq], [b_ssq])
            sc.op(V, lambda e, i=i: e.scalar_tensor_tensor(xt[i], xt[i], ssq[:, i:i + 1], fnw, ALU.mult, ALU.mult),
                  [b_xt[i], b_ssq, b_const], [b_xt[i]])
            sc.dma(SY, out_d[k * 128:(k + 1) * 128, :], xt[i], reads=[b_xt[i]], slot="xt%d" % i)
        sc.barrier()
        sc.emit(block)
    return nc


def _host_inputs(inp):
    f32 = np.float32
    bf = ml_dtypes.bfloat16
    x = np.asarray(inp["x"], f32)[0]
    xpad = np.zeros((S + 32, D), f32)
    xpad[16:16 + S] = x
    half = 64
    inv = (np.float32(10000.0) ** (-np.arange(half, dtype=f32) / np.float32(half))).astype(f32)
    common = {
        "normw": np.ascontiguousarray(np.asarray(inp["norm_w"], f32)[0].reshape(16, 128).T),
        "w_in": np.ascontiguousarray(np.asarray(inp["w_in"], f32)[0]),
        "pool_mix": np.ascontiguousarray(np.asarray(inp["pool_mix"], f32)[0]),
        "pscale": np.ascontiguousarray(np.asarray(inp["pool_scale"], f32)[0].reshape(8, 128).T),
        "pe_kT": np.ascontiguousarray(np.asarray(inp["cmp_pe_k"], f32)[0].T),
        "pe_vT": np.ascontiguousarray(np.asarray(inp["cmp_pe_v"], f32)[0].T),
        "w1k": np.ascontiguousarray(np.asarray(inp["cmp_w1_k"], f32)[0]),
        "w1v": np.ascontiguousarray(np.asarray(inp["cmp_w1_v"], f32)[0]),
        "w2k": np.ascontiguousarray(np.asarray(inp["cmp_w2_k"], f32)[0]),
        "w2v": np.ascontiguousarray(np.asarray(inp["cmp_w2_v"], f32)[0]),
        "w_pool_out": np.ascontiguousarray(np.asarray(inp["w_pool_out"], f32)[0]),
        "w_nsa_out": np.ascontiguousarray(np.asarray(inp["w_nsa_out"], f32)[0]),
        "w_merge": np.ascontiguousarray(np.asarray(inp["w_merge"], f32)[0]),
        "bmerge": np.ascontiguousarray(np.asarray(inp["b_merge"], f32)[0].reshape(32, 128).T),
        "w_out": np.ascontiguousarray(np.asarray(inp["w_out"], f32)[0]),
        "fnw": np.ascontiguousarray(np.broadcast_to(np.asarray(inp["final_norm_w"], f32)[None, :], (128, D))),
        "ident": np.eye(128, dtype=f32).astype(bf),
    }
    emat = np.zeros((128, S), f32)
    emat[np.arange(S) // 64, np.arange(S)] = 1.0
    common["emat"] = emat.astype(bf)
    kk = np.arange(128)[:, None]
    qq = np.arange(128)[None, :]
    caus = np.where(kk <= qq, 0.0, NEGB).astype(f32)
    upper = np.where(kk > qq, 0.0, NEGB).astype(f32)
    full = np.zeros((128, 128), f32)
    none = np.full((128, 128), NEGB, f32)
    maps = []
    for c in range(NC):
        m = dict(common)
        blocks = [8 * k + c for k in range(NS)]
        xT = np.zeros((D, NT), f32)
        for k, i in enumerate(blocks):
            xT[:, k * TW:(k + 1) * TW] = xpad[128 * i:128 * i + TW].T
        m["xT"] = xT
        m["xtok"] = np.ascontiguousarray(np.concatenate([x[128 * i:128 * i + 128] for i in blocks], 0))
        cs = np.zeros((128, NS, 2, 64), f32)
        for k, i in enumerate(blocks):
            pos = (128 * i + np.arange(128)).astype(f32)
            ang = pos[:, None] * inv[None, :]
            cs[:, k, 0] = np.cos(ang)
            cs[:, k, 1] = np.sin(ang)
        m["ropecs"] = cs.reshape(128, -1)
        cc = np.zeros((128, 2, 64), f32)
        for h2 in range(2):
            for k, i in enumerate(blocks):
                n = 8 * i + np.arange(8)
                pos = (n * 16 + 31).astype(f32)
                ang = pos[:, None] * inv[None, :]
                r0 = h2 * 64 + k * 8
                cc[r0:r0 + 8, 0] = np.cos(ang)
                cc[r0:r0 + 8, 1] = np.sin(ang)
        m["ropecc"] = cc.reshape(128, -1)
        ic = np.zeros((128, 4, 16), f32)
        for g in range(4):
            w = 2 << g
            t = 128 * blocks[0] + np.arange(16)
            ic[:, g, :] = 1.0 / np.minimum(t + 1, w)
        m["invc"] = ic.reshape(128, -1)
        scm = np.zeros((128, 8, 4, 128), f32)
        for j in range(8):
            mk = full if j < c else (caus if j == c else none)
            scm[:, j] = mk[:, None, :]
        m["selcm"] = scm.reshape(128, -1).astype(bf)
        wmm = np.zeros((128, 7, 4, 128), f32)
        for j in range(5):
            t = c - 4 + j
            if t < 0:
                mk = none
            elif j == 0:
                mk = upper
            elif j == 4:
                mk = caus
            else:
                mk = full
            wmm[:, j] = mk[:, None, :]
        wmm[:, 5] = upper[:, None, :]
        wmm[:, 6] = caus[:, None, :]
        m["wm"] = wmm.reshape(128, -1).astype(bf)
        npr = np.arange(512)
        r_ = npr // 64
        k_ = (npr % 64) // 8
        nl_ = npr % 8
        nglob = 8 * (8 * k_ + r_) + nl_
        cend = nglob * 16 + 31
        cm = np.zeros((NS, 512, 128), f32)
        for k, i in enumerate(blocks):
            tpos = 128 * i + np.arange(128)
            valid = (cend[:, None] <= tpos[None, :]) & (nglob[:, None] < 511)
            cm[k] = np.where(valid, 0.0, NEGB)
        cm = cm.reshape(NS, 4, 128, 128)
        cm = np.broadcast_to(cm[:, :, :, None, :], (NS, 4, 128, 4, 128))
        m["cmpm"] = np.ascontiguousarray(cm.transpose(2, 0, 1, 3, 4)).reshape(128, -1).astype(bf)
        am = np.zeros((512, 128), f32)
        for j in range(128):
            for mm_ in range(4):
                for nn in range(2):
                    idx = 4 * j + mm_ - nn
                    if 0 <= idx < 511:
                        am[nglob == idx, j] += 1.0
        m["amat"] = np.ascontiguousarray(am.reshape(4, 128, 128).transpose(1, 0, 2)).reshape(128, -1).astype(bf)
        fm = np.zeros((128, NS, 2, 128), f32)
        jb = np.arange(128)[None, :]
        for k, i in enumerate(blocks):
            tpos = 128 * i + np.arange(128)
            jt = (tpos // 64)[:, None]
            forced = (jb == 0) | (jb == jt) | (jb == jt - 1)
            fut = jb > jt
            M = np.where(forced | fut, 0.0, 1.0)
            B = np.where(fut, -1e30, np.where(forced, 1e6, 0.0))
            fm[:, k, 0] = M
            fm[:, k, 1] = B
        m["fmm"] = fm.reshape(128, -1)
        maps.append(m)
    return maps


_NC_CACHE = {}


def kernel(**inputs):
    maps = _host_inputs(inputs)
    if "nc" not in _NC_CACHE:
        _NC_CACHE["nc"] = build()
    nc = _NC_CACHE["nc"]
    res = run_bass_kernel_spmd(nc, maps, core_ids=list(range(NC)))
    out = np.zeros((S, D), np.float32)
    for c in range(NC):
        o = np.asarray(res.results[c]["out"], np.float32)
        for k in range(NS):
            i = 8 * k + c
            out[128 * i:128 * i + 128] = o[k * 128:(k + 1) * 128]
    return out[None]
```

```python
import numpy as np
import ml_dtypes
from contextlib import ExitStack
import concourse.bass as bass
import concourse.mybir as mybir
from concourse.bass_utils import run_bass_kernel_spmd

F32 = mybir.dt.float32
BF16 = mybir.dt.bfloat16
ALU = mybir.AluOpType
AF = mybir.ActivationFunctionType

S = 8192
D = 2048
NC = 8
NS = 8
TW = 160
NT = NS * TW
DIN = 9264
NEGB = -30000.0
SCALE = 128 ** -0.5
EPS = 1e-6

ENGS = ["tensor", "vector", "scalar", "gpsimd", "sync"]


class Buf:
    __slots__ = ("w", "r", "name")

    def __init__(self, name=""):
        self.w = None
        self.r = []
        self.name = name


class Sched:
    def __init__(self, nc, stack):
        self.nc = nc
        self.stack = stack
        self.ops = {e: [] for e in ENGS}
        self.sem = {e: stack.enter_context(nc.semaphore("sem_" + e)) for e in ENGS}
        self.cnt = {e: 0 for e in ENGS}
        self.opidx = {e: 0 for e in ENGS}
        self.seen = {e: {} for e in ENGS}
        self.dsem = {}
        self.dcnt = {}
        self.widx = {}

    def _wait(self, e, ev):
        key, val = ev
        if self.seen[e].get(key, 0) >= val:
            return
        self.seen[e][key] = val
        self.ops[e].append(("wait", key, val))

    def _deps(self, e, reads, writes):
        for b in reads:
            if b.w is not None:
                if b.w[0] == ("E", e):
                    if e == "tensor":
                        continue
                    if self.opidx[e] - self.widx.get(id(b), -10) >= 3:
                        continue
                self._wait(e, b.w)
        for b in writes:
            if b.w is not None and b.w[0] != ("E", e):
                self._wait(e, b.w)
            for r in b.r:
                if r[0] != ("E", e):
                    self._wait(e, r)

    def op(self, e, fn, reads=(), writes=(), inc=True):
        if e != "tensor":
            pr = [b for b in reads if b.name.startswith("ps")]
            if pr:
                reads = [b for b in reads if not b.name.startswith("ps")]
                writes = list(writes) + pr
        self._deps(e, reads, writes)
        ev = (("E", e), self.cnt[e] + 1)
        if inc:
            self.cnt[e] += 1
        self.ops[e].append(("op", fn, inc, ev[1]))
        for b in reads:
            b.r.append(ev)
        for b in writes:
            b.w = ev
            b.r = []
            self.widx[id(b)] = self.opidx[e]
        self.opidx[e] += 1

    def dma(self, q, out, in_, reads=(), writes=(), slot=None, in_fn=None, **kw):
        if slot not in self.dsem:
            self.dsem[slot] = self.stack.enter_context(self.nc.semaphore("d_" + slot))
            self.dcnt[slot] = 0
        self._deps(q, reads, writes)
        self.dcnt[slot] += 16
        ev = (("D", slot), self.dcnt[slot])
        sem = self.dsem[slot]
        if in_fn is not None:
            self.ops[q].append(("raw", lambda eng, out=out, in_fn=in_fn, sem=sem, kw=kw: eng.dma_start(
                out=out, in_=in_fn(eng), **kw).then_inc(sem, 16)))
        else:
            self.ops[q].append(("raw", lambda eng, out=out, in_=in_, sem=sem, kw=kw: eng.dma_start(
                out=out, in_=in_, **kw).then_inc(sem, 16)))
        for b in reads:
            b.r.append(ev)
        for b in writes:
            b.w = ev
            b.r = []

    def raw(self, e, fn, reads=(), writes=(), ev=None):
        self._deps(e, reads, writes)
        self.ops[e].append(("raw", fn))
        for b in reads:
            b.r.append(ev)
        for b in writes:
            b.w = ev
            b.r = []

    def barrier(self, bufs=()):
        for e in ENGS:
            for o in ENGS:
                if o != e and o != "sync" and self.cnt[o] > 0:
                    self._wait(e, (("E", o), self.cnt[o]))
            for s, v in self.dcnt.items():
                if v > 0:
                    self._wait(e, (("D", s), v))

    def emit(self, block):
        ops = self.ops
        needed = {e: set() for e in ENGS}
        for e in ENGS:
            for rec in ops[e]:
                if rec[0] == "wait" and rec[1][0] == "E":
                    needed[rec[1][1]].add(rec[2])
        remap = {e: {v: i + 1 for i, v in enumerate(sorted(needed[e]))} for e in ENGS}
        sems, dsems = self.sem, self.dsem

        def run(e, eng):
            for rec in ops[e]:
                if rec[0] == "wait":
                    key, val = rec[1], rec[2]
                    if key[0] == "E":
                        eng.wait_ge(sems[key[1]], remap[key[1]][val])
                    else:
                        eng.wait_ge(dsems[key[1]], val)
                elif rec[0] == "op":
                    _, fn, inc, val = rec
                    if inc and val in needed[e]:
                        fn(eng).then_inc(sems[e], 1)
                    else:
                        fn(eng)
                else:
                    rec[1](eng)

        @block.tensor
        def _(eng):
            run("tensor", eng)

        @block.vector
        def _(eng):
            run("vector", eng)

        @block.scalar
        def _(eng):
            run("scalar", eng)

        @block.gpsimd
        def _(eng):
            run("gpsimd", eng)

        @block.sync
        def _(eng):
            run("sync", eng)


class Arena:
    def __init__(self, tensor, base, size):
        self.t = tensor
        self.base = base
        self.size = size
        self.off = 0

    def sub(self, size):
        a = Arena(self.t, self.base + self.off, size)
        self.off += size
        assert self.off <= self.size, (self.off, self.size)
        return a

    def reset(self):
        self.off = 0

    def alloc(self, shape, dtype):
        es = 2 if dtype == BF16 else 4
        n = int(np.prod(shape))
        nbytes = (n * es + 63) // 64 * 64
        o = self.base + self.off
        self.off += nbytes
        assert self.off <= self.size, ("arena overflow", self.off, self.size)
        ap = self.t[:, o // 2: o // 2 + n * es // 2]
        if dtype != BF16:
            ap = ap.bitcast(dtype)
        if len(shape) == 2:
            ap = ap.rearrange("p (a b) -> p a b", a=shape[0])
        elif len(shape) == 3:
            ap = ap.rearrange("p (a b c) -> p a b c", a=shape[0], b=shape[1])
        elif len(shape) == 4:
            ap = ap.rearrange("p (a b c d) -> p a b c d", a=shape[0], b=shape[1], c=shape[2])
        return ap


def build(stage=99, nh=4):
    nc = bass.Bass("TRN2", target_bir_lowering=False)
    dt = nc.dram_tensor

    def ext_in(name, shape, dtype=F32):
        return dt(name, list(shape), dtype, kind="ExternalInput").ap()

    def ext_out(name, shape, dtype=F32):
        return dt(name, list(shape), dtype, kind="ExternalOutput").ap()

    xT_d = ext_in("xT", [D, NT])
    xtok_d = ext_in("xtok", [NS * 128, D])
    normw_d = ext_in("normw", [128, 16])
    w_in_d = ext_in("w_in", [D, DIN])
    mix_d = ext_in("pool_mix", [4, 256, 256])
    pscale_d = ext_in("pscale", [128, 8])
    pek_d = ext_in("pe_kT", [128, 32])
    pev_d = ext_in("pe_vT", [128, 32])
    w1k_d = ext_in("w1k", [32, 128, 256])
    w1v_d = ext_in("w1v", [32, 128, 256])
    w2k_d = ext_in("w2k", [256, 128])
    w2v_d = ext_in("w2v", [256, 128])
    wpo_d = ext_in("w_pool_out", [1024, D])
    wno_d = ext_in("w_nsa_out", [D, D])
    wmg_d = ext_in("w_merge", [D, 2 * D])
    bmg_d = ext_in("bmerge", [128, 32])
    wout_d = ext_in("w_out", [D, D])
    fnw_d = ext_in("fnw", [128, D])
    ropecs_d = ext_in("ropecs", [128, NS * 2 * 64])
    ropecc_d = ext_in("ropecc", [128, 2 * 64])
    invc_d = ext_in("invc", [128, 4 * 16])
    ident_d = ext_in("ident", [128, 128], BF16)
    emat_d = ext_in("emat", [128, S], BF16)
    selcm_d = ext_in("selcm", [128, 8 * 512], BF16)
    wm_d = ext_in("wm", [128, 7 * 512], BF16)
    cmpm_d = ext_in("cmpm", [128, NS * 4 * 512], BF16)
    amat_d = ext_in("amat", [128, 4 * 128], BF16)
    fmm_d = ext_in("fmm", [128, NS * 2 * 128])
    out_d = ext_out("out", [NS * 128, D])

    kselT_loc = dt("kselT_loc", [128, 4096], BF16)
    kwinT_loc = dt("kwinT_loc", [NS * 128, 512], BF16)
    vsel_loc = dt("vsel_loc", [128, 4160], BF16)
    vwin_loc = dt("vwin_loc", [NS * 128, 512], BF16)
    kcT_loc = dt("kcT_loc", [128, 256], BF16)
    vc_loc = dt("vc_loc", [256, 128], BF16)
    kselT_all = dt("kselT_all", [NC * 128, 4096], BF16)
    kwinT_all = dt("kwinT_all", [(32 + NC * NS) * 128, 512], BF16)
    vsel_all = dt("vsel_all", [NC * 128, 4160], BF16)
    vwin_all = dt("vwin_all", [(32 + NC * NS) * 128, 512], BF16)
    kcT_all = dt("kcT_all", [NC * 128, 256], BF16)
    vc_all = dt("vc_all", [NC * 256, 128], BF16)
    hT_sp = dt("hT_sp", [128, 16 * NS * 128], BF16)
    yp_sp = dt("yp_sp", [128, 8 * 1024], BF16)

    dbg = {}
    if stage < 99:
        dbg["QT"] = ext_out("dbg_QT", [128, 16 * 1024], BF16)
        dbg["sgn"] = ext_out("dbg_sgn", [128, NS * 2048], BF16)
        dbg["gates"] = ext_out("dbg_gates", [128, NS * 48])
        dbg["ypool"] = ext_out("dbg_ypool", [128, 8 * 1024], BF16)
        dbg["kselT"] = ext_out("dbg_kselT", [128, 4096], BF16)
        dbg["kwinT"] = ext_out("dbg_kwinT", [NS * 128, 512], BF16)
        dbg["vsel"] = ext_out("dbg_vsel", [128, 4160], BF16)
        dbg["vwin"] = ext_out("dbg_vwin", [NS * 128, 512], BF16)
        dbg["kcT"] = ext_out("dbg_kcT", [128, 256], BF16)
        dbg["vc"] = ext_out("dbg_vc", [256, 128], BF16)
        dbg["og"] = ext_out("dbg_og", [128, NS * 2048], BF16)

    stack = ExitStack()
    with stack:
        ARENA_BYTES = 204 * 1024
        arena_t = stack.enter_context(nc.sbuf_tensor("arena", [128, ARENA_BYTES // 2], BF16))
        ps = [stack.enter_context(nc.psum_tensor("ps%d" % i, [128, 512], F32))[:] for i in range(8)]
        psb = [Buf("ps%d" % i) for i in range(8)]
        sc = Sched(nc, stack)
        block = stack.enter_context(nc.Block())

        top = Arena(arena_t, 0, ARENA_BYTES)
        ident = top.alloc([128], BF16)
        ones = top.alloc([128], BF16)
        normw = top.alloc([16], F32)
        pscale = top.alloc([8], F32)
        bmerge = top.alloc([32], F32)
        gates = top.alloc([NS, 48], F32)
        b_const = Buf("const")
        b_gates = Buf("gates")
        b_pay = Buf("pay")
        qs = top.sub(64 * 1024)
        ph = top.sub(ARENA_BYTES - top.off)

        QT = qs.alloc([16, 1024], BF16)
        sgn = qs.alloc([NS, 2048], BF16)
        b_QT = Buf("QT")
        b_sgn = [Buf("sgn%d" % k) for k in range(NS)]

        V = "vector"
        A = "scalar"
        G = "gpsimd"
        T = "tensor"
        SY = "sync"

        def mm(out, lhsT, rhs, start, stop, reads, writes, inc=None, sgc=False):
            if inc is None:
                inc = stop
            if sgc:
                sc.op(T, lambda e: e.matmul(out, lhsT, rhs, start=start, stop=stop, skip_group_check=True),
                      reads, writes, inc=inc)
            else:
                sc.op(T, lambda e: e.matmul(out, lhsT, rhs, start=start, stop=stop), reads, writes, inc=inc)

        def transp(out, in_, reads, writes):
            sc.op(T, lambda e: e.transpose(out, in_, ident), reads + [b_const], writes)

        sc.dma(SY, ident, ident_d, writes=[b_const], slot="const")
        sc.dma(SY, normw, normw_d, writes=[b_const], slot="const")
        sc.dma(SY, pscale, pscale_d, writes=[b_const], slot="const")
        sc.dma(SY, bmerge, bmg_d, writes=[b_const], slot="const")
        sc.op(V, lambda e: e.memset(ones, 1.0), writes=[b_const])

        hT = ph.alloc([16, NT], BF16)
        b_hT = [Buf("hT%d" % c_) for c_ in range(16)]
        ypool = ph.alloc([8, 1024], BF16)
        b_ypool = Buf("ypool")
        NWB = 2
        wbuf = [ph.alloc([16, 512], BF16) for _ in range(NWB)]
        b_wbuf = [Buf("wbuf%d" % i) for i in range(NWB)]
        ropecs = ph.alloc([NS, 2, 64], F32)
        ropecc = ph.alloc([2, 64], F32)
        invc = ph.alloc([4, 16], F32)
        sc.dma(SY, ropecs, ropecs_d.rearrange("p (k a f) -> p k a f", k=NS, a=2), writes=[b_const], slot="const")
        sc.dma(SY, ropecc, ropecc_d.rearrange("p (a f) -> p a f", a=2), writes=[b_const], slot="const")
        sc.dma(SY, invc, invc_d.rearrange("p (g t) -> p g t", g=4), writes=[b_const], slot="const")
        mixw = ph.alloc([4, 2, 256], BF16)
        w2 = ph.alloc([2, 2, 128], BF16)
        p1mark = ph.off

        xbuf = [ph.alloc([NT], F32) for _ in range(2)]
        b_xbuf = [Buf("xbuf0"), Buf("xbuf1")]
        sq = [ph.alloc([NT], BF16) for _ in range(2)]
        b_sq = [Buf("sq0"), Buf("sq1")]
        rstd = ph.alloc([NT], F32)
        b_rstd = Buf("rstd")
        xT_v = xT_d.rearrange("(c p) n -> c p n", p=128)
        segs = [(0, 512), (512, 1024), (1024, 1280)]
        for c in range(16):
            j = c % 2
            sc.dma(SY, xbuf[j], xT_v[c], writes=[b_xbuf[j]], slot="xbuf%d" % j)
            sc.op(A, lambda e, j=j: e.activation(sq[j], xbuf[j], AF.Square), [b_xbuf[j]], [b_sq[j]])
            for si, (a, b) in enumerate(segs):
                mm(ps[si][:, 0:b - a], ones, sq[j][:, a:b], c == 0, c == 15, [b_sq[j], b_const], [psb[si]],
                   inc=(si == 2))
        for si, (a, b) in enumerate(segs):
            sc.op(A, lambda e, si=si, a=a, b=b: e.activation(rstd[:, a:b], ps[si][:, 0:b - a], AF.Sqrt,
                                                               bias=EPS, scale=1.0 / D), [psb[si]], [b_rstd])
        sc.op(V, lambda e: e.reciprocal(rstd, rstd), [b_rstd], [b_rstd])
        for c in range(16):
            j = c % 2
            sc.dma(SY, xbuf[j], xT_v[c], writes=[b_xbuf[j]], slot="xbuf%d" % j)
            sc.op(V, lambda e, c=c, j=j: e.scalar_tensor_tensor(hT[:, c, :], xbuf[j], normw[:, c:c + 1], rstd,
                                                                 ALU.mult, ALU.mult),
                  [b_xbuf[j], b_rstd, b_const], [b_hT[c]])
        ph.off = p1mark

        wcount = [0]

        def load_w(src_ap_cols, kchunks=16, ncols=512):
            i = wcount[0] % NWB
            wcount[0] += 1
            dst = wbuf[i][:, 0:kchunks, 0:ncols]
            sc.dma(G, dst, src_ap_cols.rearrange("(c p) n -> p c n", p=128), writes=[b_wbuf[i]], slot="wbuf%d" % i)
            return wbuf[i], b_wbuf[i]

        qs.reset()
        sgpool = qs.alloc([8, 1024], BF16)
        b_sgpool = Buf("sgpool")
        kcraw = qs.alloc([4, NT], BF16)
        vcraw = qs.alloc([4, NT], BF16)
        b_raw = {"k": Buf("kcraw"), "v": Buf("vcraw")}
        uT = qs.alloc([NT], F32)
        sA = qs.alloc([NT], F32)
        sB = qs.alloc([NT], F32)
        b_uT, b_sA, b_sB = Buf("uT"), Buf("sA"), Buf("sB")
        pooled = qs.alloc([2, 1024], BF16)
        b_pooled = [Buf("pooled0"), Buf("pooled1")]
        pfix = qs.alloc([16], F32)
        b_pfix = Buf("pfix")
        b_mixw = Buf("mixw")
        hdn = qs.alloc([2, 256], BF16)
        b_hdn = Buf("hdn")
        cbias = qs.alloc([2], F32)
        b_cbias = Buf("cbias")
        peT = qs.alloc([2, 32], BF16)
        b_peT = Buf("peT")
        b_w2 = Buf("w2")
        kcr = qs.alloc([2, 128], BF16)
        b_kcr = Buf("kcr")
        ctmp = qs.alloc([4, 128], F32)
        b_ctmp = Buf("ctmp")
        kcTs = qs.alloc([256], BF16)
        b_kcTs = Buf("kcTs")
        vcs = qs.alloc([2, 128], BF16)
        b_vcs = Buf("vcs")

        sc.dma(G, mixw, mix_d.rearrange("g (cc p) d -> p g cc d", p=128), writes=[b_mixw], slot="mixw")
        sc.dma(G, peT[:, 0, :], pek_d, writes=[b_peT], slot="misc_pe")
        sc.dma(G, peT[:, 1, :], pev_d, writes=[b_peT], slot="misc_pe")
        sc.dma(G, w2[:, 0], w2k_d.rearrange("(fb p) e -> p fb e", p=128), writes=[b_w2], slot="misc_w2")
        sc.dma(G, w2[:, 1], w2v_d.rearrange("(fb p) e -> p fb e", p=128), writes=[b_w2], slot="misc_w2")

        def fm_proj(col0, evac):
            wb, bwb = load_w(w_in_d[:, col0:col0 + 512])
            for j in range(4):
                base = 3 * (fm_proj.n % 2)
                fm_proj.n += 1
                for c in range(16):
                    for si, (a, b) in enumerate(segs):
                        mm(ps[base + si][:, 0:b - a], wb[:, c, j * 128:(j + 1) * 128], hT[:, c, a:b],
                           c == 0, c == 15, [bwb, b_hT[c]], [psb[base + si]], inc=(c == 15 and si == 2))
                evac(j, base)
        fm_proj.n = 0

        xs = ph.alloc([512], F32)
        b_xs = Buf("xs")
        rt = ph.alloc([4, 256], F32)
        b_rt = Buf("rt")
        qr = [ph.alloc([4, 128], BF16) for _ in range(2)]
        b_qr = [Buf("qr0"), Buf("qr1")]
        kst = ph.alloc([NS, 512], BF16)
        b_kst = Buf("kst")
        kstK = ph.alloc([4, NS, 128], BF16)
        b_kstK = Buf("kstK")
        kstV = ph.alloc([4, NS, 130], BF16)
        b_kstV = Buf("kstV")
        sc.op(V, lambda e: e.memset(kstV[:, :, :, 128:130], 1.0), writes=[b_kstV])
        tmcount = [0]

        def tm_proj(col0, ncols, evac):
            wb, bwb = load_w(w_in_d[:, col0:col0 + ncols], ncols=ncols)
            pending = None
            for k in range(NS):
                pb = tmcount[0] % 4
                tmcount[0] += 1
                t0 = k * TW + 16
                for c in range(16):
                    mm(ps[pb][:, 0:ncols], hT[:, c, t0:t0 + 128], wb[:, c, 0:ncols], c == 0, c == 15,
                       [bwb, b_hT[c]], [psb[pb]])
                if pending is not None:
                    pending()
                pending = evac(k, pb)
            if pending is not None:
                pending()

        def rope_tm(pb, k, dst_fn, bdst):
            sc.op(A, lambda e: e.activation(xs, ps[pb], AF.Copy), [psb[pb]], [b_xs])
            x4 = xs.rearrange("p (h d) -> p h d", h=4)
            x1, x2 = x4[:, :, 0:64], x4[:, :, 64:128]
            cs = ropecs[:, k, 0, :].unsqueeze(1).broadcast_to([128, 4, 64])
            sn = ropecs[:, k, 1, :].unsqueeze(1).broadcast_to([128, 4, 64])
            r4 = rt.rearrange("p a (h d) -> p a h d", h=4)
            sc.op(V, lambda e: e.tensor_tensor(r4[:, 0], x1, cs, ALU.mult), [b_xs, b_const], [b_rt])
            sc.op(V, lambda e: e.tensor_tensor(r4[:, 1], x2, sn, ALU.mult), [b_xs, b_const], [b_rt])
            sc.op(V, lambda e: e.tensor_tensor(r4[:, 2], x2, cs, ALU.mult), [b_xs, b_const], [b_rt])
            sc.op(V, lambda e: e.tensor_tensor(r4[:, 3], x1, sn, ALU.mult), [b_xs, b_const], [b_rt])
            qi = rope_tm.n % 2
            qrb, bq = qr[qi], b_qr[qi]
            sc.op(V, lambda e: e.tensor_tensor(qrb[:, :, 0:64], r4[:, 0], r4[:, 1], ALU.subtract), [b_rt], [bq])
            sc.op(V, lambda e: e.tensor_tensor(qrb[:, :, 64:128], r4[:, 2], r4[:, 3], ALU.add), [b_rt], [bq])
            pt = 4 + (rope_tm.n % 2)
            rope_tm.n += 1
            ptv = ps[pt].bitcast(BF16)

            def part2():
                for hh in range(4):
                    transp(ptv[:, hh * 128:(hh + 1) * 128], qrb[:, hh, :], [bq], [psb[pt]])
                sc.op(A, lambda e, ptv=ptv: e.activation(dst_fn(), ptv[:, 0:512].rearrange("p (h t) -> p h t", h=4),
                                                         AF.Copy), [psb[pt]], [bdst])
            return part2
        rope_tm.n = 0

        def evac_raw(which):
            dst = kcraw if which == "k" else vcraw

            def f(j, base):
                for si, (a, b) in enumerate(segs):
                    sc.op(A if si != 1 else V, (lambda e, si=si, a=a, b=b: e.activation(
                        dst[:, j, a:b], ps[base + si][:, 0:b - a], AF.Copy)) if si != 1 else
                        (lambda e, si=si, a=a, b=b: e.tensor_copy(dst[:, j, a:b], ps[base + si][:, 0:b - a])),
                        [psb[base + si]], [b_raw[which]])
            return f

        fm_proj(4096, evac_raw("k"))
        fm_proj(4608, evac_raw("v"))

        for wi, which in enumerate(["k", "v"]):
            raw = (kcraw if which == "k" else vcraw).rearrange("p h (k t) -> p h k t", k=NS)
            w1d = w1k_d if which == "k" else w1v_d
            i = wcount[0] % NWB
            wcount[0] += 1
            w1 = wbuf[i].rearrange("p a b -> p (a b)").rearrange("p (l f) -> p l f", l=32)
            bw1 = b_wbuf[i]
            sc.dma(G, w1, w1d.rearrange("l d f -> d l f"), writes=[bw1], slot="wbuf%d" % i)
            for fb in range(2):
                pb = fb
                for l in range(32):
                    mm(ps[pb][:, 0:256].rearrange("p (h k n) -> p h k n", h=4, k=NS),
                       w1[:, l, fb * 128:(fb + 1) * 128], raw[:, :, :, l + 16:l + 16 + 113:16],
                       l == 0, l == 31, [bw1, b_raw[which]], [psb[pb]])
                for l in range(32):
                    mm(ps[pb][:, 256:257], w1[:, l, fb * 128:(fb + 1) * 128], peT[:, wi, l:l + 1],
                       False, l == 31, [bw1, b_peT], [psb[pb]], sgc=True)
                sc.op(V, lambda e, pb=pb, fb=fb: e.tensor_copy(cbias[:, fb:fb + 1], ps[pb][:, 256:257]),
                      [psb[pb]], [b_cbias])
                sc.op(A, lambda e, pb=pb, fb=fb: e.activation(hdn[:, fb, :], ps[pb][:, 0:256], AF.Silu,
                                                                bias=cbias[:, fb:fb + 1]),
                      [psb[pb], b_cbias], [b_hdn])
            for mt in range(2):
                pb = 2 + mt
                for fb in range(2):
                    mm(ps[pb][:, 0:128], hdn[:, fb, mt * 128:(mt + 1) * 128], w2[:, wi, fb, :], fb == 0, fb == 1,
                       [b_hdn, b_w2], [psb[pb]])
                if which == "k":
                    x1 = ps[pb][:, 0:64]
                    x2 = ps[pb][:, 64:128]
                    cs, sn = ropecc[:, 0, :], ropecc[:, 1, :]
                    sc.op(V, lambda e, x1=x1, cs=cs: e.tensor_tensor(ctmp[:, 0, 0:64], x1, cs, ALU.mult), [psb[pb], b_const], [b_ctmp])
                    sc.op(V, lambda e, x2=x2, sn=sn: e.tensor_tensor(ctmp[:, 1, 0:64], x2, sn, ALU.mult), [psb[pb], b_const], [b_ctmp])
                    sc.op(V, lambda e, x2=x2, cs=cs: e.tensor_tensor(ctmp[:, 2, 0:64], x2, cs, ALU.mult), [psb[pb], b_const], [b_ctmp])
                    sc.op(V, lambda e, x1=x1, sn=sn: e.tensor_tensor(ctmp[:, 3, 0:64], x1, sn, ALU.mult), [psb[pb], b_const], [b_ctmp])
                    sc.op(V, lambda e, mt=mt: e.tensor_tensor(kcr[:, mt, 0:64], ctmp[:, 0, 0:64], ctmp[:, 1, 0:64], ALU.subtract), [b_ctmp], [b_kcr])
                    sc.op(V, lambda e, mt=mt: e.tensor_tensor(kcr[:, mt, 64:128], ctmp[:, 2, 0:64], ctmp[:, 3, 0:64], ALU.add), [b_ctmp], [b_kcr])
                    pt = 4 + mt
                    ptv = ps[pt].bitcast(BF16)
                    transp(ptv[:, 0:128], kcr[:, mt, :], [b_kcr], [psb[pt]])
                    sc.op(V, lambda e, ptv=ptv, mt=mt: e.tensor_copy(kcTs[:, mt * 128:(mt + 1) * 128], ptv[:, 0:128]),
                          [psb[pt]], [b_kcTs])
                else:
                    sc.op(V, lambda e, pb=pb, mt=mt: e.tensor_copy(vcs[:, mt, :], ps[pb][:, 0:128]), [psb[pb]], [b_vcs])
            if which == "k":
                sc.dma(SY, kcT_loc.ap(), kcTs, reads=[b_kcTs], writes=[b_pay], slot="pay")
            else:
                sc.dma(SY, vc_loc.ap().rearrange("(mt p) e -> p mt e", p=128), vcs, reads=[b_vcs], writes=[b_pay], slot="pay")

        def kv_block(col0, dst_loc, is_k, sel):
            if is_k and sel:
                tm_proj(col0, 512, lambda k, pb: rope_tm(pb, k, lambda: kstK[:, :, k, :], b_kstK))
                sc.dma(SY, dst_loc.ap(), kstK.rearrange("p h k t -> p (h k t)"), reads=[b_kstK], writes=[b_pay],
                       slot="pay")
            elif sel:
                tm_proj(col0, 512, lambda k, pb: sc.op(
                    V, lambda e: e.tensor_copy(kstV[:, :, k, 0:128], ps[pb].rearrange("p (h e) -> p h e", h=4)),
                    [psb[pb]], [b_kstV]))
                sc.dma(SY, dst_loc.ap(), kstV.rearrange("p h k e -> p (h k e)"), reads=[b_kstV], writes=[b_pay],
                       slot="pay")
            elif is_k:
                tm_proj(col0, 512, lambda k, pb: rope_tm(
                    pb, k, lambda: kst[:, k, :].rearrange("p (h t) -> p h t", h=4), b_kst))
                sc.dma(SY, dst_loc.ap().rearrange("(k p) n -> p k n", p=128), kst, reads=[b_kst], writes=[b_pay], slot="pay")
            else:
                tm_proj(col0, 512, lambda k, pb: sc.op(
                    V, lambda e: e.tensor_copy(kst[:, k, :], ps[pb]), [psb[pb]], [b_kst]))
                sc.dma(SY, dst_loc.ap().rearrange("(k p) n -> p k n", p=128), kst, reads=[b_kst], writes=[b_pay], slot="pay")

        kv_block(5120, kselT_loc, True, True)
        kv_block(5632, vsel_loc, False, True)
        kv_block(6144, kwinT_loc, True, False)
        kv_block(6656, vwin_loc, False, False)

        cc_sem = stack.enter_context(nc.semaphore("cc"))
        sc.dsem["cc"] = cc_sem
        sc.dcnt["cc"] = 0
        b_all = Buf("all")
        pairs = [(kcT_loc, kcT_all), (vc_loc, vc_all), (kwinT_loc, kwinT_all), (vwin_loc, vwin_all),
                 (kselT_loc, kselT_all), (vsel_loc, vsel_all)]
        for src, dst in pairs:
            sc.dcnt["cc"] += 1
            dap = dst.ap()[4096:, :] if dst in (kwinT_all, vwin_all) else dst.ap()
            sc.raw(G, lambda e, src=src, dap=dap: e.collective_compute(
                "AllGather", ALU.bypass, replica_groups=[list(range(NC))],
                ins=[src.ap().opt()], outs=[dap.opt()]).then_inc(cc_sem, 1),
                reads=[b_pay], writes=[b_all], ev=(("D", "cc"), sc.dcnt["cc"]))

        wscrK = dt("wscrK", [5 * NS * 128, 512], BF16)
        wscrV = dt("wscrV", [5 * NS * 128, 512], BF16)
        b_wscr = Buf("wscr")
        wregs = {}

        for t_all in (kwinT_all, vwin_all):
            sc.dma(SY, t_all.ap()[0:4096, :], t_all.ap()[4096 + 31 * 128:4096 + 63 * 128, :], reads=[b_all],
                   writes=[b_all], slot="xcopy")

        def wsrc(eng, t_all):
            pid = eng.partition_id()
            return t_all.ap()[bass.ds(pid * 1024, 5 * 1024), :]
        sc.dma(SY, wscrK.ap(), None, reads=[b_all], writes=[b_wscr], slot="wscr",
               in_fn=lambda eng: wsrc(eng, kwinT_all))
        sc.dma(SY, wscrV.ap(), None, reads=[b_all], writes=[b_wscr], slot="wscr",
               in_fn=lambda eng: wsrc(eng, vwin_all))

        def own(ap_seg, a, b):
            return ap_seg

        def evac_gpool(blk0):
            def f(j, base):
                blk = blk0 + j
                for k in range(NS):
                    t0 = k * TW + 16
                    si = 0 if t0 + 128 <= 512 else (1 if t0 >= 512 and t0 + 128 <= 1024 else (2 if t0 >= 1024 else -1))
                    if si >= 0:
                        a = segs[si][0]
                        sc.op(A, lambda e, blk=blk, k=k, si=si, a=a, t0=t0: e.activation(
                            sgpool[:, blk, k * 128:(k + 1) * 128], ps[base + si][:, t0 - a:t0 - a + 128], AF.Silu),
                            [psb[base + si]], [b_sgpool])
                    else:
                        for si2, (a, b) in enumerate(segs):
                            lo, hi = max(t0, a), min(t0 + 128, b)
                            if lo < hi:
                                sc.op(A, lambda e, blk=blk, k=k, si2=si2, a=a, lo=lo, hi=hi, t0=t0: e.activation(
                                    sgpool[:, blk, k * 128 + lo - t0:k * 128 + hi - t0],
                                    ps[base + si2][:, lo - a:hi - a], AF.Silu), [psb[base + si2]], [b_sgpool])
            return f

        fm_proj(1024, evac_gpool(0))
        fm_proj(1536, evac_gpool(4))

        def evac_upool(blk0):
            def f(j, base):
                blk = blk0 + j
                g = blk // 2
                cc = blk % 2
                for si, (a, b) in enumerate(segs):
                    sc.op(A, lambda e, si=si, a=a, b=b: e.activation(uT[:, a:b], ps[base + si][:, 0:b - a], AF.Copy),
                          [psb[base + si]], [b_uT])
                u3 = uT.rearrange("p (k t) -> p k t", k=NS)
                a3 = sA.rearrange("p (k t) -> p k t", k=NS)
                b3 = sB.rearrange("p (k t) -> p k t", k=NS)
                cur, bcur = u3, b_uT
                tmp = [(a3, b_sA), (b3, b_sB)]
                sh = 1
                lvl = 0
                w = 2 << g
                while sh < w:
                    dst, bdst = tmp[lvl % 2]
                    sc.op(V, lambda e, dst=dst, cur=cur, sh=sh: e.tensor_tensor(
                        dst[:, :, 2 * sh - 1:144], cur[:, :, 2 * sh - 1:144], cur[:, :, sh - 1:144 - sh], ALU.add),
                        [bcur], [bdst])
                    cur, bcur = dst, bdst
                    sh *= 2
                    lvl += 1
                p3 = pooled[:, cc, :].rearrange("p (k t) -> p k t", k=NS)
                sc.op(V, lambda e, cur=cur, p3=p3, w=w: e.scalar_tensor_tensor(
                    p3, cur[:, :, 16:144], 1.0 / w, u3[:, :, 16:144], ALU.mult, ALU.subtract),
                    [bcur, b_uT], [b_pooled[cc]])
                sc.op(V, lambda e, cur=cur, g=g: e.tensor_tensor(pfix, cur[:, 0, 16:32], invc[:, g, :], ALU.mult),
                      [bcur, b_const], [b_pfix])
                sc.op(V, lambda e, cc=cc: e.tensor_tensor(pooled[:, cc, 0:16], pfix, uT[:, 16:32], ALU.subtract),
                      [b_pfix, b_uT], [b_pooled[cc]])
                if cc == 1:
                    for db in range(2):
                        oblk = g * 2 + db
                        for half in range(2):
                            pb = 6 + half
                            for c2 in range(2):
                                mm(ps[pb], mixw[:, g, c2, db * 128:(db + 1) * 128],
                                   pooled[:, c2, half * 512:(half + 1) * 512], c2 == 0, c2 == 1,
                                   [b_mixw, b_pooled[0], b_pooled[1]], [psb[pb]])
                            sc.op(V, lambda e, pb=pb, oblk=oblk, half=half: e.scalar_tensor_tensor(
                                ypool[:, oblk, half * 512:(half + 1) * 512], ps[pb], pscale[:, oblk:oblk + 1],
                                sgpool[:, oblk, half * 512:(half + 1) * 512], ALU.mult, ALU.mult),
                                [psb[pb], b_sgpool, b_const], [b_ypool])
            return f

        fm_proj(0, evac_upool(0))
        fm_proj(512, evac_upool(4))

        sc.barrier()
        qs.reset()
        QT = qs.alloc([16, 1024], BF16)
        sgn = qs.alloc([NS, 2048], BF16)

        for qb in range(4):
            tm_proj(2048 + qb * 512, 512, lambda k, pb, qb=qb: rope_tm(
                pb, k, lambda: QT[:, qb * 4:qb * 4 + 4, k * 128:(k + 1) * 128], b_QT))

        for gb in range(4):
            tm_proj(7168 + gb * 512, 512, lambda k, pb, gb=gb: sc.op(
                A, lambda e: e.activation(sgn[:, k, gb * 512:(gb + 1) * 512], ps[pb], AF.Silu), [psb[pb]], [b_sgn[k]]))
        tm_proj(9216, 48, lambda k, pb: sc.op(
            A, lambda e: e.activation(gates[:, k, :], ps[pb][:, 0:48], AF.Sigmoid), [psb[pb]], [b_gates]))

        hT4 = hT.rearrange("p c (k t) -> p c k t", k=NS)
        for c0 in range(0, 16, 4):
            sc.dma(SY, hT_sp.ap().rearrange("p (c k t) -> p c k t", c=16, k=NS)[:, c0:c0 + 4],
                   hT4[:, c0:c0 + 4, :, 16:144], reads=b_hT[c0:c0 + 4], slot="spill")
        sc.dma(SY, yp_sp.ap().rearrange("p (b t) -> p b t", b=8), ypool, reads=[b_ypool], slot="spill")

        if stage == 1:
            sc.barrier()
            sc.dma(SY, dbg["QT"].rearrange("p (a b) -> p a b", a=16), QT, reads=[b_QT], slot="dbg")
            sc.dma(SY, dbg["sgn"].rearrange("p (a b) -> p a b", a=NS), sgn, reads=b_sgn, slot="dbg")
            sc.dma(SY, dbg["gates"].rearrange("p (a b) -> p a b", a=NS), gates, reads=[b_gates], slot="dbg")
            sc.dma(SY, dbg["ypool"].rearrange("p (a b) -> p a b", a=8), ypool, reads=[b_ypool], slot="dbg")
            for nm, loc in [("kselT", kselT_loc), ("kwinT", kwinT_loc), ("vsel", vsel_loc), ("vwin", vwin_loc),
                            ("kcT", kcT_loc), ("vc", vc_loc)]:
                sc.dma(SY, dbg[nm], loc.ap(), slot="dbg")
            sc.barrier()
            sc.emit(block)
            return nc

        sc.barrier()

        ph.reset()
        KselT2 = [ph.alloc([S], BF16) for _ in range(2)]
        Vsel2 = [ph.alloc([64, 130], BF16) for _ in range(2)]
        emat = ph.alloc([S], BF16)
        selcm = ph.alloc([8, 512], BF16)
        wmk = ph.alloc([7, 512], BF16)
        cmpm2 = [ph.alloc([4, 512], BF16) for _ in range(2)]
        b_cmpm = [Buf('cmpm0'), Buf('cmpm1')]
        amat = ph.alloc([4, 128], BF16)
        fmm = ph.alloc([NS, 2, 128], F32)
        kcT = ph.alloc([512], BF16)
        vcx = ph.alloc([4, 130], BF16)
        kwT = [ph.alloc([5, 128], BF16) for _ in range(2)]
        vwx = [ph.alloc([5, 130], BF16) for _ in range(2)]
        NPT = 4
        PT = [ph.alloc([512], BF16) for _ in range(NPT)]
        PTc = ph.alloc([4, 512], BF16)
        imp = ph.alloc([128], F32)
        imp2 = ph.alloc([128], F32)
        m8a = ph.alloc([8], F32)
        m8b = ph.alloc([8], F32)
        mb = ph.alloc([128], BF16)
        mbT = ph.alloc([4, 128], BF16)
        den = ph.alloc([3, 4], F32)
        wgt = ph.alloc([3, 4], F32)
        ocomb = ph.alloc([4, 128], F32)
        b_ksel2, b_vsel2 = [Buf("ksel0"), Buf("ksel1")], [Buf("vsel0"), Buf("vsel1")]
        b_p2c, b_kc, b_vc = Buf("p2c"), Buf("kc"), Buf("vc")
        b_kw = [Buf("kw0"), Buf("kw1")]
        b_vw = [Buf("vw0"), Buf("vw1")]
        b_PT = [Buf("PT%d" % i) for i in range(NPT)]
        b_PTc = [Buf("PTc%d" % i) for i in range(4)]
        b_imp, b_imp2, b_m8a, b_m8b, b_mb, b_mbT = (Buf("imp"), Buf("imp2"), Buf("m8a"), Buf("m8b"), Buf("mb"),
                                                     Buf("mbT"))
        b_den, b_wgt, b_ocomb = Buf("den"), Buf("wgt"), Buf("ocomb")

        sc.dma(SY, emat, emat_d, writes=[b_p2c], slot="p2c")
        sc.dma(SY, selcm, selcm_d.rearrange("p (a b) -> p a b", a=8), writes=[b_p2c], slot="p2c")
        sc.dma(SY, wmk, wm_d.rearrange("p (a b) -> p a b", a=7), writes=[b_p2c], slot="p2c")
        sc.dma(SY, amat, amat_d.rearrange("p (a b) -> p a b", a=4), writes=[b_p2c], slot="p2c")
        sc.dma(SY, fmm, fmm_d.rearrange("p (k a b) -> p k a b", k=NS, a=2), writes=[b_p2c], slot="p2c")
        sc.op(V, lambda e: e.memset(vcx[:, :, 128:130], 1.0), writes=[b_vc])
        for i in range(2):
            sc.op(V, lambda e, i=i: e.memset(vwx[i][:, :, 128:130], 1.0), writes=[b_vw[i]])

        kc_v = kcT_all.ap().rearrange("(r e) (h n) -> e h r n", r=NC, h=4)
        vc_v = vc_all.ap().rearrange("(r h n) e -> r h n e", r=NC, h=4)

        sbank = [0]
        SBANKS = [0, 1]
        ptc = [0]
        accn = [0]
        accsets = [(2, 3), (4, 5)]
        TB = 7
        wcnt = [0]

        wK_v = wscrK.ap().rearrange("(j k d) n -> d j k n", j=5, k=NS)
        wV_v = wscrV.ap().rearrange("(j k d) n -> d j k n", j=5, k=NS)

        def load_window(h, k):
            i = wcnt[0] % 2
            wcnt[0] += 1
            sc.dma(SY, kwT[i], wK_v[:, :, k, h * 128:(h + 1) * 128], reads=[b_wscr], writes=[b_kw[i]], slot="kw%d" % i)
            sc.dma(SY, vwx[i][:, :, 0:128], wV_v[:, :, k, h * 128:(h + 1) * 128], reads=[b_wscr], writes=[b_vw[i]],
                   slot="vw%d" % i)
            return i

        def branch(Qh, tiles, evac):
            n = len(tiles)
            aset = accsets[accn[0] % 2]
            accn[0] += 1
            state = {}

            def emit_S(t):
                kT, kreads, masks, vr, vreads, ptd = tiles[t]
                sb = SBANKS[sbank[0] % len(SBANKS)]
                sbank[0] += 1
                so = ps[sb].rearrange("p (g q) -> p g q", g=4)
                mm(so, kT, Qh, True, len(masks) == 0, kreads + [b_QT], [psb[sb]])
                for mi, (ml, mr, mreads) in enumerate(masks):
                    mm(ps[sb], ml, mr, False, mi == len(masks) - 1, mreads, [psb[sb]])
                if ptd is None:
                    pi = ptc[0] % NPT
                    ptc[0] += 1
                    dst, bd = PT[pi], b_PT[pi]
                else:
                    dst, bd = ptd
                sc.op(A, lambda e, dst=dst, sb=sb: e.activation(dst, ps[sb], AF.Exp, scale=SCALE), [psb[sb]], [bd])
                state[t] = (dst, bd)

            def emit_PV(t):
                kT, kreads, masks, vr, vreads, ptd = tiles[t]
                dst, bd = state[t]
                for g in range(4):
                    bank = aset[g // 2]
                    o = ps[bank][:, (g % 2) * 129:(g % 2) * 129 + 129]
                    mm(o, dst[:, g * 128:(g + 1) * 128], vr, (t == 0 and g % 2 == 0), t == n - 1,
                       [bd] + vreads, [psb[bank]], inc=(t == n - 1 and g % 2 == 1) or (g == 3), sgc=True)

            emit_S(0)
            for t in range(n):
                if t + 1 < n:
                    emit_S(t + 1)
                emit_PV(t)
            evac(aset)

        def combine(bi, aset, h, k, first, last):
            for half in range(2):
                bank = aset[half]
                dv = ps[bank][:, 128:258:129]
                sc.op(V, lambda e, dv=dv, half=half: e.tensor_scalar(den[:, bi, half * 2:half * 2 + 2], dv, 1e-30, None,
                                                                     ALU.max), [psb[bank]], [b_den])
            sc.op(V, lambda e: e.reciprocal(den[:, bi, :], den[:, bi, :]), [b_den], [b_den])
            gsl = gates[:, k, h * 12 + bi:h * 12 + 12:3]
            sc.op(V, lambda e, gsl=gsl: e.tensor_tensor(wgt[:, bi, :], den[:, bi, :], gsl, ALU.mult),
                  [b_den, b_gates], [b_wgt])
            for g in range(4):
                bank = aset[g // 2]
                o = ps[bank][:, (g % 2) * 129:(g % 2) * 129 + 128]
                if first:
                    sc.op(V, lambda e, o=o, g=g: e.tensor_scalar(ocomb[:, g, :], o, wgt[:, bi, g:g + 1], None, ALU.mult),
                          [psb[bank], b_wgt], [b_ocomb])
                else:
                    sc.op(V, lambda e, o=o, g=g: e.scalar_tensor_tensor(ocomb[:, g, :], o, wgt[:, bi, g:g + 1],
                                                                         ocomb[:, g, :], ALU.mult, ALU.add),
                          [psb[bank], b_wgt, b_ocomb], [b_ocomb])
            if last:
                sl = sgn[:, k, h * 512:(h + 1) * 512]
                sc.op(V, lambda e, sl=sl: e.tensor_tensor(sl, ocomb.rearrange("p g e -> p (g e)"), sl, ALU.mult),
                      [b_ocomb, b_sgn[k]], [b_sgn[k]])

        cmpm_v = cmpm_d.rearrange("p (k a b) -> p k a b", k=NS, a=4)

        def load_kv(h):
            hp = h % 2
            for r in range(NC):
                sc.dma(SY, KselT2[hp][:, r * 1024:(r + 1) * 1024],
                       kselT_all.ap()[r * 128:(r + 1) * 128, h * 1024:(h + 1) * 1024],
                       reads=[b_all], writes=[b_ksel2[hp]], slot="ksel%d" % hp)
                sc.dma(SY, Vsel2[hp][:, r * 8:(r + 1) * 8, :].rearrange("p k e -> p (k e)"),
                       vsel_all.ap()[r * 128:(r + 1) * 128, h * 1040:(h + 1) * 1040],
                       reads=[b_all], writes=[b_vsel2[hp]], slot="vsel%d" % hp)

        load_kv(0)
        stepn = [0]
        for h in range(nh):
            KselT, Vsel = KselT2[h % 2], Vsel2[h % 2]
            b_ksel, b_vsel = b_ksel2[h % 2], b_vsel2[h % 2]
            sc.dma(SY, kcT.rearrange("p (r n) -> p r n", r=NC), kc_v[:, h], reads=[b_all], writes=[b_kc], slot="kc")
            for tc in range(4):
                for r2 in range(2):
                    sc.dma(SY, vcx[r2 * 64:(r2 + 1) * 64, tc, 0:128], vc_v[2 * tc + r2, h], reads=[b_all],
                           writes=[b_vc], slot="vc")
            for k in range(NS):
                Qh = QT[:, 4 * h:4 * h + 4, k * 128:(k + 1) * 128]
                wi = load_window(h, k)
                ci = stepn[0] % 2
                stepn[0] += 1
                sc.dma(SY, cmpm2[ci], cmpm_v[:, k], writes=[b_cmpm[ci]], slot="cmpm%d" % ci)
                if k == 1 and h + 1 < nh:
                    load_kv(h + 1)
                tiles = []
                for tc in range(4):
                    tiles.append((kcT[:, tc * 128:(tc + 1) * 128], [b_kc],
                                  [(ident, cmpm2[ci][:, tc, :], [b_const, b_cmpm[ci]])],
                                  vcx[:, tc, 0:129], [b_vc], (PTc[:, tc, :], b_PTc[tc])))

                def evac_c(aset, h=h, k=k):
                    UB = 6
                    for tc in range(4):
                        for g in range(4):
                            mm(ps[UB][:, g * 128:(g + 1) * 128], PTc[:, tc, g * 128:(g + 1) * 128], amat[:, tc, :],
                               (tc == 0 and g == 0), tc == 3, [b_PTc[tc], b_p2c], [psb[UB]],
                               inc=(tc == 3 and g == 3), sgc=True)
                    combine(0, aset, h, k, True, False)
                    sc.op(V, lambda e: e.tensor_scalar(imp, ps[UB][:, 0:128], den[:, 0, 0:1], None, ALU.mult),
                          [psb[UB], b_den], [b_imp])
                    for g in range(1, 4):
                        sc.op(V, lambda e, g=g: e.scalar_tensor_tensor(imp, ps[UB][:, g * 128:(g + 1) * 128],
                                                                        den[:, 0, g:g + 1], imp, ALU.mult, ALU.add),
                              [psb[UB], b_den, b_imp], [b_imp])
                    sc.op(V, lambda e, k=k: e.tensor_tensor(imp, imp, fmm[:, k, 0, :], ALU.mult), [b_imp, b_p2c], [b_imp])
                    sc.op(V, lambda e, k=k: e.tensor_tensor(imp, imp, fmm[:, k, 1, :], ALU.add), [b_imp, b_p2c], [b_imp])
                    sc.op(V, lambda e: e.max(m8a, imp), [b_imp], [b_m8a])
                    sc.op(V, lambda e: e.match_replace(imp2, m8a, imp, -3.0e38), [b_m8a, b_imp], [b_imp2])
                    sc.op(V, lambda e: e.max(m8b, imp2), [b_imp2], [b_m8b])
                    sc.op(V, lambda e: e.tensor_scalar(mb, imp, m8b[:, 7:8], NEGB, ALU.is_lt, ALU.mult),
                          [b_imp, b_m8b], [b_mb])
                    tv = ps[TB].bitcast(BF16)
                    transp(tv[:, 0:128], mb, [b_mb], [psb[TB]])
                    sc.op(A, lambda e, tv=tv: e.activation(mbT, tv[:, 0:128].unsqueeze(1).broadcast_to([128, 4, 128]),
                                                           AF.Copy), [psb[TB]], [b_mbT])

                branch(Qh, tiles, evac_c)
                tiles = []
                for j in range(5):
                    if k == 0:
                        masks = [(ident, wmk[:, j, :], [b_const, b_p2c])]
                    elif j == 0:
                        masks = [(ident, wmk[:, 5, :], [b_const, b_p2c])]
                    elif j == 4:
                        masks = [(ident, wmk[:, 6, :], [b_const, b_p2c])]
                    else:
                        masks = []
                    tiles.append((kwT[wi][:, j, :], [b_kw[wi]], masks, vwx[wi][:, j, 0:129], [b_vw[wi]], None))
                branch(Qh, tiles, lambda aset, h=h, k=k: combine(2, aset, h, k, False, False))
                tiles = []
                mbT2 = mbT.rearrange("p g q -> p (g q)")
                for t in range(8 * k + 8):
                    masks = [(emat[:, t * 128:(t + 1) * 128], mbT2, [b_p2c, b_mbT])]
                    if t >= 8 * k:
                        masks.append((ident, selcm[:, t - 8 * k, :], [b_const, b_p2c]))
                    pos = (t % 8) * 8 + t // 8
                    tiles.append((KselT[:, pos * 128:(pos + 1) * 128], [b_ksel], masks, Vsel[:, pos, 0:129], [b_vsel],
                                  None))
                branch(Qh, tiles, lambda aset, h=h, k=k: combine(1, aset, h, k, False, True))

        if stage == 2:
            import os
            for _i in range(int(os.environ.get("K_DUMMY_MM", "0"))):
                mm(ps[6][:, 0:8], ident, ident[:, 0:8], True, True, [b_const], [psb[6]], inc=(_i % 64 == 63))
            sc.barrier()
            sc.dma(SY, dbg["og"].rearrange("p (a b) -> p a b", a=NS), sgn, reads=b_sgn, slot="dbg")
            sc.barrier()
            sc.emit(block)
            return nc

        sc.barrier()
        ph.reset()
        qs.reset()
        ogT = qs.alloc([16, 1024], BF16)
        sgn = qs.alloc([NS, 2048], BF16)
        b_ogT = Buf("ogT")
        hTo = ph.alloc([16, 1024], BF16)
        b_hTo = Buf("hTo")
        ypl = ph.alloc([8, 1024], BF16)
        b_ypl = Buf("ypl")
        NW3 = 2
        w3 = [ph.alloc([56, 256], BF16) for _ in range(NW3)]
        b_w3 = [Buf("w3_%d" % i) for i in range(NW3)]
        sg0 = ph.alloc([512], F32)
        sg1 = ph.alloc([512], F32)
        m0 = ph.alloc([512], F32)
        m1 = ph.alloc([512], F32)
        b_sg0, b_sg1, b_m0, b_m1 = Buf("sg0"), Buf("sg1"), Buf("m0"), Buf("m1")
        sc.dma(SY, hTo, hT_sp.ap().rearrange("p (c t) -> p c t", c=16), writes=[b_hTo], slot="p3a")
        sc.dma(SY, ypl, yp_sp.ap().rearrange("p (b t) -> p b t", b=8), writes=[b_ypl], slot="p3a")
        tcount = [0]
        for k in range(NS):
            for c4 in range(4):
                tb = 6 + (tcount[0] % 2)
                tcount[0] += 1
                tv = ps[tb].bitcast(BF16)
                for cc in range(4):
                    c = c4 * 4 + cc
                    transp(tv[:, cc * 128:(cc + 1) * 128], sgn[:, k, c * 128:(c + 1) * 128], [b_sgn[k]], [psb[tb]])
                sc.op(A if (tcount[0] % 2) else V,
                      (lambda e, tv=tv, c4=c4, k=k: e.activation(
                          ogT[:, c4 * 4:c4 * 4 + 4, k * 128:(k + 1) * 128],
                          tv[:, 0:512].rearrange("p (c t) -> p c t", c=4), AF.Copy)) if (tcount[0] % 2) else
                      (lambda e, tv=tv, c4=c4, k=k: e.tensor_copy(
                          ogT[:, c4 * 4:c4 * 4 + 4, k * 128:(k + 1) * 128],
                          tv[:, 0:512].rearrange("p (c t) -> p c t", c=4))),
                      [psb[tb]], [b_ogT])
        sc.barrier()
        mergedT = sgn.rearrange("p k n -> p (k n)").rearrange("p (c t) -> p c t", c=16)
        b_mg = Buf("merged")
        wpo_v = wpo_d.rearrange("(c p) n -> p c n", p=128)
        wno_v = wno_d.rearrange("(c p) n -> p c n", p=128)
        wmg_v = wmg_d.rearrange("(c p) n -> p c n", p=128)
        pcount = [0]
        for ob in range(16):
            i = (ob // 2) % NW3
            o2 = ob % 2
            if o2 == 0:
                cols = slice(ob * 128, ob * 128 + 256)
                sc.dma(G, w3[i][:, 0:8, :], wpo_v[:, :, cols], writes=[b_w3[i]], slot="w3_%d" % i)
                sc.dma(G, w3[i][:, 8:24, :], wno_v[:, :, cols], writes=[b_w3[i]], slot="w3_%d" % i)
                sc.dma(G, w3[i][:, 24:40, :], wmg_v[:, :, cols], writes=[b_w3[i]], slot="w3_%d" % i)
                sc.dma(G, w3[i][:, 40:56, :], wmg_v[:, :, D + ob * 128:D + ob * 128 + 256], writes=[b_w3[i]],
                       slot="w3_%d" % i)
            w3s = w3[i][:, :, o2 * 128:(o2 + 1) * 128]
            for tg in range(2):
                tsl = slice(tg * 512, (tg + 1) * 512)
                pa, pg0, pbk, pg1 = [(pcount[0] * 4 + x) % 6 for x in range(4)]
                pcount[0] += 1
                for c in range(8):
                    mm(ps[pa], w3s[:, c, :], ypl[:, c, tsl], c == 0, c == 7, [b_w3[i], b_ypl], [psb[pa]])
                for c in range(16):
                    mm(ps[pg0], w3s[:, 24 + c, :], hTo[:, c, tsl], c == 0, c == 15, [b_w3[i], b_hTo], [psb[pg0]])
                for c in range(16):
                    mm(ps[pbk], w3s[:, 8 + c, :], ogT[:, c, tsl], c == 0, c == 15, [b_w3[i], b_ogT], [psb[pbk]])
                for c in range(16):
                    mm(ps[pg1], w3s[:, 40 + c, :], hTo[:, c, tsl], c == 0, c == 15, [b_w3[i], b_hTo], [psb[pg1]])
                sc.op(A, lambda e, pg0=pg0, ob=ob: e.activation(sg0, ps[pg0], AF.Sigmoid, bias=bmerge[:, ob:ob + 1]),
                      [psb[pg0], b_const], [b_sg0])
                sc.op(A, lambda e, pg1=pg1, ob=ob: e.activation(sg1, ps[pg1], AF.Sigmoid,
                                                                 bias=bmerge[:, 16 + ob:17 + ob]),
                      [psb[pg1], b_const], [b_sg1])
                sc.op(V, lambda e, pa=pa: e.tensor_tensor(m0, ps[pa], sg0, ALU.mult), [psb[pa], b_sg0], [b_m0])
                sc.op(V, lambda e, pbk=pbk: e.tensor_tensor(m1, ps[pbk], sg1, ALU.mult), [psb[pbk], b_sg1], [b_m1])
                sc.op(V, lambda e, ob=ob, tsl=tsl: e.tensor_tensor(mergedT[:, ob, tsl], m0, m1, ALU.add),
                      [b_m0, b_m1], [b_mg])
        sc.barrier()
        ph.reset()
        wout = ph.alloc([16, D], BF16)
        b_wout = [Buf("wout%d" % c_) for c_ in range(4)]
        fnw = ph.alloc([D], F32)
        xt = [ph.alloc([D], F32) for _ in range(2)]
        b_xt = [Buf("xt0"), Buf("xt1")]
        junk = ph.alloc([D], BF16)
        b_junk = Buf("junk")
        ssq = ph.alloc([2], F32)
        b_ssq = Buf("ssq")
        wout_v = wout_d.rearrange("(c p) n -> p c n", p=128)
        for cg in range(4):
            sc.dma(G, wout[:, :, cg * 512:(cg + 1) * 512], wout_v[:, :, cg * 512:(cg + 1) * 512], writes=[b_wout[cg]],
                   slot="wout%d" % cg)
        sc.dma(SY, fnw, fnw_d, writes=[b_const], slot="const")
        for k in range(NS):
            i = k % 2
            sc.dma(SY, xt[i], xtok_d[k * 128:(k + 1) * 128, :], writes=[b_xt[i]], slot="xt%d" % i)
            for cg in range(4):
                pb = (k * 4 + cg) % 6
                for c in range(16):
                    mm(ps[pb], mergedT[:, c, k * 128:(k + 1) * 128], wout[:, c, cg * 512:(cg + 1) * 512],
                       c == 0, c == 15, [b_mg, b_wout[cg]], [psb[pb]])
                sc.op(V, lambda e, i=i, pb=pb, cg=cg: e.tensor_tensor(xt[i][:, cg * 512:(cg + 1) * 512], ps[pb],
                                                                      xt[i][:, cg * 512:(cg + 1) * 512], ALU.add),
                      [psb[pb], b_xt[i]], [b_xt[i]])
            sc.op(V, lambda e, i=i: e.memset(ssq[:, i:i + 1], 0.0), [], [b_ssq])
            sc.op(A, lambda e, i=i: e.activation(junk, xt[i], AF.Square, accum_out=ssq[:, i:i + 1]),
                  [b_xt[i]], [b_junk, b_ssq])
            sc.op(A, lambda e, i=i: e.activation(ssq[:, i:i + 1], ssq[:, i:i + 1], AF.Sqrt, bias=EPS, scale=1.0 / D),
                  [b_ssq], [b_ssq])
            sc.op(V, lambda e, i=i: e.reciprocal(ssq[:, i:i + 1], ssq[:, i:i + 1]), [b_ssq], [b_ssq])
            sc.op(V, lambda e, i=i: e.scalar_tensor_tensor(xt[i], xt[i], ssq[:, i:i + 1], fnw, ALU.mult, ALU.mult),
                  [b_xt[i], b_ssq, b_const], [b_xt[i]])
            sc.dma(SY, out_d[k * 128:(k + 1) * 128, :], xt[i], reads=[b_xt[i]], slot="xt%d" % i)
        sc.barrier()
        sc.emit(block)
    return nc


def _host_inputs(inp):
    f32 = np.float32
    bf = ml_dtypes.bfloat16
    x = np.asarray(inp["x"], f32)[0]
    xpad = np.zeros((S + 32, D), f32)
    xpad[16:16 + S] = x
    half = 64
    inv = (np.float32(10000.0) ** (-np.arange(half, dtype=f32) / np.float32(half))).astype(f32)
    common = {
        "normw": np.ascontiguousarray(np.asarray(inp["norm_w"], f32)[0].reshape(16, 128).T),
        "w_in": np.ascontiguousarray(np.asarray(inp["w_in"], f32)[0]),
        "pool_mix": np.ascontiguousarray(np.asarray(inp["pool_mix"], f32)[0]),
        "pscale": np.ascontiguousarray(np.asarray(inp["pool_scale"], f32)[0].reshape(8, 128).T),
        "pe_kT": np.ascontiguousarray(np.asarray(inp["cmp_pe_k"], f32)[0].T),
        "pe_vT": np.ascontiguousarray(np.asarray(inp["cmp_pe_v"], f32)[0].T),
        "w1k": np.ascontiguousarray(np.asarray(inp["cmp_w1_k"], f32)[0]),
        "w1v": np.ascontiguousarray(np.asarray(inp["cmp_w1_v"], f32)[0]),
        "w2k": np.ascontiguousarray(np.asarray(inp["cmp_w2_k"], f32)[0]),
        "w2v": np.ascontiguousarray(np.asarray(inp["cmp_w2_v"], f32)[0]),
        "w_pool_out": np.ascontiguousarray(np.asarray(inp["w_pool_out"], f32)[0]),
        "w_nsa_out": np.ascontiguousarray(np.asarray(inp["w_nsa_out"], f32)[0]),
        "w_merge": np.ascontiguousarray(np.asarray(inp["w_merge"], f32)[0]),
        "bmerge": np.ascontiguousarray(np.asarray(inp["b_merge"], f32)[0].reshape(32, 128).T),
        "w_out": np.ascontiguousarray(np.asarray(inp["w_out"], f32)[0]),
        "fnw": np.ascontiguousarray(np.broadcast_to(np.asarray(inp["final_norm_w"], f32)[None, :], (128, D))),
        "ident": np.eye(128, dtype=f32).astype(bf),
    }
    emat = np.zeros((128, S), f32)
    emat[np.arange(S) // 64, np.arange(S)] = 1.0
    common["emat"] = emat.astype(bf)
    kk = np.arange(128)[:, None]
    qq = np.arange(128)[None, :]
    caus = np.where(kk <= qq, 0.0, NEGB).astype(f32)
    upper = np.where(kk > qq, 0.0, NEGB).astype(f32)
    full = np.zeros((128, 128), f32)
    none = np.full((128, 128), NEGB, f32)
    maps = []
    for c in range(NC):
        m = dict(common)
        blocks = [8 * k + c for k in range(NS)]
        xT = np.zeros((D, NT), f32)
        for k, i in enumerate(blocks):
            xT[:, k * TW:(k + 1) * TW] = xpad[128 * i:128 * i + TW].T
        m["xT"] = xT
        m["xtok"] = np.ascontiguousarray(np.concatenate([x[128 * i:128 * i + 128] for i in blocks], 0))
        cs = np.zeros((128, NS, 2, 64), f32)
        for k, i in enumerate(blocks):
            pos = (128 * i + np.arange(128)).astype(f32)
            ang = pos[:, None] * inv[None, :]
            cs[:, k, 0] = np.cos(ang)
            cs[:, k, 1] = np.sin(ang)
        m["ropecs"] = cs.reshape(128, -1)
        cc = np.zeros((128, 2, 64), f32)
        for h2 in range(2):
            for k, i in enumerate(blocks):
                n = 8 * i + np.arange(8)
                pos = (n * 16 + 31).astype(f32)
                ang = pos[:, None] * inv[None, :]
                r0 = h2 * 64 + k * 8
                cc[r0:r0 + 8, 0] = np.cos(ang)
                cc[r0:r0 + 8, 1] = np.sin(ang)
        m["ropecc"] = cc.reshape(128, -1)
        ic = np.zeros((128, 4, 16), f32)
        for g in range(4):
            w = 2 << g
            t = 128 * blocks[0] + np.arange(16)
            ic[:, g, :] = 1.0 / np.minimum(t + 1, w)
        m["invc"] = ic.reshape(128, -1)
        scm = np.zeros((128, 8, 4, 128), f32)
        for j in range(8):
            mk = full if j < c else (caus if j == c else none)
            scm[:, j] = mk[:, None, :]
        m["selcm"] = scm.reshape(128, -1).astype(bf)
        wmm = np.zeros((128, 7, 4, 128), f32)
        for j in range(5):
            t = c - 4 + j
            if t < 0:
                mk = none
            elif j == 0:
                mk = upper
            elif j == 4:
                mk = caus
            else:
                mk = full
            wmm[:, j] = mk[:, None, :]
        wmm[:, 5] = upper[:, None, :]
        wmm[:, 6] = caus[:, None, :]
        m["wm"] = wmm.reshape(128, -1).astype(bf)
        npr = np.arange(512)
        r_ = npr // 64
        k_ = (npr % 64) // 8
        nl_ = npr % 8
        nglob = 8 * (8 * k_ + r_) + nl_
        cend = nglob * 16 + 31
        cm = np.zeros((NS, 512, 128), f32)
        for k, i in enumerate(blocks):
            tpos = 128 * i + np.arange(128)
            valid = (cend[:, None] <= tpos[None, :]) & (nglob[:, None] < 511)
            cm[k] = np.where(valid, 0.0, NEGB)
        cm = cm.reshape(NS, 4, 128, 128)
        cm = np.broadcast_to(cm[:, :, :, None, :], (NS, 4, 128, 4, 128))
        m["cmpm"] = np.ascontiguousarray(cm.transpose(2, 0, 1, 3, 4)).reshape(128, -1).astype(bf)
        am = np.zeros((512, 128), f32)
        for j in range(128):
            for mm_ in range(4):
                for nn in range(2):
                    idx = 4 * j + mm_ - nn
                    if 0 <= idx < 511:
                        am[nglob == idx, j] += 1.0
        m["amat"] = np.ascontiguousarray(am.reshape(4, 128, 128).transpose(1, 0, 2)).reshape(128, -1).astype(bf)
        fm = np.zeros((128, NS, 2, 128), f32)
        jb = np.arange(128)[None, :]
        for k, i in enumerate(blocks):
            tpos = 128 * i + np.arange(128)
            jt = (tpos // 64)[:, None]
            forced = (jb == 0) | (jb == jt) | (jb == jt - 1)
            fut = jb > jt
            M = np.where(forced | fut, 0.0, 1.0)
            B = np.where(fut, -1e30, np.where(forced, 1e6, 0.0))
            fm[:, k, 0] = M
            fm[:, k, 1] = B
        m["fmm"] = fm.reshape(128, -1)
        maps.append(m)
    return maps


_NC_CACHE = {}


def kernel(**inputs):
    maps = _host_inputs(inputs)
    if "nc" not in _NC_CACHE:
        _NC_CACHE["nc"] = build()
    nc = _NC_CACHE["nc"]
    res = run_bass_kernel_spmd(nc, maps, core_ids=list(range(NC)))
    out = np.zeros((S, D), np.float32)
    for c in range(NC):
        o = np.asarray(res.results[c]["out"], np.float32)
        for k in range(NS):
            i = 8 * k + c
            out[128 * i:128 * i + 128] = o[k * 128:(k + 1) * 128]
    return out[None]
```

```python
import numpy as np
import ml_dtypes
from contextlib import ExitStack
import concourse.bass as bass
import concourse.mybir as mybir
from concourse.bass_utils import run_bass_kernel_spmd

F32 = mybir.dt.float32
BF16 = mybir.dt.bfloat16
ALU = mybir.AluOpType
AF = mybir.ActivationFunctionType

S = 8192
D = 2048
NC = 8
NS = 8
TW = 160
NT = NS * TW
DIN = 9264
NEGB = -30000.0
SCALE = 128 ** -0.5
EPS = 1e-6

ENGS = ["tensor", "vector", "scalar", "gpsimd", "sync"]


class Buf:
    __slots__ = ("w", "r", "name")

    def __init__(self, name=""):
        self.w = None
        self.r = []
        self.name = name


class Sched:
    def __init__(self, nc, stack):
        self.nc = nc
        self.stack = stack
        self.ops = {e: [] for e in ENGS}
        self.sem = {e: stack.enter_context(nc.semaphore("sem_" + e)) for e in ENGS}
        self.cnt = {e: 0 for e in ENGS}
        self.opidx = {e: 0 for e in ENGS}
        self.seen = {e: {} for e in ENGS}
        self.dsem = {}
        self.dcnt = {}
        self.widx = {}

    def _wait(self, e, ev):
        key, val = ev
        if self.seen[e].get(key, 0) >= val:
            return
        self.seen[e][key] = val
        self.ops[e].append(("wait", key, val))

    def _deps(self, e, reads, writes):
        for b in reads:
            if b.w is not None:
                if b.w[0] == ("E", e):
                    if e == "tensor":
                        continue
                    if self.opidx[e] - self.widx.get(id(b), -10) >= 3:
                        continue
                self._wait(e, b.w)
        for b in writes:
            if b.w is not None and b.w[0] != ("E", e):
                self._wait(e, b.w)
            for r in b.r:
                if r[0] != ("E", e):
                    self._wait(e, r)

    def op(self, e, fn, reads=(), writes=(), inc=True):
        if e != "tensor":
            pr = [b for b in reads if b.name.startswith("ps")]
            if pr:
                reads = [b for b in reads if not b.name.startswith("ps")]
                writes = list(writes) + pr
        self._deps(e, reads, writes)
        ev = (("E", e), self.cnt[e] + 1)
        if inc:
            self.cnt[e] += 1
        self.ops[e].append(("op", fn, inc, ev[1]))
        for b in reads:
            b.r.append(ev)
        for b in writes:
            b.w = ev
            b.r = []
            self.widx[id(b)] = self.opidx[e]
        self.opidx[e] += 1

    def dma(self, q, out, in_, reads=(), writes=(), slot=None, in_fn=None, **kw):
        if slot not in self.dsem:
            self.dsem[slot] = self.stack.enter_context(self.nc.semaphore("d_" + slot))
            self.dcnt[slot] = 0
        self._deps(q, reads, writes)
        self.dcnt[slot] += 16
        ev = (("D", slot), self.dcnt[slot])
        sem = self.dsem[slot]
        if in_fn is not None:
            self.ops[q].append(("raw", lambda eng, out=out, in_fn=in_fn, sem=sem, kw=kw: eng.dma_start(
                out=out, in_=in_fn(eng), **kw).then_inc(sem, 16)))
        else:
            self.ops[q].append(("raw", lambda eng, out=out, in_=in_, sem=sem, kw=kw: eng.dma_start(
                out=out, in_=in_, **kw).then_inc(sem, 16)))
        for b in reads:
            b.r.append(ev)
        for b in writes:
            b.w = ev
            b.r = []

    def raw(self, e, fn, reads=(), writes=(), ev=None):
        self._deps(e, reads, writes)
        self.ops[e].append(("raw", fn))
        for b in reads:
            b.r.append(ev)
        for b in writes:
            b.w = ev
            b.r = []

    def barrier(self, bufs=()):
        for e in ENGS:
            for o in ENGS:
                if o != e and o != "sync" and self.cnt[o] > 0:
                    self._wait(e, (("E", o), self.cnt[o]))
            for s, v in self.dcnt.items():
                if v > 0:
                    self._wait(e, (("D", s), v))

    def emit(self, block):
        ops = self.ops
        needed = {e: set() for e in ENGS}
        for e in ENGS:
            for rec in ops[e]:
                if rec[0] == "wait" and rec[1][0] == "E":
                    needed[rec[1][1]].add(rec[2])
        remap = {e: {v: i + 1 for i, v in enumerate(sorted(needed[e]))} for e in ENGS}
        sems, dsems = self.sem, self.dsem

        def run(e, eng):
            for rec in ops[e]:
                if rec[0] == "wait":
                    key, val = rec[1], rec[2]
                    if key[0] == "E":
                        eng.wait_ge(sems[key[1]], remap[key[1]][val])
                    else:
                        eng.wait_ge(dsems[key[1]], val)
                elif rec[0] == "op":
                    _, fn, inc, val = rec
                    if inc and val in needed[e]:
                        fn(eng).then_inc(sems[e], 1)
                    else:
                        fn(eng)
                else:
                    rec[1](eng)

        @block.tensor
        def _(eng):
            run("tensor", eng)

        @block.vector
        def _(eng):
            run("vector", eng)

        @block.scalar
        def _(eng):
            run("scalar", eng)

        @block.gpsimd
        def _(eng):
            run("gpsimd", eng)

        @block.sync
        def _(eng):
            run("sync", eng)


class Arena:
    def __init__(self, tensor, base, size):
        self.t = tensor
        self.base = base
        self.size = size
        self.off = 0

    def sub(self, size):
        a = Arena(self.t, self.base + self.off, size)
        self.off += size
        assert self.off <= self.size, (self.off, self.size)
        return a

    def reset(self):
        self.off = 0

    def alloc(self, shape, dtype):
        es = 2 if dtype == BF16 else 4
        n = int(np.prod(shape))
        nbytes = (n * es + 63) // 64 * 64
        o = self.base + self.off
        self.off += nbytes
        assert self.off <= self.size, ("arena overflow", self.off, self.size)
        ap = self.t[:, o // 2: o // 2 + n * es // 2]
        if dtype != BF16:
            ap = ap.bitcast(dtype)
        if len(shape) == 2:
            ap = ap.rearrange("p (a b) -> p a b", a=shape[0])
        elif len(shape) == 3:
            ap = ap.rearrange("p (a b c) -> p a b c", a=shape[0], b=shape[1])
        elif len(shape) == 4:
            ap = ap.rearrange("p (a b c d) -> p a b c d", a=shape[0], b=shape[1], c=shape[2])
        return ap


def build(stage=99, nh=4):
    nc = bass.Bass("TRN2", target_bir_lowering=False)
    dt = nc.dram_tensor

    def ext_in(name, shape, dtype=F32):
        return dt(name, list(shape), dtype, kind="ExternalInput").ap()

    def ext_out(name, shape, dtype=F32):
        return dt(name, list(shape), dtype, kind="ExternalOutput").ap()

    xT_d = ext_in("xT", [D, NT])
    xtok_d = ext_in("xtok", [NS * 128, D])
    normw_d = ext_in("normw", [128, 16])
    w_in_d = ext_in("w_in", [D, DIN])
    mix_d = ext_in("pool_mix", [4, 256, 256])
    pscale_d = ext_in("pscale", [128, 8])
    pek_d = ext_in("pe_kT", [128, 32])
    pev_d = ext_in("pe_vT", [128, 32])
    w1k_d = ext_in("w1k", [32, 128, 256])
    w1v_d = ext_in("w1v", [32, 128, 256])
    w2k_d = ext_in("w2k", [256, 128])
    w2v_d = ext_in("w2v", [256, 128])
    wpo_d = ext_in("w_pool_out", [1024, D])
    wno_d = ext_in("w_nsa_out", [D, D])
    wmg_d = ext_in("w_merge", [D, 2 * D])
    bmg_d = ext_in("bmerge", [128, 32])
    wout_d = ext_in("w_out", [D, D])
    fnw_d = ext_in("fnw", [128, D])
    ropecs_d = ext_in("ropecs", [128, NS * 2 * 64])
    ropecc_d = ext_in("ropecc", [128, 2 * 64])
    invc_d = ext_in("invc", [128, 4 * 16])
    ident_d = ext_in("ident", [128, 128], BF16)
    emat_d = ext_in("emat", [128, S], BF16)
    selcm_d = ext_in("selcm", [128, 8 * 512], BF16)
    wm_d = ext_in("wm", [128, 7 * 512], BF16)
    cmpm_d = ext_in("cmpm", [128, NS * 4 * 512], BF16)
    amat_d = ext_in("amat", [128, 4 * 128], BF16)
    fmm_d = ext_in("fmm", [128, NS * 2 * 128])
    out_d = ext_out("out", [NS * 128, D])

    kselT_loc = dt("kselT_loc", [128, 4096], BF16)
    kwinT_loc = dt("kwinT_loc", [NS * 128, 512], BF16)
    vsel_loc = dt("vsel_loc", [128, 4160], BF16)
    vwin_loc = dt("vwin_loc", [NS * 128, 512], BF16)
    kcT_loc = dt("kcT_loc", [128, 256], BF16)
    vc_loc = dt("vc_loc", [256, 128], BF16)
    kselT_all = dt("kselT_all", [NC * 128, 4096], BF16)
    kwinT_all = dt("kwinT_all", [(32 + NC * NS) * 128, 512], BF16)
    vsel_all = dt("vsel_all", [NC * 128, 4160], BF16)
    vwin_all = dt("vwin_all", [(32 + NC * NS) * 128, 512], BF16)
    kcT_all = dt("kcT_all", [NC * 128, 256], BF16)
    vc_all = dt("vc_all", [NC * 256, 128], BF16)
    hT_sp = dt("hT_sp", [128, 16 * NS * 128], BF16)
    yp_sp = dt("yp_sp", [128, 8 * 1024], BF16)

    dbg = {}
    if stage < 99:
        dbg["QT"] = ext_out("dbg_QT", [128, 16 * 1024], BF16)
        dbg["sgn"] = ext_out("dbg_sgn", [128, NS * 2048], BF16)
        dbg["gates"] = ext_out("dbg_gates", [128, NS * 48])
        dbg["ypool"] = ext_out("dbg_ypool", [128, 8 * 1024], BF16)
        dbg["kselT"] = ext_out("dbg_kselT", [128, 4096], BF16)
        dbg["kwinT"] = ext_out("dbg_kwinT", [NS * 128, 512], BF16)
        dbg["vsel"] = ext_out("dbg_vsel", [128, 4160], BF16)
        dbg["vwin"] = ext_out("dbg_vwin", [NS * 128, 512], BF16)
        dbg["kcT"] = ext_out("dbg_kcT", [128, 256], BF16)
        dbg["vc"] = ext_out("dbg_vc", [256, 128], BF16)
        dbg["og"] = ext_out("dbg_og", [128, NS * 2048], BF16)

    stack = ExitStack()
    with stack:
        ARENA_BYTES = 204 * 1024
        arena_t = stack.enter_context(nc.sbuf_tensor("arena", [128, ARENA_BYTES // 2], BF16))
        ps = [stack.enter_context(nc.psum_tensor("ps%d" % i, [128, 512], F32))[:] for i in range(8)]
        psb = [Buf("ps%d" % i) for i in range(8)]
        sc = Sched(nc, stack)
        block = stack.enter_context(nc.Block())

        top = Arena(arena_t, 0, ARENA_BYTES)
        ident = top.alloc([128], BF16)
        ones = top.alloc([128], BF16)
        normw = top.alloc([16], F32)
        pscale = top.alloc([8], F32)
        bmerge = top.alloc([32], F32)
        gates = top.alloc([NS, 48], F32)
        b_const = Buf("const")
        b_gates = Buf("gates")
        b_pay = Buf("pay")
        qs = top.sub(64 * 1024)
        ph = top.sub(ARENA_BYTES - top.off)

        QT = qs.alloc([16, 1024], BF16)
        sgn = qs.alloc([NS, 2048], BF16)
        b_QT = Buf("QT")
        b_sgn = [Buf("sgn%d" % k) for k in range(NS)]

        V = "vector"
        A = "scalar"
        G = "gpsimd"
        T = "tensor"
        SY = "sync"

        def mm(out, lhsT, rhs, start, stop, reads, writes, inc=None, sgc=False):
            if inc is None:
                inc = stop
            if sgc:
                sc.op(T, lambda e: e.matmul(out, lhsT, rhs, start=start, stop=stop, skip_group_check=True),
                      reads, writes, inc=inc)
            else:
                sc.op(T, lambda e: e.matmul(out, lhsT, rhs, start=start, stop=stop), reads, writes, inc=inc)

        def transp(out, in_, reads, writes):
            sc.op(T, lambda e: e.transpose(out, in_, ident), reads + [b_const], writes)

        sc.dma(SY, ident, ident_d, writes=[b_const], slot="const")
        sc.dma(SY, normw, normw_d, writes=[b_const], slot="const")
        sc.dma(SY, pscale, pscale_d, writes=[b_const], slot="const")
        sc.dma(SY, bmerge, bmg_d, writes=[b_const], slot="const")
        sc.op(V, lambda e: e.memset(ones, 1.0), writes=[b_const])

        hT = ph.alloc([16, NT], BF16)
        b_hT = [Buf("hT%d" % c_) for c_ in range(16)]
        ypool = ph.alloc([8, 1024], BF16)
        b_ypool = Buf("ypool")
        NWB = 2
        wbuf = [ph.alloc([16, 512], BF16) for _ in range(NWB)]
        b_wbuf = [Buf("wbuf%d" % i) for i in range(NWB)]
        ropecs = ph.alloc([NS, 2, 64], F32)
        ropecc = ph.alloc([2, 64], F32)
        invc = ph.alloc([4, 16], F32)
        sc.dma(SY, ropecs, ropecs_d.rearrange("p (k a f) -> p k a f", k=NS, a=2), writes=[b_const], slot="const")
        sc.dma(SY, ropecc, ropecc_d.rearrange("p (a f) -> p a f", a=2), writes=[b_const], slot="const")
        sc.dma(SY, invc, invc_d.rearrange("p (g t) -> p g t", g=4), writes=[b_const], slot="const")
        mixw = ph.alloc([4, 2, 256], BF16)
        w2 = ph.alloc([2, 2, 128], BF16)
        p1mark = ph.off

        xbuf = [ph.alloc([NT], F32) for _ in range(2)]
        b_xbuf = [Buf("xbuf0"), Buf("xbuf1")]
        sq = [ph.alloc([NT], BF16) for _ in range(2)]
        b_sq = [Buf("sq0"), Buf("sq1")]
        rstd = ph.alloc([NT], F32)
        b_rstd = Buf("rstd")
        xT_v = xT_d.rearrange("(c p) n -> c p n", p=128)
        segs = [(0, 512), (512, 1024), (1024, 1280)]
        for c in range(16):
            j = c % 2
            sc.dma(SY, xbuf[j], xT_v[c], writes=[b_xbuf[j]], slot="xbuf%d" % j)
            sc.op(A, lambda e, j=j: e.activation(sq[j], xbuf[j], AF.Square), [b_xbuf[j]], [b_sq[j]])
            for si, (a, b) in enumerate(segs):
                mm(ps[si][:, 0:b - a], ones, sq[j][:, a:b], c == 0, c == 15, [b_sq[j], b_const], [psb[si]],
                   inc=(si == 2))
        for si, (a, b) in enumerate(segs):
            sc.op(A, lambda e, si=si, a=a, b=b: e.activation(rstd[:, a:b], ps[si][:, 0:b - a], AF.Sqrt,
                                                               bias=EPS, scale=1.0 / D), [psb[si]], [b_rstd])
        sc.op(V, lambda e: e.reciprocal(rstd, rstd), [b_rstd], [b_rstd])
        for c in range(16):
            j = c % 2
            sc.dma(SY, xbuf[j], xT_v[c], writes=[b_xbuf[j]], slot="xbuf%d" % j)
            sc.op(V, lambda e, c=c, j=j: e.scalar_tensor_tensor(hT[:, c, :], xbuf[j], normw[:, c:c + 1], rstd,
                                                                 ALU.mult, ALU.mult),
                  [b_xbuf[j], b_rstd, b_const], [b_hT[c]])
        ph.off = p1mark

        wcount = [0]

        def load_w(src_ap_cols, kchunks=16, ncols=512):
            i = wcount[0] % NWB
            wcount[0] += 1
            dst = wbuf[i][:, 0:kchunks, 0:ncols]
            sc.dma(G, dst, src_ap_cols.rearrange("(c p) n -> p c n", p=128), writes=[b_wbuf[i]], slot="wbuf%d" % i)
            return wbuf[i], b_wbuf[i]

        qs.reset()
        sgpool = qs.alloc([8, 1024], BF16)
        b_sgpool = Buf("sgpool")
        kcraw = qs.alloc([4, NT], BF16)
        vcraw = qs.alloc([4, NT], BF16)
        b_raw = {"k": Buf("kcraw"), "v": Buf("vcraw")}
        uT = qs.alloc([NT], F32)
        sA = qs.alloc([NT], F32)
        sB = qs.alloc([NT], F32)
        b_uT, b_sA, b_sB = Buf("uT"), Buf("sA"), Buf("sB")
        pooled = qs.alloc([2, 1024], BF16)
        b_pooled = [Buf("pooled0"), Buf("pooled1")]
        pfix = qs.alloc([16], F32)
        b_pfix = Buf("pfix")
        b_mixw = Buf("mixw")
        hdn = qs.alloc([2, 256], BF16)
        b_hdn = Buf("hdn")
        cbias = qs.alloc([2], F32)
        b_cbias = Buf("cbias")
        peT = qs.alloc([2, 32], BF16)
        b_peT = Buf("peT")
        b_w2 = Buf("w2")
        kcr = qs.alloc([2, 128], BF16)
        b_kcr = Buf("kcr")
        ctmp = qs.alloc([4, 128], F32)
        b_ctmp = Buf("ctmp")
        kcTs = qs.alloc([256], BF16)
        b_kcTs = Buf("kcTs")
        vcs = qs.alloc([2, 128], BF16)
        b_vcs = Buf("vcs")

        sc.dma(G, mixw, mix_d.rearrange("g (cc p) d -> p g cc d", p=128), writes=[b_mixw], slot="mixw")
        sc.dma(G, peT[:, 0, :], pek_d, writes=[b_peT], slot="misc_pe")
        sc.dma(G, peT[:, 1, :], pev_d, writes=[b_peT], slot="misc_pe")
        sc.dma(G, w2[:, 0], w2k_d.rearrange("(fb p) e -> p fb e", p=128), writes=[b_w2], slot="misc_w2")
        sc.dma(G, w2[:, 1], w2v_d.rearrange("(fb p) e -> p fb e", p=128), writes=[b_w2], slot="misc_w2")

        def fm_proj(col0, evac):
            wb, bwb = load_w(w_in_d[:, col0:col0 + 512])
            for j in range(4):
                base = 3 * (fm_proj.n % 2)
                fm_proj.n += 1
                for c in range(16):
                    for si, (a, b) in enumerate(segs):
                        mm(ps[base + si][:, 0:b - a], wb[:, c, j * 128:(j + 1) * 128], hT[:, c, a:b],
                           c == 0, c == 15, [bwb, b_hT[c]], [psb[base + si]], inc=(c == 15 and si == 2))
                evac(j, base)
        fm_proj.n = 0

        xs = ph.alloc([512], F32)
        b_xs = Buf("xs")
        rt = ph.alloc([4, 256], F32)
        b_rt = Buf("rt")
        qr = [ph.alloc([4, 128], BF16) for _ in range(2)]
        b_qr = [Buf("qr0"), Buf("qr1")]
        kst = ph.alloc([NS, 512], BF16)
        b_kst = Buf("kst")
        kstK = ph.alloc([4, NS, 128], BF16)
        b_kstK = Buf("kstK")
        kstV = ph.alloc([4, NS, 130], BF16)
        b_kstV = Buf("kstV")
        sc.op(V, lambda e: e.memset(kstV[:, :, :, 128:130], 1.0), writes=[b_kstV])
        tmcount = [0]

        def tm_proj(col0, ncols, evac):
            wb, bwb = load_w(w_in_d[:, col0:col0 + ncols], ncols=ncols)
            pending = None
            for k in range(NS):
                pb = tmcount[0] % 4
                tmcount[0] += 1
                t0 = k * TW + 16
                for c in range(16):
                    mm(ps[pb][:, 0:ncols], hT[:, c, t0:t0 + 128], wb[:, c, 0:ncols], c == 0, c == 15,
                       [bwb, b_hT[c]], [psb[pb]])
                if pending is not None:
                    pending()
                pending = evac(k, pb)
            if pending is not None:
                pending()

        def rope_tm(pb, k, dst_fn, bdst):
            sc.op(A, lambda e: e.activation(xs, ps[pb], AF.Copy), [psb[pb]], [b_xs])
            x4 = xs.rearrange("p (h d) -> p h d", h=4)
            x1, x2 = x4[:, :, 0:64], x4[:, :, 64:128]
            cs = ropecs[:, k, 0, :].unsqueeze(1).broadcast_to([128, 4, 64])
            sn = ropecs[:, k, 1, :].unsqueeze(1).broadcast_to([128, 4, 64])
            r4 = rt.rearrange("p a (h d) -> p a h d", h=4)
            sc.op(V, lambda e: e.tensor_tensor(r4[:, 0], x1, cs, ALU.mult), [b_xs, b_const], [b_rt])
            sc.op(V, lambda e: e.tensor_tensor(r4[:, 1], x2, sn, ALU.mult), [b_xs, b_const], [b_rt])
            sc.op(V, lambda e: e.tensor_tensor(r4[:, 2], x2, cs, ALU.mult), [b_xs, b_const], [b_rt])
            sc.op(V, lambda e: e.tensor_tensor(r4[:, 3], x1, sn, ALU.mult), [b_xs, b_const], [b_rt])
            qi = rope_tm.n % 2
            qrb, bq = qr[qi], b_qr[qi]
            sc.op(V, lambda e: e.tensor_tensor(qrb[:, :, 0:64], r4[:, 0], r4[:, 1], ALU.subtract), [b_rt], [bq])
            sc.op(V, lambda e: e.tensor_tensor(qrb[:, :, 64:128], r4[:, 2], r4[:, 3], ALU.add), [b_rt], [bq])
            pt = 4 + (rope_tm.n % 2)
            rope_tm.n += 1
            ptv = ps[pt].bitcast(BF16)

            def part2():
                for hh in range(4):
                    transp(ptv[:, hh * 128:(hh + 1) * 128], qrb[:, hh, :], [bq], [psb[pt]])
                sc.op(A, lambda e, ptv=ptv: e.activation(dst_fn(), ptv[:, 0:512].rearrange("p (h t) -> p h t", h=4),
                                                         AF.Copy), [psb[pt]], [bdst])
            return part2
        rope_tm.n = 0

        def evac_raw(which):
            dst = kcraw if which == "k" else vcraw

            def f(j, base):
                for si, (a, b) in enumerate(segs):
                    sc.op(A if si != 1 else V, (lambda e, si=si, a=a, b=b: e.activation(
                        dst[:, j, a:b], ps[base + si][:, 0:b - a], AF.Copy)) if si != 1 else
                        (lambda e, si=si, a=a, b=b: e.tensor_copy(dst[:, j, a:b], ps[base + si][:, 0:b - a])),
                        [psb[base + si]], [b_raw[which]])
            return f

        fm_proj(4096, evac_raw("k"))
        fm_proj(4608, evac_raw("v"))

        for wi, which in enumerate(["k", "v"]):
            raw = (kcraw if which == "k" else vcraw).rearrange("p h (k t) -> p h k t", k=NS)
            w1d = w1k_d if which == "k" else w1v_d
            i = wcount[0] % NWB
            wcount[0] += 1
            w1 = wbuf[i].rearrange("p a b -> p (a b)").rearrange("p (l f) -> p l f", l=32)
            bw1 = b_wbuf[i]
            sc.dma(G, w1, w1d.rearrange("l d f -> d l f"), writes=[bw1], slot="wbuf%d" % i)
            for fb in range(2):
                pb = fb
                for l in range(32):
                    mm(ps[pb][:, 0:256].rearrange("p (h k n) -> p h k n", h=4, k=NS),
                       w1[:, l, fb * 128:(fb + 1) * 128], raw[:, :, :, l + 16:l + 16 + 113:16],
                       l == 0, l == 31, [bw1, b_raw[which]], [psb[pb]])
                for l in range(32):
                    mm(ps[pb][:, 256:257], w1[:, l, fb * 128:(fb + 1) * 128], peT[:, wi, l:l + 1],
                       False, l == 31, [bw1, b_peT], [psb[pb]], sgc=True)
                sc.op(V, lambda e, pb=pb, fb=fb: e.tensor_copy(cbias[:, fb:fb + 1], ps[pb][:, 256:257]),
                      [psb[pb]], [b_cbias])
                sc.op(A, lambda e, pb=pb, fb=fb: e.activation(hdn[:, fb, :], ps[pb][:, 0:256], AF.Silu,
                                                                bias=cbias[:, fb:fb + 1]),
                      [psb[pb], b_cbias], [b_hdn])
            for mt in range(2):
                pb = 2 + mt
                for fb in range(2):
                    mm(ps[pb][:, 0:128], hdn[:, fb, mt * 128:(mt + 1) * 128], w2[:, wi, fb, :], fb == 0, fb == 1,
                       [b_hdn, b_w2], [psb[pb]])
                if which == "k":
                    x1 = ps[pb][:, 0:64]
                    x2 = ps[pb][:, 64:128]
                    cs, sn = ropecc[:, 0, :], ropecc[:, 1, :]
                    sc.op(V, lambda e, x1=x1, cs=cs: e.tensor_tensor(ctmp[:, 0, 0:64], x1, cs, ALU.mult), [psb[pb], b_const], [b_ctmp])
                    sc.op(V, lambda e, x2=x2, sn=sn: e.tensor_tensor(ctmp[:, 1, 0:64], x2, sn, ALU.mult), [psb[pb], b_const], [b_ctmp])
                    sc.op(V, lambda e, x2=x2, cs=cs: e.tensor_tensor(ctmp[:, 2, 0:64], x2, cs, ALU.mult), [psb[pb], b_const], [b_ctmp])
                    sc.op(V, lambda e, x1=x1, sn=sn: e.tensor_tensor(ctmp[:, 3, 0:64], x1, sn, ALU.mult), [psb[pb], b_const], [b_ctmp])
                    sc.op(V, lambda e, mt=mt: e.tensor_tensor(kcr[:, mt, 0:64], ctmp[:, 0, 0:64], ctmp[:, 1, 0:64], ALU.subtract), [b_ctmp], [b_kcr])
                    sc.op(V, lambda e, mt=mt: e.tensor_tensor(kcr[:, mt, 64:128], ctmp[:, 2, 0:64], ctmp[:, 3, 0:64], ALU.add), [b_ctmp], [b_kcr])
                    pt = 4 + mt
                    ptv = ps[pt].bitcast(BF16)
                    transp(ptv[:, 0:128], kcr[:, mt, :], [b_kcr], [psb[pt]])
                    sc.op(V, lambda e, ptv=ptv, mt=mt: e.tensor_copy(kcTs[:, mt * 128:(mt + 1) * 128], ptv[:, 0:128]),
                          [psb[pt]], [b_kcTs])
                else:
                    sc.op(V, lambda e, pb=pb, mt=mt: e.tensor_copy(vcs[:, mt, :], ps[pb][:, 0:128]), [psb[pb]], [b_vcs])
            if which == "k":
                sc.dma(SY, kcT_loc.ap(), kcTs, reads=[b_kcTs], writes=[b_pay], slot="pay")
            else:
                sc.dma(SY, vc_loc.ap().rearrange("(mt p) e -> p mt e", p=128), vcs, reads=[b_vcs], writes=[b_pay], slot="pay")

        def kv_block(col0, dst_loc, is_k, sel):
            if is_k and sel:
                tm_proj(col0, 512, lambda k, pb: rope_tm(pb, k, lambda: kstK[:, :, k, :], b_kstK))
                sc.dma(SY, dst_loc.ap(), kstK.rearrange("p h k t -> p (h k t)"), reads=[b_kstK], writes=[b_pay],
                       slot="pay")
            elif sel:
                tm_proj(col0, 512, lambda k, pb: sc.op(
                    V, lambda e: e.tensor_copy(kstV[:, :, k, 0:128], ps[pb].rearrange("p (h e) -> p h e", h=4)),
                    [psb[pb]], [b_kstV]))
                sc.dma(SY, dst_loc.ap(), kstV.rearrange("p h k e -> p (h k e)"), reads=[b_kstV], writes=[b_pay],
                       slot="pay")
            elif is_k:
                tm_proj(col0, 512, lambda k, pb: rope_tm(
                    pb, k, lambda: kst[:, k, :].rearrange("p (h t) -> p h t", h=4), b_kst))
                sc.dma(SY, dst_loc.ap().rearrange("(k p) n -> p k n", p=128), kst, reads=[b_kst], writes=[b_pay], slot="pay")
            else:
                tm_proj(col0, 512, lambda k, pb: sc.op(
                    V, lambda e: e.tensor_copy(kst[:, k, :], ps[pb]), [psb[pb]], [b_kst]))
                sc.dma(SY, dst_loc.ap().rearrange("(k p) n -> p k n", p=128), kst, reads=[b_kst], writes=[b_pay], slot="pay")

        kv_block(5120, kselT_loc, True, True)
        kv_block(5632, vsel_loc, False, True)
        kv_block(6144, kwinT_loc, True, False)
        kv_block(6656, vwin_loc, False, False)

        cc_sem = stack.enter_context(nc.semaphore("cc"))
        sc.dsem["cc"] = cc_sem
        sc.dcnt["cc"] = 0
        b_all = Buf("all")
        pairs = [(kcT_loc, kcT_all), (vc_loc, vc_all), (kwinT_loc, kwinT_all), (vwin_loc, vwin_all),
                 (kselT_loc, kselT_all), (vsel_loc, vsel_all)]
        for src, dst in pairs:
            sc.dcnt["cc"] += 1
            dap = dst.ap()[4096:, :] if dst in (kwinT_all, vwin_all) else dst.ap()
            sc.raw(G, lambda e, src=src, dap=dap: e.collective_compute(
                "AllGather", ALU.bypass, replica_groups=[list(range(NC))],
                ins=[src.ap().opt()], outs=[dap.opt()]).then_inc(cc_sem, 1),
                reads=[b_pay], writes=[b_all], ev=(("D", "cc"), sc.dcnt["cc"]))

        wscrK = dt("wscrK", [5 * NS * 128, 512], BF16)
        wscrV = dt("wscrV", [5 * NS * 128, 512], BF16)
        b_wscr = Buf("wscr")
        wregs = {}

        for t_all in (kwinT_all, vwin_all):
            sc.dma(SY, t_all.ap()[0:4096, :], t_all.ap()[4096 + 31 * 128:4096 + 63 * 128, :], reads=[b_all],
                   writes=[b_all], slot="xcopy")

        def wsrc(eng, t_all):
            pid = eng.partition_id()
            return t_all.ap()[bass.ds(pid * 1024, 5 * 1024), :]
        sc.dma(SY, wscrK.ap(), None, reads=[b_all], writes=[b_wscr], slot="wscr",
               in_fn=lambda eng: wsrc(eng, kwinT_all))
        sc.dma(SY, wscrV.ap(), None, reads=[b_all], writes=[b_wscr], slot="wscr",
               in_fn=lambda eng: wsrc(eng, vwin_all))

        def own(ap_seg, a, b):
            return ap_seg

        def evac_gpool(blk0):
            def f(j, base):
                blk = blk0 + j
                for k in range(NS):
                    t0 = k * TW + 16
                    si = 0 if t0 + 128 <= 512 else (1 if t0 >= 512 and t0 + 128 <= 1024 else (2 if t0 >= 1024 else -1))
                    if si >= 0:
                        a = segs[si][0]
                        sc.op(A, lambda e, blk=blk, k=k, si=si, a=a, t0=t0: e.activation(
                            sgpool[:, blk, k * 128:(k + 1) * 128], ps[base + si][:, t0 - a:t0 - a + 128], AF.Silu),
                            [psb[base + si]], [b_sgpool])
                    else:
                        for si2, (a, b) in enumerate(segs):
                            lo, hi = max(t0, a), min(t0 + 128, b)
                            if lo < hi:
                                sc.op(A, lambda e, blk=blk, k=k, si2=si2, a=a, lo=lo, hi=hi, t0=t0: e.activation(
                                    sgpool[:, blk, k * 128 + lo - t0:k * 128 + hi - t0],
                                    ps[base + si2][:, lo - a:hi - a], AF.Silu), [psb[base + si2]], [b_sgpool])
            return f

        fm_proj(1024, evac_gpool(0))
        fm_proj(1536, evac_gpool(4))

        def evac_upool(blk0):
            def f(j, base):
                blk = blk0 + j
                g = blk // 2
                cc = blk % 2
                for si, (a, b) in enumerate(segs):
                    sc.op(A, lambda e, si=si, a=a, b=b: e.activation(uT[:, a:b], ps[base + si][:, 0:b - a], AF.Copy),
                          [psb[base + si]], [b_uT])
                u3 = uT.rearrange("p (k t) -> p k t", k=NS)
                a3 = sA.rearrange("p (k t) -> p k t", k=NS)
                b3 = sB.rearrange("p (k t) -> p k t", k=NS)
                cur, bcur = u3, b_uT
                tmp = [(a3, b_sA), (b3, b_sB)]
                sh = 1
                lvl = 0
                w = 2 << g
                while sh < w:
                    dst, bdst = tmp[lvl % 2]
                    sc.op(V, lambda e, dst=dst, cur=cur, sh=sh: e.tensor_tensor(
                        dst[:, :, 2 * sh - 1:144], cur[:, :, 2 * sh - 1:144], cur[:, :, sh - 1:144 - sh], ALU.add),
                        [bcur], [bdst])
                    cur, bcur = dst, bdst
                    sh *= 2
                    lvl += 1
                p3 = pooled[:, cc, :].rearrange("p (k t) -> p k t", k=NS)
                sc.op(V, lambda e, cur=cur, p3=p3, w=w: e.scalar_tensor_tensor(
                    p3, cur[:, :, 16:144], 1.0 / w, u3[:, :, 16:144], ALU.mult, ALU.subtract),
                    [bcur, b_uT], [b_pooled[cc]])
                sc.op(V, lambda e, cur=cur, g=g: e.tensor_tensor(pfix, cur[:, 0, 16:32], invc[:, g, :], ALU.mult),
                      [bcur, b_const], [b_pfix])
                sc.op(V, lambda e, cc=cc: e.tensor_tensor(pooled[:, cc, 0:16], pfix, uT[:, 16:32], ALU.subtract),
                      [b_pfix, b_uT], [b_pooled[cc]])
                if cc == 1:
                    for db in range(2):
                        oblk = g * 2 + db
                        for half in range(2):
                            pb = 6 + half
                            for c2 in range(2):
                                mm(ps[pb], mixw[:, g, c2, db * 128:(db + 1) * 128],
                                   pooled[:, c2, half * 512:(half + 1) * 512], c2 == 0, c2 == 1,
                                   [b_mixw, b_pooled[0], b_pooled[1]], [psb[pb]])
                            sc.op(V, lambda e, pb=pb, oblk=oblk, half=half: e.scalar_tensor_tensor(
                                ypool[:, oblk, half * 512:(half + 1) * 512], ps[pb], pscale[:, oblk:oblk + 1],
                                sgpool[:, oblk, half * 512:(half + 1) * 512], ALU.mult, ALU.mult),
                                [psb[pb], b_sgpool, b_const], [b_ypool])
            return f

        fm_proj(0, evac_upool(0))
        fm_proj(512, evac_upool(4))

        sc.barrier()
        qs.reset()
        QT = qs.alloc([16, 1024], BF16)
        sgn = qs.alloc([NS, 2048], BF16)

        for qb in range(4):
            tm_proj(2048 + qb * 512, 512, lambda k, pb, qb=qb: rope_tm(
                pb, k, lambda: QT[:, qb * 4:qb * 4 + 4, k * 128:(k + 1) * 128], b_QT))

        for gb in range(4):
            tm_proj(7168 + gb * 512, 512, lambda k, pb, gb=gb: sc.op(
                A, lambda e: e.activation(sgn[:, k, gb * 512:(gb + 1) * 512], ps[pb], AF.Silu), [psb[pb]], [b_sgn[k]]))
        tm_proj(9216, 48, lambda k, pb: sc.op(
            A, lambda e: e.activation(gates[:, k, :], ps[pb][:, 0:48], AF.Sigmoid), [psb[pb]], [b_gates]))

        hT4 = hT.rearrange("p c (k t) -> p c k t", k=NS)
        for c0 in range(0, 16, 4):
            sc.dma(SY, hT_sp.ap().rearrange("p (c k t) -> p c k t", c=16, k=NS)[:, c0:c0 + 4],
                   hT4[:, c0:c0 + 4, :, 16:144], reads=b_hT[c0:c0 + 4], slot="spill")
        sc.dma(SY, yp_sp.ap().rearrange("p (b t) -> p b t", b=8), ypool, reads=[b_ypool], slot="spill")

        if stage == 1:
            sc.barrier()
            sc.dma(SY, dbg["QT"].rearrange("p (a b) -> p a b", a=16), QT, reads=[b_QT], slot="dbg")
            sc.dma(SY, dbg["sgn"].rearrange("p (a b) -> p a b", a=NS), sgn, reads=b_sgn, slot="dbg")
            sc.dma(SY, dbg["gates"].rearrange("p (a b) -> p a b", a=NS), gates, reads=[b_gates], slot="dbg")
            sc.dma(SY, dbg["ypool"].rearrange("p (a b) -> p a b", a=8), ypool, reads=[b_ypool], slot="dbg")
            for nm, loc in [("kselT", kselT_loc), ("kwinT", kwinT_loc), ("vsel", vsel_loc), ("vwin", vwin_loc),
                            ("kcT", kcT_loc), ("vc", vc_loc)]:
                sc.dma(SY, dbg[nm], loc.ap(), slot="dbg")
            sc.barrier()
            sc.emit(block)
            return nc

        sc.barrier()

        ph.reset()
        KselT2 = [ph.alloc([S], BF16) for _ in range(2)]
        Vsel2 = [ph.alloc([64, 130], BF16) for _ in range(2)]
        emat = ph.alloc([S], BF16)
        selcm = ph.alloc([8, 512], BF16)
        wmk = ph.alloc([7, 512], BF16)
        cmpm2 = [ph.alloc([4, 512], BF16) for _ in range(2)]
        b_cmpm = [Buf('cmpm0'), Buf('cmpm1')]
        amat = ph.alloc([4, 128], BF16)
        fmm = ph.alloc([NS, 2, 128], F32)
        kcT = ph.alloc([512], BF16)
        vcx = ph.alloc([4, 130], BF16)
        kwT = [ph.alloc([5, 128], BF16) for _ in range(2)]
        vwx = [ph.alloc([5, 130], BF16) for _ in range(2)]
        NPT = 4
        PT = [ph.alloc([512], BF16) for _ in range(NPT)]
        PTc = ph.alloc([4, 512], BF16)
        imp = ph.alloc([128], F32)
        imp2 = ph.alloc([128], F32)
        m8a = ph.alloc([8], F32)
        m8b = ph.alloc([8], F32)
        mb = ph.alloc([128], BF16)
        mbT = ph.alloc([4, 128], BF16)
        den = ph.alloc([3, 4], F32)
        wgt = ph.alloc([3, 4], F32)
        ocomb = ph.alloc([4, 128], F32)
        b_ksel2, b_vsel2 = [Buf("ksel0"), Buf("ksel1")], [Buf("vsel0"), Buf("vsel1")]
        b_p2c, b_kc, b_vc = Buf("p2c"), Buf("kc"), Buf("vc")
        b_kw = [Buf("kw0"), Buf("kw1")]
        b_vw = [Buf("vw0"), Buf("vw1")]
        b_PT = [Buf("PT%d" % i) for i in range(NPT)]
        b_PTc = [Buf("PTc%d" % i) for i in range(4)]
        b_imp, b_imp2, b_m8a, b_m8b, b_mb, b_mbT = (Buf("imp"), Buf("imp2"), Buf("m8a"), Buf("m8b"), Buf("mb"),
                                                     Buf("mbT"))
        b_den, b_wgt, b_ocomb = Buf("den"), Buf("wgt"), Buf("ocomb")

        b_csm, b_cbig = Buf("csm"), Buf("cbig")
        sc.dma(SY, amat, amat_d.rearrange("p (a b) -> p a b", a=4), writes=[b_csm], slot="csm")
        sc.dma(SY, fmm, fmm_d.rearrange("p (k a b) -> p k a b", k=NS, a=2), writes=[b_csm], slot="csm")
        sc.dma(SY, wmk, wm_d.rearrange("p (a b) -> p a b", a=7), writes=[b_csm], slot="csm")
        sc.op(V, lambda e: e.memset(vcx[:, :, 128:130], 1.0), writes=[b_vc])
        for i in range(2):
            sc.op(V, lambda e, i=i: e.memset(vwx[i][:, :, 128:130], 1.0), writes=[b_vw[i]])

        kc_v = kcT_all.ap().rearrange("(r e) (h n) -> e h r n", r=NC, h=4)
        vc_v = vc_all.ap().rearrange("(r h n) e -> r h n e", r=NC, h=4)

        sbank = [0]
        SBANKS = [0, 1]
        ptc = [0]
        accn = [0]
        accsets = [(2, 3), (4, 5)]
        TB = 7
        wcnt = [0]

        wK_v = wscrK.ap().rearrange("(j k d) n -> d j k n", j=5, k=NS)
        wV_v = wscrV.ap().rearrange("(j k d) n -> d j k n", j=5, k=NS)

        def load_window(h, k):
            i = wcnt[0] % 2
            wcnt[0] += 1
            sc.dma(SY, kwT[i], wK_v[:, :, k, h * 128:(h + 1) * 128], reads=[b_wscr], writes=[b_kw[i]], slot="kw%d" % i)
            sc.dma(SY, vwx[i][:, :, 0:128], wV_v[:, :, k, h * 128:(h + 1) * 128], reads=[b_wscr], writes=[b_vw[i]],
                   slot="vw%d" % i)
            return i

        def branch(Qh, tiles, evac):
            n = len(tiles)
            aset = accsets[accn[0] % 2]
            accn[0] += 1
            state = {}

            def emit_S(t):
                kT, kreads, masks, vr, vreads, ptd = tiles[t]
                sb = SBANKS[sbank[0] % len(SBANKS)]
                sbank[0] += 1
                so = ps[sb].rearrange("p (g q) -> p g q", g=4)
                mm(so, kT, Qh, True, len(masks) == 0, kreads + [b_QT], [psb[sb]])
                for mi, (ml, mr, mreads) in enumerate(masks):
                    mm(ps[sb], ml, mr, False, mi == len(masks) - 1, mreads, [psb[sb]])
                if ptd is None:
                    pi = ptc[0] % NPT
                    ptc[0] += 1
                    dst, bd = PT[pi], b_PT[pi]
                else:
                    dst, bd = ptd
                sc.op(A, lambda e, dst=dst, sb=sb: e.activation(dst, ps[sb], AF.Exp, scale=SCALE), [psb[sb]], [bd])
                state[t] = (dst, bd)

            def emit_PV(t):
                kT, kreads, masks, vr, vreads, ptd = tiles[t]
                dst, bd = state[t]
                for g in range(4):
                    bank = aset[g // 2]
                    o = ps[bank][:, (g % 2) * 129:(g % 2) * 129 + 129]
                    mm(o, dst[:, g * 128:(g + 1) * 128], vr, (t == 0 and g % 2 == 0), t == n - 1,
                       [bd] + vreads, [psb[bank]], inc=(t == n - 1 and g % 2 == 1) or (g == 3), sgc=True)

            emit_S(0)
            for t in range(n):
                if t + 1 < n:
                    emit_S(t + 1)
                emit_PV(t)
            evac(aset)

        def combine(bi, aset, h, k, first, last):
            for half in range(2):
                bank = aset[half]
                dv = ps[bank][:, 128:258:129]
                sc.op(V, lambda e, dv=dv, half=half: e.tensor_scalar(den[:, bi, half * 2:half * 2 + 2], dv, 1e-30, None,
                                                                     ALU.max), [psb[bank]], [b_den])
            sc.op(V, lambda e: e.reciprocal(den[:, bi, :], den[:, bi, :]), [b_den], [b_den])
            gsl = gates[:, k, h * 12 + bi:h * 12 + 12:3]
            sc.op(V, lambda e, gsl=gsl: e.tensor_tensor(wgt[:, bi, :], den[:, bi, :], gsl, ALU.mult),
                  [b_den, b_gates], [b_wgt])
            for g in range(4):
                bank = aset[g // 2]
                o = ps[bank][:, (g % 2) * 129:(g % 2) * 129 + 128]
                if first:
                    sc.op(V, lambda e, o=o, g=g: e.tensor_scalar(ocomb[:, g, :], o, wgt[:, bi, g:g + 1], None, ALU.mult),
                          [psb[bank], b_wgt], [b_ocomb])
                else:
                    sc.op(V, lambda e, o=o, g=g: e.scalar_tensor_tensor(ocomb[:, g, :], o, wgt[:, bi, g:g + 1],
                                                                         ocomb[:, g, :], ALU.mult, ALU.add),
                          [psb[bank], b_wgt, b_ocomb], [b_ocomb])
            if last:
                sl = sgn[:, k, h * 512:(h + 1) * 512]
                sc.op(V, lambda e, sl=sl: e.tensor_tensor(sl, ocomb.rearrange("p g e -> p (g e)"), sl, ALU.mult),
                      [b_ocomb, b_sgn[k]], [b_sgn[k]])

        cmpm_v = cmpm_d.rearrange("p (k a b) -> p k a b", k=NS, a=4)

        def load_kv(h):
            hp = h % 2
            for r in range(NC):
                sc.dma(SY, KselT2[hp][:, r * 1024:(r + 1) * 1024],
                       kselT_all.ap()[r * 128:(r + 1) * 128, h * 1024:(h + 1) * 1024],
                       reads=[b_all], writes=[b_ksel2[hp]], slot="ksel%d" % hp)
                sc.dma(SY, Vsel2[hp][:, r * 8:(r + 1) * 8, :].rearrange("p k e -> p (k e)"),
                       vsel_all.ap()[r * 128:(r + 1) * 128, h * 1040:(h + 1) * 1040],
                       reads=[b_all], writes=[b_vsel2[hp]], slot="vsel%d" % hp)

        stepn = [0]
        for h in range(nh):
            KselT, Vsel = KselT2[h % 2], Vsel2[h % 2]
            b_ksel, b_vsel = b_ksel2[h % 2], b_vsel2[h % 2]
            sc.dma(SY, kcT.rearrange("p (r n) -> p r n", r=NC), kc_v[:, h], reads=[b_all], writes=[b_kc], slot="kc")
            for tc in range(4):
                for r2 in range(2):
                    sc.dma(SY, vcx[r2 * 64:(r2 + 1) * 64, tc, 0:128], vc_v[2 * tc + r2, h], reads=[b_all],
                           writes=[b_vc], slot="vc")
            for k in range(NS):
                Qh = QT[:, 4 * h:4 * h + 4, k * 128:(k + 1) * 128]
                wi = load_window(h, k)
                ci = stepn[0] % 2
                stepn[0] += 1
                sc.dma(SY, cmpm2[ci], cmpm_v[:, k], writes=[b_cmpm[ci]], slot="cmpm%d" % ci)
                if h == 0 and k == 0:
                    sc.dma(SY, emat, emat_d, writes=[b_cbig], slot="cbig")
                    sc.dma(SY, selcm, selcm_d.rearrange("p (a b) -> p a b", a=8), writes=[b_cbig], slot="cbig")
                    load_kv(0)
                if k == 1 and h + 1 < nh:
                    load_kv(h + 1)
                tiles = []
                for tc in range(4):
                    tiles.append((kcT[:, tc * 128:(tc + 1) * 128], [b_kc],
                                  [(ident, cmpm2[ci][:, tc, :], [b_const, b_cmpm[ci]])],
                                  vcx[:, tc, 0:129], [b_vc], (PTc[:, tc, :], b_PTc[tc])))

                def evac_c(aset, h=h, k=k):
                    UB = 6
                    for tc in range(4):
                        for g in range(4):
                            mm(ps[UB][:, g * 128:(g + 1) * 128], PTc[:, tc, g * 128:(g + 1) * 128], amat[:, tc, :],
                               (tc == 0 and g == 0), tc == 3, [b_PTc[tc], b_csm], [psb[UB]],
                               inc=(tc == 3 and g == 3), sgc=True)
                    combine(0, aset, h, k, True, False)
                    sc.op(V, lambda e: e.tensor_scalar(imp, ps[UB][:, 0:128], den[:, 0, 0:1], None, ALU.mult),
                          [psb[UB], b_den], [b_imp])
                    for g in range(1, 4):
                        sc.op(V, lambda e, g=g: e.scalar_tensor_tensor(imp, ps[UB][:, g * 128:(g + 1) * 128],
                                                                        den[:, 0, g:g + 1], imp, ALU.mult, ALU.add),
                              [psb[UB], b_den, b_imp], [b_imp])
                    sc.op(V, lambda e, k=k: e.tensor_tensor(imp, imp, fmm[:, k, 0, :], ALU.mult), [b_imp, b_csm], [b_imp])
                    sc.op(V, lambda e, k=k: e.tensor_tensor(imp, imp, fmm[:, k, 1, :], ALU.add), [b_imp, b_csm], [b_imp])
                    sc.op(V, lambda e: e.max(m8a, imp), [b_imp], [b_m8a])
                    sc.op(V, lambda e: e.match_replace(imp2, m8a, imp, -3.0e38), [b_m8a, b_imp], [b_imp2])
                    sc.op(V, lambda e: e.max(m8b, imp2), [b_imp2], [b_m8b])
                    sc.op(V, lambda e: e.tensor_scalar(mb, imp, m8b[:, 7:8], NEGB, ALU.is_lt, ALU.mult),
                          [b_imp, b_m8b], [b_mb])
                    tv = ps[TB].bitcast(BF16)
                    transp(tv[:, 0:128], mb, [b_mb], [psb[TB]])
                    sc.op(A, lambda e, tv=tv: e.activation(mbT, tv[:, 0:128].unsqueeze(1).broadcast_to([128, 4, 128]),
                                                           AF.Copy), [psb[TB]], [b_mbT])

                branch(Qh, tiles, evac_c)
                tiles = []
                for j in range(5):
                    if k == 0:
                        masks = [(ident, wmk[:, j, :], [b_const, b_csm])]
                    elif j == 0:
                        masks = [(ident, wmk[:, 5, :], [b_const, b_csm])]
                    elif j == 4:
                        masks = [(ident, wmk[:, 6, :], [b_const, b_csm])]
                    else:
                        masks = []
                    tiles.append((kwT[wi][:, j, :], [b_kw[wi]], masks, vwx[wi][:, j, 0:129], [b_vw[wi]], None))
                branch(Qh, tiles, lambda aset, h=h, k=k: combine(2, aset, h, k, False, False))
                tiles = []
                mbT2 = mbT.rearrange("p g q -> p (g q)")
                for t in range(8 * k + 8):
                    masks = [(emat[:, t * 128:(t + 1) * 128], mbT2, [b_cbig, b_mbT])]
                    if t >= 8 * k:
                        masks.append((ident, selcm[:, t - 8 * k, :], [b_const, b_cbig]))
                    pos = (t % 8) * 8 + t // 8
                    tiles.append((KselT[:, pos * 128:(pos + 1) * 128], [b_ksel], masks, Vsel[:, pos, 0:129], [b_vsel],
                                  None))
                branch(Qh, tiles, lambda aset, h=h, k=k: combine(1, aset, h, k, False, True))

        if stage == 2:
            import os
            for _i in range(int(os.environ.get("K_DUMMY_MM", "0"))):
                mm(ps[6][:, 0:8], ident, ident[:, 0:8], True, True, [b_const], [psb[6]], inc=(_i % 64 == 63))
            sc.barrier()
            sc.dma(SY, dbg["og"].rearrange("p (a b) -> p a b", a=NS), sgn, reads=b_sgn, slot="dbg")
            sc.barrier()
            sc.emit(block)
            return nc

        sc.barrier()
        ph.reset()
        qs.reset()
        ogT = qs.alloc([16, 1024], BF16)
        sgn = qs.alloc([NS, 2048], BF16)
        b_ogT = Buf("ogT")
        hTo = ph.alloc([16, 1024], BF16)
        b_hTo = Buf("hTo")
        ypl = ph.alloc([8, 1024], BF16)
        b_ypl = Buf("ypl")
        NW3 = 2
        w3 = [ph.alloc([56, 256], BF16) for _ in range(NW3)]
        b_w3 = [Buf("w3_%d" % i) for i in range(NW3)]
        sg0 = ph.alloc([512], F32)
        sg1 = ph.alloc([512], F32)
        m0 = ph.alloc([512], F32)
        m1 = ph.alloc([512], F32)
        b_sg0, b_sg1, b_m0, b_m1 = Buf("sg0"), Buf("sg1"), Buf("m0"), Buf("m1")
        sc.dma(SY, hTo, hT_sp.ap().rearrange("p (c t) -> p c t", c=16), writes=[b_hTo], slot="p3a")
        sc.dma(SY, ypl, yp_sp.ap().rearrange("p (b t) -> p b t", b=8), writes=[b_ypl], slot="p3a")
        tcount = [0]
        for k in range(NS):
            for c4 in range(4):
                tb = 6 + (tcount[0] % 2)
                tcount[0] += 1
                tv = ps[tb].bitcast(BF16)
                for cc in range(4):
                    c = c4 * 4 + cc
                    transp(tv[:, cc * 128:(cc + 1) * 128], sgn[:, k, c * 128:(c + 1) * 128], [b_sgn[k]], [psb[tb]])
                sc.op(A if (tcount[0] % 2) else V,
                      (lambda e, tv=tv, c4=c4, k=k: e.activation(
                          ogT[:, c4 * 4:c4 * 4 + 4, k * 128:(k + 1) * 128],
                          tv[:, 0:512].rearrange("p (c t) -> p c t", c=4), AF.Copy)) if (tcount[0] % 2) else
                      (lambda e, tv=tv, c4=c4, k=k: e.tensor_copy(
                          ogT[:, c4 * 4:c4 * 4 + 4, k * 128:(k + 1) * 128],
                          tv[:, 0:512].rearrange("p (c t) -> p c t", c=4))),
                      [psb[tb]], [b_ogT])
        sc.barrier()
        mergedT = sgn.rearrange("p k n -> p (k n)").rearrange("p (c t) -> p c t", c=16)
        b_mg = Buf("merged")
        wpo_v = wpo_d.rearrange("(c p) n -> p c n", p=128)
        wno_v = wno_d.rearrange("(c p) n -> p c n", p=128)
        wmg_v = wmg_d.rearrange("(c p) n -> p c n", p=128)
        pcount = [0]
        for ob in range(16):
            i = (ob // 2) % NW3
            o2 = ob % 2
            if o2 == 0:
                cols = slice(ob * 128, ob * 128 + 256)
                sc.dma(G, w3[i][:, 0:8, :], wpo_v[:, :, cols], writes=[b_w3[i]], slot="w3_%d" % i)
                sc.dma(G, w3[i][:, 8:24, :], wno_v[:, :, cols], writes=[b_w3[i]], slot="w3_%d" % i)
                sc.dma(G, w3[i][:, 24:40, :], wmg_v[:, :, cols], writes=[b_w3[i]], slot="w3_%d" % i)
                sc.dma(G, w3[i][:, 40:56, :], wmg_v[:, :, D + ob * 128:D + ob * 128 + 256], writes=[b_w3[i]],
                       slot="w3_%d" % i)
            w3s = w3[i][:, :, o2 * 128:(o2 + 1) * 128]
            for tg in range(2):
                tsl = slice(tg * 512, (tg + 1) * 512)
                pa, pg0, pbk, pg1 = [(pcount[0] * 4 + x) % 6 for x in range(4)]
                pcount[0] += 1
                for c in range(8):
                    mm(ps[pa], w3s[:, c, :], ypl[:, c, tsl], c == 0, c == 7, [b_w3[i], b_ypl], [psb[pa]])
                for c in range(16):
                    mm(ps[pg0], w3s[:, 24 + c, :], hTo[:, c, tsl], c == 0, c == 15, [b_w3[i], b_hTo], [psb[pg0]])
                for c in range(16):
                    mm(ps[pbk], w3s[:, 8 + c, :], ogT[:, c, tsl], c == 0, c == 15, [b_w3[i], b_ogT], [psb[pbk]])
                for c in range(16):
                    mm(ps[pg1], w3s[:, 40 + c, :], hTo[:, c, tsl], c == 0, c == 15, [b_w3[i], b_hTo], [psb[pg1]])
                sc.op(A, lambda e, pg0=pg0, ob=ob: e.activation(sg0, ps[pg0], AF.Sigmoid, bias=bmerge[:, ob:ob + 1]),
                      [psb[pg0], b_const], [b_sg0])
                sc.op(A, lambda e, pg1=pg1, ob=ob: e.activation(sg1, ps[pg1], AF.Sigmoid,
                                                                 bias=bmerge[:, 16 + ob:17 + ob]),
                      [psb[pg1], b_const], [b_sg1])
                sc.op(V, lambda e, pa=pa: e.tensor_tensor(m0, ps[pa], sg0, ALU.mult), [psb[pa], b_sg0], [b_m0])
                sc.op(V, lambda e, pbk=pbk: e.tensor_tensor(m1, ps[pbk], sg1, ALU.mult), [psb[pbk], b_sg1], [b_m1])
                sc.op(V, lambda e, ob=ob, tsl=tsl: e.tensor_tensor(mergedT[:, ob, tsl], m0, m1, ALU.add),
                      [b_m0, b_m1], [b_mg])
        sc.barrier()
        ph.reset()
        wout = ph.alloc([16, D], BF16)
        b_wout = [Buf("wout%d" % c_) for c_ in range(4)]
        fnw = ph.alloc([D], F32)
        xt = [ph.alloc([D], F32) for _ in range(2)]
        b_xt = [Buf("xt0"), Buf("xt1")]
        junk = ph.alloc([D], BF16)
        b_junk = Buf("junk")
        ssq = ph.alloc([2], F32)
        b_ssq = Buf("ssq")
        wout_v = wout_d.rearrange("(c p) n -> p c n", p=128)
        for cg in range(4):
            sc.dma(G, wout[:, :, cg * 512:(cg + 1) * 512], wout_v[:, :, cg * 512:(cg + 1) * 512], writes=[b_wout[cg]],
                   slot="wout%d" % cg)
        sc.dma(SY, fnw, fnw_d, writes=[b_const], slot="const")
        for k in range(NS):
            i = k % 2
            sc.dma(SY, xt[i], xtok_d[k * 128:(k + 1) * 128, :], writes=[b_xt[i]], slot="xt%d" % i)
            for cg in range(4):
                pb = (k * 4 + cg) % 6
                for c in range(16):
                    mm(ps[pb], mergedT[:, c, k * 128:(k + 1) * 128], wout[:, c, cg * 512:(cg + 1) * 512],
                       c == 0, c == 15, [b_mg, b_wout[cg]], [psb[pb]])
                sc.op(V, lambda e, i=i, pb=pb, cg=cg: e.tensor_tensor(xt[i][:, cg * 512:(cg + 1) * 512], ps[pb],
                                                                      xt[i][:, cg * 512:(cg + 1) * 512], ALU.add),
                      [psb[pb], b_xt[i]], [b_xt[i]])
            sc.op(V, lambda e, i=i: e.memset(ssq[:, i:i + 1], 0.0), [], [b_ssq])
            sc.op(A, lambda e, i=i: e.activation(junk, xt[i], AF.Square, accum_out=ssq[:, i:i + 1]),
                  [b_xt[i]], [b_junk, b_ssq])
            sc.op(A, lambda e, i=i: e.activation(ssq[:, i:i + 1], ssq[:, i:i + 1], AF.Sqrt, bias=EPS, scale=1.0 / D),
                  [b_ssq], [b_ssq])
            sc.op(V, lambda e, i=i: e.reciprocal(ssq[:, i:i + 1], ssq[:, i:i + 1]), [b_ssq], [b_ssq])
            sc.op(V, lambda e, i=i: e.scalar_tensor_tensor(xt[i], xt[i], ssq[:, i:i + 1], fnw, ALU.mult, ALU.mult),
                  [b_xt[i], b_ssq, b_const], [b_xt[i]])
            sc.dma(SY, out_d[k * 128:(k + 1) * 128, :], xt[i], reads=[b_xt[i]], slot="xt%d" % i)
        sc.barrier()
        sc.emit(block)
    return nc


def _host_inputs(inp):
    f32 = np.float32
    bf = ml_dtypes.bfloat16
    x = np.asarray(inp["x"], f32)[0]
    xpad = np.zeros((S + 32, D), f32)
    xpad[16:16 + S] = x
    half = 64
    inv = (np.float32(10000.0) ** (-np.arange(half, dtype=f32) / np.float32(half))).astype(f32)
    common = {
        "normw": np.ascontiguousarray(np.asarray(inp["norm_w"], f32)[0].reshape(16, 128).T),
        "w_in": np.ascontiguousarray(np.asarray(inp["w_in"], f32)[0]),
        "pool_mix": np.ascontiguousarray(np.asarray(inp["pool_mix"], f32)[0]),
        "pscale": np.ascontiguousarray(np.asarray(inp["pool_scale"], f32)[0].reshape(8, 128).T),
        "pe_kT": np.ascontiguousarray(np.asarray(inp["cmp_pe_k"], f32)[0].T),
        "pe_vT": np.ascontiguousarray(np.asarray(inp["cmp_pe_v"], f32)[0].T),
        "w1k": np.ascontiguousarray(np.asarray(inp["cmp_w1_k"], f32)[0]),
        "w1v": np.ascontiguousarray(np.asarray(inp["cmp_w1_v"], f32)[0]),
        "w2k": np.ascontiguousarray(np.asarray(inp["cmp_w2_k"], f32)[0]),
        "w2v": np.ascontiguousarray(np.asarray(inp["cmp_w2_v"], f32)[0]),
        "w_pool_out": np.ascontiguousarray(np.asarray(inp["w_pool_out"], f32)[0]),
        "w_nsa_out": np.ascontiguousarray(np.asarray(inp["w_nsa_out"], f32)[0]),
        "w_merge": np.ascontiguousarray(np.asarray(inp["w_merge"], f32)[0]),
        "bmerge": np.ascontiguousarray(np.asarray(inp["b_merge"], f32)[0].reshape(32, 128).T),
        "w_out": np.ascontiguousarray(np.asarray(inp["w_out"], f32)[0]),
        "fnw": np.ascontiguousarray(np.broadcast_to(np.asarray(inp["final_norm_w"], f32)[None, :], (128, D))),
        "ident": np.eye(128, dtype=f32).astype(bf),
    }
    emat = np.zeros((128, S), f32)
    emat[np.arange(S) // 64, np.arange(S)] = 1.0
    common["emat"] = emat.astype(bf)
    kk = np.arange(128)[:, None]
    qq = np.arange(128)[None, :]
    caus = np.where(kk <= qq, 0.0, NEGB).astype(f32)
    upper = np.where(kk > qq, 0.0, NEGB).astype(f32)
    full = np.zeros((128, 128), f32)
    none = np.full((128, 128), NEGB, f32)
    maps = []
    for c in range(NC):
        m = dict(common)
        blocks = [8 * k + c for k in range(NS)]
        xT = np.zeros((D, NT), f32)
        for k, i in enumerate(blocks):
            xT[:, k * TW:(k + 1) * TW] = xpad[128 * i:128 * i + TW].T
        m["xT"] = xT
        m["xtok"] = np.ascontiguousarray(np.concatenate([x[128 * i:128 * i + 128] for i in blocks], 0))
        cs = np.zeros((128, NS, 2, 64), f32)
        for k, i in enumerate(blocks):
            pos = (128 * i + np.arange(128)).astype(f32)
            ang = pos[:, None] * inv[None, :]
            cs[:, k, 0] = np.cos(ang)
            cs[:, k, 1] = np.sin(ang)
        m["ropecs"] = cs.reshape(128, -1)
        cc = np.zeros((128, 2, 64), f32)
        for h2 in range(2):
            for k, i in enumerate(blocks):
                n = 8 * i + np.arange(8)
                pos = (n * 16 + 31).astype(f32)
                ang = pos[:, None] * inv[None, :]
                r0 = h2 * 64 + k * 8
                cc[r0:r0 + 8, 0] = np.cos(ang)
                cc[r0:r0 + 8, 1] = np.sin(ang)
        m["ropecc"] = cc.reshape(128, -1)
        ic = np.zeros((128, 4, 16), f32)
        for g in range(4):
            w = 2 << g
            t = 128 * blocks[0] + np.arange(16)
            ic[:, g, :] = 1.0 / np.minimum(t + 1, w)
        m["invc"] = ic.reshape(128, -1)
        scm = np.zeros((128, 8, 4, 128), f32)
        for j in range(8):
            mk = full if j < c else (caus if j == c else none)
            scm[:, j] = mk[:, None, :]
        m["selcm"] = scm.reshape(128, -1).astype(bf)
        wmm = np.zeros((128, 7, 4, 128), f32)
        for j in range(5):
            t = c - 4 + j
            if t < 0:
                mk = none
            elif j == 0:
                mk = upper
            elif j == 4:
                mk = caus
            else:
                mk = full
            wmm[:, j] = mk[:, None, :]
        wmm[:, 5] = upper[:, None, :]
        wmm[:, 6] = caus[:, None, :]
        m["wm"] = wmm.reshape(128, -1).astype(bf)
        npr = np.arange(512)
        r_ = npr // 64
        k_ = (npr % 64) // 8
        nl_ = npr % 8
        nglob = 8 * (8 * k_ + r_) + nl_
        cend = nglob * 16 + 31
        cm = np.zeros((NS, 512, 128), f32)
        for k, i in enumerate(blocks):
            tpos = 128 * i + np.arange(128)
            valid = (cend[:, None] <= tpos[None, :]) & (nglob[:, None] < 511)
            cm[k] = np.where(valid, 0.0, NEGB)
        cm = cm.reshape(NS, 4, 128, 128)
        cm = np.broadcast_to(cm[:, :, :, None, :], (NS, 4, 128, 4, 128))
        m["cmpm"] = np.ascontiguousarray(cm.transpose(2, 0, 1, 3, 4)).reshape(128, -1).astype(bf)
        am = np.zeros((512, 128), f32)
        for j in range(128):
            for mm_ in range(4):
                for nn in range(2):
                    idx = 4 * j + mm_ - nn
                    if 0 <= idx < 511:
                        am[nglob == idx, j] += 1.0
        m["amat"] = np.ascontiguousarray(am.reshape(4, 128, 128).transpose(1, 0, 2)).reshape(128, -1).astype(bf)
        fm = np.zeros((128, NS, 2, 128), f32)
        jb = np.arange(128)[None, :]
        for k, i in enumerate(blocks):
            tpos = 128 * i + np.arange(128)
            jt = (tpos // 64)[:, None]
            forced = (jb == 0) | (jb == jt) | (jb == jt - 1)
            fut = jb > jt
            M = np.where(forced | fut, 0.0, 1.0)
            B = np.where(fut, -1e30, np.where(forced, 1e6, 0.0))
            fm[:, k, 0] = M
            fm[:, k, 1] = B
        m["fmm"] = fm.reshape(128, -1)
        maps.append(m)
    return maps


_NC_CACHE = {}


def kernel(**inputs):
    maps = _host_inputs(inputs)
    if "nc" not in _NC_CACHE:
        _NC_CACHE["nc"] = build()
    nc = _NC_CACHE["nc"]
    res = run_bass_kernel_spmd(nc, maps, core_ids=list(range(NC)))
    out = np.zeros((S, D), np.float32)
    for c in range(NC):
        o = np.asarray(res.results[c]["out"], np.float32)
        for k in range(NS):
            i = 8 * k + c
            out[128 * i:128 * i + 128] = o[k * 128:(k + 1) * 128]
    return out[None]
```

```python
import numpy as np
import ml_dtypes
from contextlib import ExitStack
import concourse.bass as bass
import concourse.mybir as mybir
from concourse.bass_utils import run_bass_kernel_spmd

F32 = mybir.dt.float32
BF16 = mybir.dt.bfloat16
ALU = mybir.AluOpType
AF = mybir.ActivationFunctionType

S = 8192
D = 2048
NC = 8
NS = 8
TW = 160
NT = NS * TW
DIN = 9264
NEGB = -30000.0
SCALE = 128 ** -0.5
EPS = 1e-6

ENGS = ["tensor", "vector", "scalar", "gpsimd", "sync"]


class Buf:
    __slots__ = ("w", "r", "name")

    def __init__(self, name=""):
        self.w = None
        self.r = []
        self.name = name


class Sched:
    def __init__(self, nc, stack):
        self.nc = nc
        self.stack = stack
        self.ops = {e: [] for e in ENGS}
        self.sem = {e: stack.enter_context(nc.semaphore("sem_" + e)) for e in ENGS}
        self.cnt = {e: 0 for e in ENGS}
        self.opidx = {e: 0 for e in ENGS}
        self.seen = {e: {} for e in ENGS}
        self.dsem = {}
        self.dcnt = {}
        self.widx = {}

    def _wait(self, e, ev):
        key, val = ev
        if self.seen[e].get(key, 0) >= val:
            return
        self.seen[e][key] = val
        self.ops[e].append(("wait", key, val))

    def _deps(self, e, reads, writes):
        for b in reads:
            if b.w is not None:
                if b.w[0] == ("E", e):
                    if e == "tensor":
                        continue
                    if self.opidx[e] - self.widx.get(id(b), -10) >= 3:
                        continue
                self._wait(e, b.w)
        for b in writes:
            if b.w is not None and b.w[0] != ("E", e):
                self._wait(e, b.w)
            for r in b.r:
                if r[0] != ("E", e):
                    self._wait(e, r)

    def op(self, e, fn, reads=(), writes=(), inc=True):
        if e != "tensor":
            pr = [b for b in reads if b.name.startswith("ps")]
            if pr:
                reads = [b for b in reads if not b.name.startswith("ps")]
                writes = list(writes) + pr
        self._deps(e, reads, writes)
        ev = (("E", e), self.cnt[e] + 1)
        if inc:
            self.cnt[e] += 1
        self.ops[e].append(("op", fn, inc, ev[1]))
        for b in reads:
            b.r.append(ev)
        for b in writes:
            b.w = ev
            b.r = []
            self.widx[id(b)] = self.opidx[e]
        self.opidx[e] += 1

    def dma(self, q, out, in_, reads=(), writes=(), slot=None, in_fn=None, **kw):
        if slot not in self.dsem:
            self.dsem[slot] = self.stack.enter_context(self.nc.semaphore("d_" + slot))
            self.dcnt[slot] = 0
        self._deps(q, reads, writes)
        self.dcnt[slot] += 16
        ev = (("D", slot), self.dcnt[slot])
        sem = self.dsem[slot]
        if in_fn is not None:
            self.ops[q].append(("raw", lambda eng, out=out, in_fn=in_fn, sem=sem, kw=kw: eng.dma_start(
                out=out, in_=in_fn(eng), **kw).then_inc(sem, 16)))
        else:
            self.ops[q].append(("raw", lambda eng, out=out, in_=in_, sem=sem, kw=kw: eng.dma_start(
                out=out, in_=in_, **kw).then_inc(sem, 16)))
        for b in reads:
            b.r.append(ev)
        for b in writes:
            b.w = ev
            b.r = []

    def raw(self, e, fn, reads=(), writes=(), ev=None):
        self._deps(e, reads, writes)
        self.ops[e].append(("raw", fn))
        for b in reads:
            b.r.append(ev)
        for b in writes:
            b.w = ev
            b.r = []

    def barrier(self, bufs=()):
        for e in ENGS:
            for o in ENGS:
                if o != e and o != "sync" and self.cnt[o] > 0:
                    self._wait(e, (("E", o), self.cnt[o]))
            for s, v in self.dcnt.items():
                if v > 0:
                    self._wait(e, (("D", s), v))

    def emit(self, block):
        ops = self.ops
        needed = {e: set() for e in ENGS}
        for e in ENGS:
            for rec in ops[e]:
                if rec[0] == "wait" and rec[1][0] == "E":
                    needed[rec[1][1]].add(rec[2])
        remap = {e: {v: i + 1 for i, v in enumerate(sorted(needed[e]))} for e in ENGS}
        sems, dsems = self.sem, self.dsem

        def run(e, eng):
            for rec in ops[e]:
                if rec[0] == "wait":
                    key, val = rec[1], rec[2]
                    if key[0] == "E":
                        eng.wait_ge(sems[key[1]], remap[key[1]][val])
                    else:
                        eng.wait_ge(dsems[key[1]], val)
                elif rec[0] == "op":
                    _, fn, inc, val = rec
                    if inc and val in needed[e]:
                        fn(eng).then_inc(sems[e], 1)
                    else:
                        fn(eng)
                else:
                    rec[1](eng)

        @block.tensor
        def _(eng):
            run("tensor", eng)

        @block.vector
        def _(eng):
            run("vector", eng)

        @block.scalar
        def _(eng):
            run("scalar", eng)

        @block.gpsimd
        def _(eng):
            run("gpsimd", eng)

        @block.sync
        def _(eng):
            run("sync", eng)


class Arena:
    def __init__(self, tensor, base, size):
        self.t = tensor
        self.base = base
        self.size = size
        self.off = 0

    def sub(self, size):
        a = Arena(self.t, self.base + self.off, size)
        self.off += size
        assert self.off <= self.size, (self.off, self.size)
        return a

    def reset(self):
        self.off = 0

    def alloc(self, shape, dtype):
        es = 2 if dtype == BF16 else 4
        n = int(np.prod(shape))
        nbytes = (n * es + 63) // 64 * 64
        o = self.base + self.off
        self.off += nbytes
        assert self.off <= self.size, ("arena overflow", self.off, self.size)
        ap = self.t[:, o // 2: o // 2 + n * es // 2]
        if dtype != BF16:
            ap = ap.bitcast(dtype)
        if len(shape) == 2:
            ap = ap.rearrange("p (a b) -> p a b", a=shape[0])
        elif len(shape) == 3:
            ap = ap.rearrange("p (a b c) -> p a b c", a=shape[0], b=shape[1])
        elif len(shape) == 4:
            ap = ap.rearrange("p (a b c d) -> p a b c d", a=shape[0], b=shape[1], c=shape[2])
        return ap


def build(stage=99, nh=4):
    nc = bass.Bass("TRN2", target_bir_lowering=False)
    dt = nc.dram_tensor

    def ext_in(name, shape, dtype=F32):
        return dt(name, list(shape), dtype, kind="ExternalInput").ap()

    def ext_out(name, shape, dtype=F32):
        return dt(name, list(shape), dtype, kind="ExternalOutput").ap()

    xT_d = ext_in("xT", [D, NT])
    xtok_d = ext_in("xtok", [NS * 128, D])
    normw_d = ext_in("normw", [128, 16])
    w_in_d = ext_in("w_in", [D, DIN])
    mix_d = ext_in("pool_mix", [4, 256, 256])
    pscale_d = ext_in("pscale", [128, 8])
    pek_d = ext_in("pe_kT", [128, 32])
    pev_d = ext_in("pe_vT", [128, 32])
    w1k_d = ext_in("w1k", [32, 128, 256])
    w1v_d = ext_in("w1v", [32, 128, 256])
    w2k_d = ext_in("w2k", [256, 128])
    w2v_d = ext_in("w2v", [256, 128])
    wpo_d = ext_in("w_pool_out", [1024, D])
    wno_d = ext_in("w_nsa_out", [D, D])
    wmg_d = ext_in("w_merge", [D, 2 * D])
    bmg_d = ext_in("bmerge", [128, 32])
    wout_d = ext_in("w_out", [D, D])
    fnw_d = ext_in("fnw", [128, D])
    ropecs_d = ext_in("ropecs", [128, NS * 2 * 64])
    ropecc_d = ext_in("ropecc", [128, 2 * 64])
    invc_d = ext_in("invc", [128, 4 * 16])
    ident_d = ext_in("ident", [128, 128], BF16)
    emat_d = ext_in("emat", [128, S], BF16)
    selcm_d = ext_in("selcm", [128, 8 * 512], BF16)
    wm_d = ext_in("wm", [128, 7 * 512], BF16)
    cmpm_d = ext_in("cmpm", [128, NS * 4 * 512], BF16)
    amat_d = ext_in("amat", [128, 4 * 128], BF16)
    fmm_d = ext_in("fmm", [128, NS * 2 * 128])
    out_d = ext_out("out", [NS * 128, D])

    kselT_loc = dt("kselT_loc", [128, 4096], BF16)
    kwinT_loc = dt("kwinT_loc", [NS * 128, 512], BF16)
    vsel_loc = dt("vsel_loc", [128, 4160], BF16)
    vwin_loc = dt("vwin_loc", [NS * 128, 512], BF16)
    kcT_loc = dt("kcT_loc", [128, 256], BF16)
    vc_loc = dt("vc_loc", [256, 128], BF16)
    kselT_all = dt("kselT_all", [NC * 128, 4096], BF16)
    kwinT_all = dt("kwinT_all", [(32 + NC * NS) * 128, 512], BF16)
    vsel_all = dt("vsel_all", [NC * 128, 4160], BF16)
    vwin_all = dt("vwin_all", [(32 + NC * NS) * 128, 512], BF16)
    kcT_all = dt("kcT_all", [NC * 128, 256], BF16)
    vc_all = dt("vc_all", [NC * 256, 128], BF16)
    hT_sp = dt("hT_sp", [128, 16 * NS * 128], BF16)
    yp_sp = dt("yp_sp", [128, 8 * 1024], BF16)

    dbg = {}
    if stage < 99:
        dbg["QT"] = ext_out("dbg_QT", [128, 16 * 1024], BF16)
        dbg["sgn"] = ext_out("dbg_sgn", [128, NS * 2048], BF16)
        dbg["gates"] = ext_out("dbg_gates", [128, NS * 48])
        dbg["ypool"] = ext_out("dbg_ypool", [128, 8 * 1024], BF16)
        dbg["kselT"] = ext_out("dbg_kselT", [128, 4096], BF16)
        dbg["kwinT"] = ext_out("dbg_kwinT", [NS * 128, 512], BF16)
        dbg["vsel"] = ext_out("dbg_vsel", [128, 4160], BF16)
        dbg["vwin"] = ext_out("dbg_vwin", [NS * 128, 512], BF16)
        dbg["kcT"] = ext_out("dbg_kcT", [128, 256], BF16)
        dbg["vc"] = ext_out("dbg_vc", [256, 128], BF16)
        dbg["og"] = ext_out("dbg_og", [128, NS * 2048], BF16)

    stack = ExitStack()
    with stack:
        ARENA_BYTES = 204 * 1024
        arena_t = stack.enter_context(nc.sbuf_tensor("arena", [128, ARENA_BYTES // 2], BF16))
        ps = [stack.enter_context(nc.psum_tensor("ps%d" % i, [128, 512], F32))[:] for i in range(8)]
        psb = [Buf("ps%d" % i) for i in range(8)]
        sc = Sched(nc, stack)
        block = stack.enter_context(nc.Block())

        top = Arena(arena_t, 0, ARENA_BYTES)
        ident = top.alloc([128], BF16)
        ones = top.alloc([128], BF16)
        normw = top.alloc([16], F32)
        pscale = top.alloc([8], F32)
        bmerge = top.alloc([32], F32)
        gates = top.alloc([NS, 48], F32)
        b_const = Buf("const")
        b_gates = Buf("gates")
        b_pay = Buf("pay")
        qs = top.sub(64 * 1024)
        ph = top.sub(ARENA_BYTES - top.off)

        QT = qs.alloc([16, 1024], BF16)
        sgn = qs.alloc([NS, 2048], BF16)
        b_QT = Buf("QT")
        b_sgn = [Buf("sgn%d" % k) for k in range(NS)]

        V = "vector"
        A = "scalar"
        G = "gpsimd"
        T = "tensor"
        SY = "sync"

        def mm(out, lhsT, rhs, start, stop, reads, writes, inc=None, sgc=False):
            if inc is None:
                inc = stop
            if sgc:
                sc.op(T, lambda e: e.matmul(out, lhsT, rhs, start=start, stop=stop, skip_group_check=True),
                      reads, writes, inc=inc)
            else:
                sc.op(T, lambda e: e.matmul(out, lhsT, rhs, start=start, stop=stop), reads, writes, inc=inc)

        def transp(out, in_, reads, writes):
            sc.op(T, lambda e: e.transpose(out, in_, ident), reads + [b_const], writes)

        sc.dma(SY, ident, ident_d, writes=[b_const], slot="const")
        sc.dma(SY, normw, normw_d, writes=[b_const], slot="const")
        sc.dma(SY, pscale, pscale_d, writes=[b_const], slot="const")
        sc.dma(SY, bmerge, bmg_d, writes=[b_const], slot="const")
        sc.op(V, lambda e: e.memset(ones, 1.0), writes=[b_const])

        hT = ph.alloc([16, NT], BF16)
        b_hT = [Buf("hT%d" % c_) for c_ in range(16)]
        ypool = ph.alloc([8, 1024], BF16)
        b_ypool = Buf("ypool")
        NWB = 2
        wbuf = [ph.alloc([16, 512], BF16) for _ in range(NWB)]
        b_wbuf = [Buf("wbuf%d" % i) for i in range(NWB)]
        ropecs = ph.alloc([NS, 2, 64], F32)
        ropecc = ph.alloc([2, 64], F32)
        invc = ph.alloc([4, 16], F32)
        sc.dma(SY, ropecs, ropecs_d.rearrange("p (k a f) -> p k a f", k=NS, a=2), writes=[b_const], slot="const")
        sc.dma(SY, ropecc, ropecc_d.rearrange("p (a f) -> p a f", a=2), writes=[b_const], slot="const")
        sc.dma(SY, invc, invc_d.rearrange("p (g t) -> p g t", g=4), writes=[b_const], slot="const")
        mixw = ph.alloc([4, 2, 256], BF16)
        w2 = ph.alloc([2, 2, 128], BF16)
        p1mark = ph.off

        xbuf = [ph.alloc([NT], F32) for _ in range(2)]
        b_xbuf = [Buf("xbuf0"), Buf("xbuf1")]
        sq = [ph.alloc([NT], BF16) for _ in range(2)]
        b_sq = [Buf("sq0"), Buf("sq1")]
        rstd = ph.alloc([NT], F32)
        b_rstd = Buf("rstd")
        xT_v = xT_d.rearrange("(c p) n -> c p n", p=128)
        segs = [(0, 512), (512, 1024), (1024, 1280)]
        for c in range(16):
            j = c % 2
            sc.dma(SY, xbuf[j], xT_v[c], writes=[b_xbuf[j]], slot="xbuf%d" % j)
            sc.op(A, lambda e, j=j: e.activation(sq[j], xbuf[j], AF.Square), [b_xbuf[j]], [b_sq[j]])
            for si, (a, b) in enumerate(segs):
                mm(ps[si][:, 0:b - a], ones, sq[j][:, a:b], c == 0, c == 15, [b_sq[j], b_const], [psb[si]],
                   inc=(si == 2))
        for si, (a, b) in enumerate(segs):
            sc.op(A, lambda e, si=si, a=a, b=b: e.activation(rstd[:, a:b], ps[si][:, 0:b - a], AF.Sqrt,
                                                               bias=EPS, scale=1.0 / D), [psb[si]], [b_rstd])
        sc.op(V, lambda e: e.reciprocal(rstd, rstd), [b_rstd], [b_rstd])
        for c in range(16):
            j = c % 2
            sc.dma(SY, xbuf[j], xT_v[c], writes=[b_xbuf[j]], slot="xbuf%d" % j)
            sc.op(V, lambda e, c=c, j=j: e.scalar_tensor_tensor(hT[:, c, :], xbuf[j], normw[:, c:c + 1], rstd,
                                                                 ALU.mult, ALU.mult),
                  [b_xbuf[j], b_rstd, b_const], [b_hT[c]])
        ph.off = p1mark

        wcount = [0]

        WPLAN = ([("win", 4096, 512), ("win", 4608, 512), ("w1", "k", 0), ("w1", "v", 0),
                  ("win", 5120, 512), ("win", 5632, 512), ("win", 6144, 512), ("win", 6656, 512),
                  ("win", 1024, 512), ("win", 1536, 512), ("win", 0, 512), ("win", 512, 512)]
                 + [("win", 2048 + 512 * i_, 512) for i_ in range(4)]
                 + [("win", 7168 + 512 * i_, 512) for i_ in range(4)] + [("win", 9216, 48)])
        wissued = [0]

        def w_issue(n):
            if n >= len(WPLAN) or n < wissued[0]:
                return
            assert n == wissued[0]
            wissued[0] += 1
            i = n % NWB
            ent = WPLAN[n]
            if ent[0] == "win":
                _, col0, ncols = ent
                sc.dma(G, wbuf[i][:, 0:16, 0:ncols], w_in_d[:, col0:col0 + ncols].rearrange("(c p) n -> p c n", p=128),
                       writes=[b_wbuf[i]], slot="wbuf%d" % i)
            else:
                w1d = w1k_d if ent[1] == "k" else w1v_d
                w1x = wbuf[i].rearrange("p a b -> p (a b)").rearrange("p (l f) -> p l f", l=32)
                sc.dma(G, w1x, w1d.rearrange("l d f -> d l f"), writes=[b_wbuf[i]], slot="wbuf%d" % i)

        def w_take(kind, key):
            n = wcount[0]
            wcount[0] += 1
            assert WPLAN[n][0] == kind and WPLAN[n][1] == key, (WPLAN[n], kind, key)
            w_issue(n)
            w_issue(n + 1)
            return wbuf[n % NWB], b_wbuf[n % NWB]

        def load_w(col0, ncols=512):
            return w_take("win", col0)

        cc_sem = stack.enter_context(nc.semaphore("cc"))
        sc.dsem["cc"] = cc_sem
        sc.dcnt["cc"] = 0
        b_all = Buf("all")

        b_pays = {}

        def bpay(t):
            if id(t) not in b_pays:
                b_pays[id(t)] = Buf("pay_" + str(len(b_pays)))
            return b_pays[id(t)]

        def allgather(src, dst):
            sc.dcnt["cc"] += 1
            dap = dst.ap()[4096:, :] if dst in (kwinT_all, vwin_all) else dst.ap()
            sc.raw(G, lambda e, src=src, dap=dap: e.collective_compute(
                "AllGather", ALU.bypass, replica_groups=[list(range(NC))],
                ins=[src.ap().opt()], outs=[dap.opt()]).then_inc(cc_sem, 1),
                reads=[bpay(src)], writes=[b_all], ev=(("D", "cc"), sc.dcnt["cc"]))

        qs.reset()
        sgpool = qs.alloc([8, 1024], BF16)
        b_sgpool = Buf("sgpool")
        kcraw = qs.alloc([4, NT], BF16)
        vcraw = qs.alloc([4, NT], BF16)
        b_raw = {"k": Buf("kcraw"), "v": Buf("vcraw")}
        uT = qs.alloc([NT], F32)
        sA = qs.alloc([NT], F32)
        sB = qs.alloc([NT], F32)
        b_uT, b_sA, b_sB = Buf("uT"), Buf("sA"), Buf("sB")
        pooled = qs.alloc([2, 1024], BF16)
        b_pooled = [Buf("pooled0"), Buf("pooled1")]
        pfix = qs.alloc([16], F32)
        b_pfix = Buf("pfix")
        b_mixw = Buf("mixw")
        hdn = qs.alloc([2, 256], BF16)
        b_hdn = Buf("hdn")
        cbias = qs.alloc([2], F32)
        b_cbias = Buf("cbias")
        peT = qs.alloc([2, 32], BF16)
        b_peT = Buf("peT")
        b_w2 = Buf("w2")
        kcr = qs.alloc([2, 128], BF16)
        b_kcr = Buf("kcr")
        ctmp = qs.alloc([4, 128], F32)
        b_ctmp = Buf("ctmp")
        kcTs = qs.alloc([256], BF16)
        b_kcTs = Buf("kcTs")
        vcs = qs.alloc([2, 128], BF16)
        b_vcs = Buf("vcs")

        sc.dma(G, mixw, mix_d.rearrange("g (cc p) d -> p g cc d", p=128), writes=[b_mixw], slot="mixw")
        sc.dma(G, peT[:, 0, :], pek_d, writes=[b_peT], slot="misc_pe")
        sc.dma(G, peT[:, 1, :], pev_d, writes=[b_peT], slot="misc_pe")
        sc.dma(G, w2[:, 0], w2k_d.rearrange("(fb p) e -> p fb e", p=128), writes=[b_w2], slot="misc_w2")
        sc.dma(G, w2[:, 1], w2v_d.rearrange("(fb p) e -> p fb e", p=128), writes=[b_w2], slot="misc_w2")

        def fm_proj(col0, evac):
            wb, bwb = load_w(col0)
            for j in range(4):
                base = 3 * (fm_proj.n % 2)
                fm_proj.n += 1
                for c in range(16):
                    for si, (a, b) in enumerate(segs):
                        mm(ps[base + si][:, 0:b - a], wb[:, c, j * 128:(j + 1) * 128], hT[:, c, a:b],
                           c == 0, c == 15, [bwb, b_hT[c]], [psb[base + si]], inc=(c == 15 and si == 2))
                evac(j, base)
        fm_proj.n = 0

        xs = ph.alloc([512], F32)
        b_xs = Buf("xs")
        rt = ph.alloc([4, 256], F32)
        b_rt = Buf("rt")
        qr = [ph.alloc([4, 128], BF16) for _ in range(2)]
        b_qr = [Buf("qr0"), Buf("qr1")]
        kst = ph.alloc([NS, 512], BF16)
        b_kst = Buf("kst")
        kstK = ph.alloc([4, NS, 128], BF16)
        b_kstK = Buf("kstK")
        kstV = ph.alloc([4, NS, 130], BF16)
        b_kstV = Buf("kstV")
        sc.op(V, lambda e: e.memset(kstV[:, :, :, 128:130], 1.0), writes=[b_kstV])
        tmcount = [0]

        def tm_proj(col0, ncols, evac):
            wb, bwb = load_w(col0, ncols)
            pending = None
            for k in range(NS):
                pb = tmcount[0] % 4
                tmcount[0] += 1
                t0 = k * TW + 16
                for c in range(16):
                    mm(ps[pb][:, 0:ncols], hT[:, c, t0:t0 + 128], wb[:, c, 0:ncols], c == 0, c == 15,
                       [bwb, b_hT[c]], [psb[pb]])
                if pending is not None:
                    pending()
                pending = evac(k, pb)
            if pending is not None:
                pending()

        def rope_tm(pb, k, dst_fn, bdst):
            sc.op(A, lambda e: e.activation(xs, ps[pb], AF.Copy), [psb[pb]], [b_xs])
            x4 = xs.rearrange("p (h d) -> p h d", h=4)
            x1, x2 = x4[:, :, 0:64], x4[:, :, 64:128]
            cs = ropecs[:, k, 0, :].unsqueeze(1).broadcast_to([128, 4, 64])
            sn = ropecs[:, k, 1, :].unsqueeze(1).broadcast_to([128, 4, 64])
            r4 = rt.rearrange("p a (h d) -> p a h d", h=4)
            sc.op(V, lambda e: e.tensor_tensor(r4[:, 0], x1, cs, ALU.mult), [b_xs, b_const], [b_rt])
            sc.op(V, lambda e: e.tensor_tensor(r4[:, 1], x2, sn, ALU.mult), [b_xs, b_const], [b_rt])
            sc.op(V, lambda e: e.tensor_tensor(r4[:, 2], x2, cs, ALU.mult), [b_xs, b_const], [b_rt])
            sc.op(V, lambda e: e.tensor_tensor(r4[:, 3], x1, sn, ALU.mult), [b_xs, b_const], [b_rt])
            qi = rope_tm.n % 2
            qrb, bq = qr[qi], b_qr[qi]
            sc.op(V, lambda e: e.tensor_tensor(qrb[:, :, 0:64], r4[:, 0], r4[:, 1], ALU.subtract), [b_rt], [bq])
            sc.op(V, lambda e: e.tensor_tensor(qrb[:, :, 64:128], r4[:, 2], r4[:, 3], ALU.add), [b_rt], [bq])
            pt = 4 + (rope_tm.n % 2)
            rope_tm.n += 1
            ptv = ps[pt].bitcast(BF16)

            def part2():
                for hh in range(4):
                    transp(ptv[:, hh * 128:(hh + 1) * 128], qrb[:, hh, :], [bq], [psb[pt]])
                sc.op(A, lambda e, ptv=ptv: e.activation(dst_fn(), ptv[:, 0:512].rearrange("p (h t) -> p h t", h=4),
                                                         AF.Copy), [psb[pt]], [bdst])
            return part2
        rope_tm.n = 0

        def evac_raw(which):
            dst = kcraw if which == "k" else vcraw

            def f(j, base):
                for si, (a, b) in enumerate(segs):
                    sc.op(A if si != 1 else V, (lambda e, si=si, a=a, b=b: e.activation(
                        dst[:, j, a:b], ps[base + si][:, 0:b - a], AF.Copy)) if si != 1 else
                        (lambda e, si=si, a=a, b=b: e.tensor_copy(dst[:, j, a:b], ps[base + si][:, 0:b - a])),
                        [psb[base + si]], [b_raw[which]])
            return f

        fm_proj(4096, evac_raw("k"))
        fm_proj(4608, evac_raw("v"))

        for wi, which in enumerate(["k", "v"]):
            raw = (kcraw if which == "k" else vcraw).rearrange("p h (k t) -> p h k t", k=NS)
            w1d = w1k_d if which == "k" else w1v_d
            wbx, bw1 = w_take("w1", which)
            w1 = wbx.rearrange("p a b -> p (a b)").rearrange("p (l f) -> p l f", l=32)
            for fb in range(2):
                pb = fb
                for l in range(32):
                    mm(ps[pb][:, 0:256].rearrange("p (h k n) -> p h k n", h=4, k=NS),
                       w1[:, l, fb * 128:(fb + 1) * 128], raw[:, :, :, l + 16:l + 16 + 113:16],
                       l == 0, l == 31, [bw1, b_raw[which]], [psb[pb]])
                for l in range(32):
                    mm(ps[pb][:, 256:257], w1[:, l, fb * 128:(fb + 1) * 128], peT[:, wi, l:l + 1],
                       False, l == 31, [bw1, b_peT], [psb[pb]], sgc=True)
                sc.op(V, lambda e, pb=pb, fb=fb: e.tensor_copy(cbias[:, fb:fb + 1], ps[pb][:, 256:257]),
                      [psb[pb]], [b_cbias])
                sc.op(A, lambda e, pb=pb, fb=fb: e.activation(hdn[:, fb, :], ps[pb][:, 0:256], AF.Silu,
                                                                bias=cbias[:, fb:fb + 1]),
                      [psb[pb], b_cbias], [b_hdn])
            for mt in range(2):
                pb = 2 + mt
                for fb in range(2):
                    mm(ps[pb][:, 0:128], hdn[:, fb, mt * 128:(mt + 1) * 128], w2[:, wi, fb, :], fb == 0, fb == 1,
                       [b_hdn, b_w2], [psb[pb]])
                if which == "k":
                    x1 = ps[pb][:, 0:64]
                    x2 = ps[pb][:, 64:128]
                    cs, sn = ropecc[:, 0, :], ropecc[:, 1, :]
                    sc.op(V, lambda e, x1=x1, cs=cs: e.tensor_tensor(ctmp[:, 0, 0:64], x1, cs, ALU.mult), [psb[pb], b_const], [b_ctmp])
                    sc.op(V, lambda e, x2=x2, sn=sn: e.tensor_tensor(ctmp[:, 1, 0:64], x2, sn, ALU.mult), [psb[pb], b_const], [b_ctmp])
                    sc.op(V, lambda e, x2=x2, cs=cs: e.tensor_tensor(ctmp[:, 2, 0:64], x2, cs, ALU.mult), [psb[pb], b_const], [b_ctmp])
                    sc.op(V, lambda e, x1=x1, sn=sn: e.tensor_tensor(ctmp[:, 3, 0:64], x1, sn, ALU.mult), [psb[pb], b_const], [b_ctmp])
                    sc.op(V, lambda e, mt=mt: e.tensor_tensor(kcr[:, mt, 0:64], ctmp[:, 0, 0:64], ctmp[:, 1, 0:64], ALU.subtract), [b_ctmp], [b_kcr])
                    sc.op(V, lambda e, mt=mt: e.tensor_tensor(kcr[:, mt, 64:128], ctmp[:, 2, 0:64], ctmp[:, 3, 0:64], ALU.add), [b_ctmp], [b_kcr])
                    pt = 4 + mt
                    ptv = ps[pt].bitcast(BF16)
                    transp(ptv[:, 0:128], kcr[:, mt, :], [b_kcr], [psb[pt]])
                    sc.op(V, lambda e, ptv=ptv, mt=mt: e.tensor_copy(kcTs[:, mt * 128:(mt + 1) * 128], ptv[:, 0:128]),
                          [psb[pt]], [b_kcTs])
                else:
                    sc.op(V, lambda e, pb=pb, mt=mt: e.tensor_copy(vcs[:, mt, :], ps[pb][:, 0:128]), [psb[pb]], [b_vcs])
            if which == "k":
                sc.dma(SY, kcT_loc.ap(), kcTs, reads=[b_kcTs], writes=[bpay(kcT_loc)], slot="pay_kc")
                allgather(kcT_loc, kcT_all)
            else:
                sc.dma(SY, vc_loc.ap().rearrange("(mt p) e -> p mt e", p=128), vcs, reads=[b_vcs], writes=[bpay(vc_loc)], slot="pay_vc")
                allgather(vc_loc, vc_all)

        def kv_block(col0, dst_loc, is_k, sel):
            if is_k and sel:
                tm_proj(col0, 512, lambda k, pb: rope_tm(pb, k, lambda: kstK[:, :, k, :], b_kstK))
                sc.dma(SY, dst_loc.ap(), kstK.rearrange("p h k t -> p (h k t)"), reads=[b_kstK], writes=[bpay(dst_loc)],
                       slot="pay_" + dst_loc.name)
            elif sel:
                tm_proj(col0, 512, lambda k, pb: sc.op(
                    V, lambda e: e.tensor_copy(kstV[:, :, k, 0:128], ps[pb].rearrange("p (h e) -> p h e", h=4)),
                    [psb[pb]], [b_kstV]))
                sc.dma(SY, dst_loc.ap(), kstV.rearrange("p h k e -> p (h k e)"), reads=[b_kstV], writes=[bpay(dst_loc)],
                       slot="pay_" + dst_loc.name)
            elif is_k:
                tm_proj(col0, 512, lambda k, pb: rope_tm(
                    pb, k, lambda: kst[:, k, :].rearrange("p (h t) -> p h t", h=4), b_kst))
                sc.dma(SY, dst_loc.ap().rearrange("(k p) n -> p k n", p=128), kst, reads=[b_kst], writes=[bpay(dst_loc)], slot="pay_" + dst_loc.name)
            else:
                tm_proj(col0, 512, lambda k, pb: sc.op(
                    V, lambda e: e.tensor_copy(kst[:, k, :], ps[pb]), [psb[pb]], [b_kst]))
                sc.dma(SY, dst_loc.ap().rearrange("(k p) n -> p k n", p=128), kst, reads=[b_kst], writes=[bpay(dst_loc)], slot="pay_" + dst_loc.name)

        kv_block(5120, kselT_loc, True, True)
        allgather(kselT_loc, kselT_all)
        kv_block(5632, vsel_loc, False, True)
        allgather(vsel_loc, vsel_all)
        kv_block(6144, kwinT_loc, True, False)
        allgather(kwinT_loc, kwinT_all)
        kv_block(6656, vwin_loc, False, False)
        allgather(vwin_loc, vwin_all)

        wscrK = dt("wscrK", [5 * NS * 128, 512], BF16)
        wscrV = dt("wscrV", [5 * NS * 128, 512], BF16)
        b_wscr = Buf("wscr")
        wregs = {}

        for t_all in (kwinT_all, vwin_all):
            sc.dma(SY, t_all.ap()[0:4096, :], t_all.ap()[4096 + 31 * 128:4096 + 63 * 128, :], reads=[b_all],
                   writes=[b_all], slot="xcopy")

        def wsrc(eng, t_all):
            pid = eng.partition_id()
            return t_all.ap()[bass.ds(pid * 1024, 5 * 1024), :]
        sc.dma(SY, wscrK.ap(), None, reads=[b_all], writes=[b_wscr], slot="wscr",
               in_fn=lambda eng: wsrc(eng, kwinT_all))
        sc.dma(SY, wscrV.ap(), None, reads=[b_all], writes=[b_wscr], slot="wscr",
               in_fn=lambda eng: wsrc(eng, vwin_all))

        def own(ap_seg, a, b):
            return ap_seg

        def evac_gpool(blk0):
            def f(j, base):
                blk = blk0 + j
                for k in range(NS):
                    t0 = k * TW + 16
                    si = 0 if t0 + 128 <= 512 else (1 if t0 >= 512 and t0 + 128 <= 1024 else (2 if t0 >= 1024 else -1))
                    if si >= 0:
                        a = segs[si][0]
                        sc.op(A, lambda e, blk=blk, k=k, si=si, a=a, t0=t0: e.activation(
                            sgpool[:, blk, k * 128:(k + 1) * 128], ps[base + si][:, t0 - a:t0 - a + 128], AF.Silu),
                            [psb[base + si]], [b_sgpool])
                    else:
                        for si2, (a, b) in enumerate(segs):
                            lo, hi = max(t0, a), min(t0 + 128, b)
                            if lo < hi:
                                sc.op(A, lambda e, blk=blk, k=k, si2=si2, a=a, lo=lo, hi=hi, t0=t0: e.activation(
                                    sgpool[:, blk, k * 128 + lo - t0:k * 128 + hi - t0],
                                    ps[base + si2][:, lo - a:hi - a], AF.Silu), [psb[base + si2]], [b_sgpool])
            return f

        fm_proj(1024, evac_gpool(0))
        fm_proj(1536, evac_gpool(4))

        def evac_upool(blk0):
            def f(j, base):
                blk = blk0 + j
                g = blk // 2
                cc = blk % 2
                for si, (a, b) in enumerate(segs):
                    sc.op(A, lambda e, si=si, a=a, b=b: e.activation(uT[:, a:b], ps[base + si][:, 0:b - a], AF.Copy),
                          [psb[base + si]], [b_uT])
                u3 = uT.rearrange("p (k t) -> p k t", k=NS)
                a3 = sA.rearrange("p (k t) -> p k t", k=NS)
                b3 = sB.rearrange("p (k t) -> p k t", k=NS)
                cur, bcur = u3, b_uT
                tmp = [(a3, b_sA), (b3, b_sB)]
                sh = 1
                lvl = 0
                w = 2 << g
                while sh < w:
                    dst, bdst = tmp[lvl % 2]
                    sc.op(V, lambda e, dst=dst, cur=cur, sh=sh: e.tensor_tensor(
                        dst[:, :, 2 * sh - 1:144], cur[:, :, 2 * sh - 1:144], cur[:, :, sh - 1:144 - sh], ALU.add),
                        [bcur], [bdst])
                    cur, bcur = dst, bdst
                    sh *= 2
                    lvl += 1
                p3 = pooled[:, cc, :].rearrange("p (k t) -> p k t", k=NS)
                sc.op(V, lambda e, cur=cur, p3=p3, w=w: e.scalar_tensor_tensor(
                    p3, cur[:, :, 16:144], 1.0 / w, u3[:, :, 16:144], ALU.mult, ALU.subtract),
                    [bcur, b_uT], [b_pooled[cc]])
                sc.op(V, lambda e, cur=cur, g=g: e.tensor_tensor(pfix, cur[:, 0, 16:32], invc[:, g, :], ALU.mult),
                      [bcur, b_const], [b_pfix])
                sc.op(V, lambda e, cc=cc: e.tensor_tensor(pooled[:, cc, 0:16], pfix, uT[:, 16:32], ALU.subtract),
                      [b_pfix, b_uT], [b_pooled[cc]])
                if cc == 1:
                    for db in range(2):
                        oblk = g * 2 + db
                        for half in range(2):
                            pb = 6 + half
                            for c2 in range(2):
                                mm(ps[pb], mixw[:, g, c2, db * 128:(db + 1) * 128],
                                   pooled[:, c2, half * 512:(half + 1) * 512], c2 == 0, c2 == 1,
                                   [b_mixw, b_pooled[0], b_pooled[1]], [psb[pb]])
                            sc.op(V, lambda e, pb=pb, oblk=oblk, half=half: e.scalar_tensor_tensor(
                                ypool[:, oblk, half * 512:(half + 1) * 512], ps[pb], pscale[:, oblk:oblk + 1],
                                sgpool[:, oblk, half * 512:(half + 1) * 512], ALU.mult, ALU.mult),
                                [psb[pb], b_sgpool, b_const], [b_ypool])
            return f

        fm_proj(0, evac_upool(0))
        fm_proj(512, evac_upool(4))

        sc.barrier()
        qs.reset()
        QT = qs.alloc([16, 1024], BF16)
        sgn = qs.alloc([NS, 2048], BF16)

        for qb in range(4):
            tm_proj(2048 + qb * 512, 512, lambda k, pb, qb=qb: rope_tm(
                pb, k, lambda: QT[:, qb * 4:qb * 4 + 4, k * 128:(k + 1) * 128], b_QT))

        for gb in range(4):
            tm_proj(7168 + gb * 512, 512, lambda k, pb, gb=gb: sc.op(
                A, lambda e: e.activation(sgn[:, k, gb * 512:(gb + 1) * 512], ps[pb], AF.Silu), [psb[pb]], [b_sgn[k]]))
        tm_proj(9216, 48, lambda k, pb: sc.op(
            A, lambda e: e.activation(gates[:, k, :], ps[pb][:, 0:48], AF.Sigmoid), [psb[pb]], [b_gates]))

        hT4 = hT.rearrange("p c (k t) -> p c k t", k=NS)
        for c0 in range(0, 16, 4):
            sc.dma(SY, hT_sp.ap().rearrange("p (c k t) -> p c k t", c=16, k=NS)[:, c0:c0 + 4],
                   hT4[:, c0:c0 + 4, :, 16:144], reads=b_hT[c0:c0 + 4], slot="spill")
        sc.dma(SY, yp_sp.ap().rearrange("p (b t) -> p b t", b=8), ypool, reads=[b_ypool], slot="spill")

        if stage == 1:
            sc.barrier()
            sc.dma(SY, dbg["QT"].rearrange("p (a b) -> p a b", a=16), QT, reads=[b_QT], slot="dbg")
            sc.dma(SY, dbg["sgn"].rearrange("p (a b) -> p a b", a=NS), sgn, reads=b_sgn, slot="dbg")
            sc.dma(SY, dbg["gates"].rearrange("p (a b) -> p a b", a=NS), gates, reads=[b_gates], slot="dbg")
            sc.dma(SY, dbg["ypool"].rearrange("p (a b) -> p a b", a=8), ypool, reads=[b_ypool], slot="dbg")
            for nm, loc in [("kselT", kselT_loc), ("kwinT", kwinT_loc), ("vsel", vsel_loc), ("vwin", vwin_loc),
                            ("kcT", kcT_loc), ("vc", vc_loc)]:
                sc.dma(SY, dbg[nm], loc.ap(), slot="dbg")
            sc.barrier()
            sc.emit(block)
            return nc

        sc.barrier()

        ph.reset()
        KselT2 = [ph.alloc([S], BF16) for _ in range(2)]
        Vsel2 = [ph.alloc([64, 130], BF16) for _ in range(2)]
        emat = ph.alloc([S], BF16)
        selcm = ph.alloc([8, 512], BF16)
        wmk = ph.alloc([7, 512], BF16)
        cmpm2 = [ph.alloc([4, 512], BF16) for _ in range(2)]
        b_cmpm = [Buf('cmpm0'), Buf('cmpm1')]
        amat = ph.alloc([4, 128], BF16)
        fmm = ph.alloc([NS, 2, 128], F32)
        kcT = ph.alloc([512], BF16)
        vcx = ph.alloc([4, 130], BF16)
        kwT = [ph.alloc([5, 128], BF16) for _ in range(2)]
        vwx = [ph.alloc([5, 130], BF16) for _ in range(2)]
        NPT = 4
        PT = [ph.alloc([512], BF16) for _ in range(NPT)]
        PTc = ph.alloc([4, 512], BF16)
        imp = ph.alloc([128], F32)
        imp2 = ph.alloc([128], F32)
        m8a = ph.alloc([8], F32)
        m8b = ph.alloc([8], F32)
        mb = ph.alloc([128], BF16)
        mbT = ph.alloc([4, 128], BF16)
        den = ph.alloc([3, 4], F32)
        wgt = ph.alloc([3, 4], F32)
        ocomb = ph.alloc([4, 128], F32)
        b_ksel2, b_vsel2 = [Buf("ksel0"), Buf("ksel1")], [Buf("vsel0"), Buf("vsel1")]
        b_p2c, b_kc, b_vc = Buf("p2c"), Buf("kc"), Buf("vc")
        b_kw = [Buf("kw0"), Buf("kw1")]
        b_vw = [Buf("vw0"), Buf("vw1")]
        b_PT = [Buf("PT%d" % i) for i in range(NPT)]
        b_PTc = [Buf("PTc%d" % i) for i in range(4)]
        b_imp, b_imp2, b_m8a, b_m8b, b_mb, b_mbT = (Buf("imp"), Buf("imp2"), Buf("m8a"), Buf("m8b"), Buf("mb"),
                                                     Buf("mbT"))
        b_den, b_wgt, b_ocomb = Buf("den"), Buf("wgt"), Buf("ocomb")

        b_csm, b_cbig = Buf("csm"), Buf("cbig")
        sc.dma(SY, amat, amat_d.rearrange("p (a b) -> p a b", a=4), writes=[b_csm], slot="csm")
        sc.dma(SY, fmm, fmm_d.rearrange("p (k a b) -> p k a b", k=NS, a=2), writes=[b_csm], slot="csm")
        sc.dma(SY, wmk, wm_d.rearrange("p (a b) -> p a b", a=7), writes=[b_csm], slot="csm")
        sc.op(V, lambda e: e.memset(vcx[:, :, 128:130], 1.0), writes=[b_vc])
        for i in range(2):
            sc.op(V, lambda e, i=i: e.memset(vwx[i][:, :, 128:130], 1.0), writes=[b_vw[i]])

        kc_v = kcT_all.ap().rearrange("(r e) (h n) -> e h r n", r=NC, h=4)
        vc_v = vc_all.ap().rearrange("(r h n) e -> r h n e", r=NC, h=4)

        sbank = [0]
        SBANKS = [0, 1]
        ptc = [0]
        accn = [0]
        accsets = [(2, 3), (4, 5)]
        TB = 7
        wcnt = [0]

        wK_v = wscrK.ap().rearrange("(j k d) n -> d j k n", j=5, k=NS)
        wV_v = wscrV.ap().rearrange("(j k d) n -> d j k n", j=5, k=NS)

        def load_window(h, k):
            i = wcnt[0] % 2
            wcnt[0] += 1
            sc.dma(SY, kwT[i], wK_v[:, :, k, h * 128:(h + 1) * 128], reads=[b_wscr], writes=[b_kw[i]], slot="kw%d" % i)
            sc.dma(SY, vwx[i][:, :, 0:128], wV_v[:, :, k, h * 128:(h + 1) * 128], reads=[b_wscr], writes=[b_vw[i]],
                   slot="vw%d" % i)
            return i

        def branch(Qh, tiles, evac):
            n = len(tiles)
            aset = accsets[accn[0] % 2]
            accn[0] += 1
            state = {}

            def emit_S(t):
                kT, kreads, masks, vr, vreads, ptd = tiles[t]
                sb = SBANKS[sbank[0] % len(SBANKS)]
                sbank[0] += 1
                so = ps[sb].rearrange("p (g q) -> p g q", g=4)
                mm(so, kT, Qh, True, len(masks) == 0, kreads + [b_QT], [psb[sb]])
                for mi, (ml, mr, mreads) in enumerate(masks):
                    mm(ps[sb], ml, mr, False, mi == len(masks) - 1, mreads, [psb[sb]])
                if ptd is None:
                    pi = ptc[0] % NPT
                    ptc[0] += 1
                    dst, bd = PT[pi], b_PT[pi]
                else:
                    dst, bd = ptd
                sc.op(A, lambda e, dst=dst, sb=sb: e.activation(dst, ps[sb], AF.Exp, scale=SCALE), [psb[sb]], [bd])
                state[t] = (dst, bd)

            def emit_PV(t):
                kT, kreads, masks, vr, vreads, ptd = tiles[t]
                dst, bd = state[t]
                for g in range(4):
                    bank = aset[g // 2]
                    o = ps[bank][:, (g % 2) * 129:(g % 2) * 129 + 129]
                    mm(o, dst[:, g * 128:(g + 1) * 128], vr, (t == 0 and g % 2 == 0), t == n - 1,
                       [bd] + vreads, [psb[bank]], inc=(t == n - 1 and g % 2 == 1) or (g == 3), sgc=True)

            emit_S(0)
            for t in range(n):
                if t + 1 < n:
                    emit_S(t + 1)
                emit_PV(t)
            evac(aset)

        def combine(bi, aset, h, k, first, last):
            for half in range(2):
                bank = aset[half]
                dv = ps[bank][:, 128:258:129]
                sc.op(V, lambda e, dv=dv, half=half: e.tensor_scalar(den[:, bi, half * 2:half * 2 + 2], dv, 1e-30, None,
                                                                     ALU.max), [psb[bank]], [b_den])
            sc.op(V, lambda e: e.reciprocal(den[:, bi, :], den[:, bi, :]), [b_den], [b_den])
            gsl = gates[:, k, h * 12 + bi:h * 12 + 12:3]
            sc.op(V, lambda e, gsl=gsl: e.tensor_tensor(wgt[:, bi, :], den[:, bi, :], gsl, ALU.mult),
                  [b_den, b_gates], [b_wgt])
            for g in range(4):
                bank = aset[g // 2]
                o = ps[bank][:, (g % 2) * 129:(g % 2) * 129 + 128]
                if first:
                    sc.op(V, lambda e, o=o, g=g: e.tensor_scalar(ocomb[:, g, :], o, wgt[:, bi, g:g + 1], None, ALU.mult),
                          [psb[bank], b_wgt], [b_ocomb])
                else:
                    sc.op(V, lambda e, o=o, g=g: e.scalar_tensor_tensor(ocomb[:, g, :], o, wgt[:, bi, g:g + 1],
                                                                         ocomb[:, g, :], ALU.mult, ALU.add),
                          [psb[bank], b_wgt, b_ocomb], [b_ocomb])
            if last:
                sl = sgn[:, k, h * 512:(h + 1) * 512]
                sc.op(V, lambda e, sl=sl: e.tensor_tensor(sl, ocomb.rearrange("p g e -> p (g e)"), sl, ALU.mult),
                      [b_ocomb, b_sgn[k]], [b_sgn[k]])

        cmpm_v = cmpm_d.rearrange("p (k a b) -> p k a b", k=NS, a=4)

        def load_kv(h):
            hp = h % 2
            for r in range(NC):
                sc.dma(SY, KselT2[hp][:, r * 1024:(r + 1) * 1024],
                       kselT_all.ap()[r * 128:(r + 1) * 128, h * 1024:(h + 1) * 1024],
                       reads=[b_all], writes=[b_ksel2[hp]], slot="ksel%d" % hp)
                sc.dma(SY, Vsel2[hp][:, r * 8:(r + 1) * 8, :].rearrange("p k e -> p (k e)"),
                       vsel_all.ap()[r * 128:(r + 1) * 128, h * 1040:(h + 1) * 1040],
                       reads=[b_all], writes=[b_vsel2[hp]], slot="vsel%d" % hp)

        stepn = [0]
        for h in range(nh):
            KselT, Vsel = KselT2[h % 2], Vsel2[h % 2]
            b_ksel, b_vsel = b_ksel2[h % 2], b_vsel2[h % 2]
            sc.dma(SY, kcT.rearrange("p (r n) -> p r n", r=NC), kc_v[:, h], reads=[b_all], writes=[b_kc], slot="kc")
            for tc in range(4):
                for r2 in range(2):
                    sc.dma(SY, vcx[r2 * 64:(r2 + 1) * 64, tc, 0:128], vc_v[2 * tc + r2, h], reads=[b_all],
                           writes=[b_vc], slot="vc")
            for k in range(NS):
                Qh = QT[:, 4 * h:4 * h + 4, k * 128:(k + 1) * 128]
                wi = load_window(h, k)
                ci = stepn[0] % 2
                stepn[0] += 1
                sc.dma(SY, cmpm2[ci], cmpm_v[:, k], writes=[b_cmpm[ci]], slot="cmpm%d" % ci)
                if h == 0 and k == 0:
                    sc.dma(SY, emat, emat_d, writes=[b_cbig], slot="cbig")
                    sc.dma(SY, selcm, selcm_d.rearrange("p (a b) -> p a b", a=8), writes=[b_cbig], slot="cbig")
                    load_kv(0)
                if k == 1 and h + 1 < nh:
                    load_kv(h + 1)
                tiles = []
                for tc in range(4):
                    tiles.append((kcT[:, tc * 128:(tc + 1) * 128], [b_kc],
                                  [(ident, cmpm2[ci][:, tc, :], [b_const, b_cmpm[ci]])],
                                  vcx[:, tc, 0:129], [b_vc], (PTc[:, tc, :], b_PTc[tc])))

                def evac_c(aset, h=h, k=k):
                    UB = 6
                    for tc in range(4):
                        for g in range(4):
                            mm(ps[UB][:, g * 128:(g + 1) * 128], PTc[:, tc, g * 128:(g + 1) * 128], amat[:, tc, :],
                               (tc == 0 and g == 0), tc == 3, [b_PTc[tc], b_csm], [psb[UB]],
                               inc=(tc == 3 and g == 3), sgc=True)
                    combine(0, aset, h, k, True, False)
                    sc.op(V, lambda e: e.tensor_scalar(imp, ps[UB][:, 0:128], den[:, 0, 0:1], None, ALU.mult),
                          [psb[UB], b_den], [b_imp])
                    for g in range(1, 4):
                        sc.op(V, lambda e, g=g: e.scalar_tensor_tensor(imp, ps[UB][:, g * 128:(g + 1) * 128],
                                                                        den[:, 0, g:g + 1], imp, ALU.mult, ALU.add),
                              [psb[UB], b_den, b_imp], [b_imp])
                    sc.op(V, lambda e, k=k: e.tensor_tensor(imp, imp, fmm[:, k, 0, :], ALU.mult), [b_imp, b_csm], [b_imp])
                    sc.op(V, lambda e, k=k: e.tensor_tensor(imp, imp, fmm[:, k, 1, :], ALU.add), [b_imp, b_csm], [b_imp])
                    sc.op(V, lambda e: e.max(m8a, imp), [b_imp], [b_m8a])
                    sc.op(V, lambda e: e.match_replace(imp2, m8a, imp, -3.0e38), [b_m8a, b_imp], [b_imp2])
                    sc.op(V, lambda e: e.max(m8b, imp2), [b_imp2], [b_m8b])
                    sc.op(V, lambda e: e.tensor_scalar(mb, imp, m8b[:, 7:8], NEGB, ALU.is_lt, ALU.mult),
                          [b_imp, b_m8b], [b_mb])
                    tv = ps[TB].bitcast(BF16)
                    transp(tv[:, 0:128], mb, [b_mb], [psb[TB]])
                    sc.op(A, lambda e, tv=tv: e.activation(mbT, tv[:, 0:128].unsqueeze(1).broadcast_to([128, 4, 128]),
                                                           AF.Copy), [psb[TB]], [b_mbT])

                branch(Qh, tiles, evac_c)
                tiles = []
                for j in range(5):
                    if k == 0:
                        masks = [(ident, wmk[:, j, :], [b_const, b_csm])]
                    elif j == 0:
                        masks = [(ident, wmk[:, 5, :], [b_const, b_csm])]
                    elif j == 4:
                        masks = [(ident, wmk[:, 6, :], [b_const, b_csm])]
                    else:
                        masks = []
                    tiles.append((kwT[wi][:, j, :], [b_kw[wi]], masks, vwx[wi][:, j, 0:129], [b_vw[wi]], None))
                branch(Qh, tiles, lambda aset, h=h, k=k: combine(2, aset, h, k, False, False))
                tiles = []
                mbT2 = mbT.rearrange("p g q -> p (g q)")
                for t in range(8 * k + 8):
                    masks = [(emat[:, t * 128:(t + 1) * 128], mbT2, [b_cbig, b_mbT])]
                    if t >= 8 * k:
                        masks.append((ident, selcm[:, t - 8 * k, :], [b_const, b_cbig]))
                    pos = (t % 8) * 8 + t // 8
                    tiles.append((KselT[:, pos * 128:(pos + 1) * 128], [b_ksel], masks, Vsel[:, pos, 0:129], [b_vsel],
                                  None))
                branch(Qh, tiles, lambda aset, h=h, k=k: combine(1, aset, h, k, False, True))

        if stage == 2:
            import os
            for _i in range(int(os.environ.get("K_DUMMY_MM", "0"))):
                mm(ps[6][:, 0:8], ident, ident[:, 0:8], True, True, [b_const], [psb[6]], inc=(_i % 64 == 63))
            sc.barrier()
            sc.dma(SY, dbg["og"].rearrange("p (a b) -> p a b", a=NS), sgn, reads=b_sgn, slot="dbg")
            sc.barrier()
            sc.emit(block)
            return nc

        sc.barrier()
        ph.reset()
        qs.reset()
        ogT = qs.alloc([16, 1024], BF16)
        sgn = qs.alloc([NS, 2048], BF16)
        b_ogT = Buf("ogT")
        hTo = ph.alloc([16, 1024], BF16)
        b_hTo = Buf("hTo")
        ypl = ph.alloc([8, 1024], BF16)
        b_ypl = Buf("ypl")
        NW3 = 2
        w3 = [ph.alloc([56, 256], BF16) for _ in range(NW3)]
        b_w3 = [Buf("w3_%d" % i) for i in range(NW3)]
        sg0 = ph.alloc([512], F32)
        sg1 = ph.alloc([512], F32)
        m0 = ph.alloc([512], F32)
        m1 = ph.alloc([512], F32)
        b_sg0, b_sg1, b_m0, b_m1 = Buf("sg0"), Buf("sg1"), Buf("m0"), Buf("m1")
        sc.dma(SY, hTo, hT_sp.ap().rearrange("p (c t) -> p c t", c=16), writes=[b_hTo], slot="p3a")
        sc.dma(SY, ypl, yp_sp.ap().rearrange("p (b t) -> p b t", b=8), writes=[b_ypl], slot="p3a")
        tcount = [0]
        for k in range(NS):
            for c4 in range(4):
                tb = 6 + (tcount[0] % 2)
                tcount[0] += 1
                tv = ps[tb].bitcast(BF16)
                for cc in range(4):
                    c = c4 * 4 + cc
                    transp(tv[:, cc * 128:(cc + 1) * 128], sgn[:, k, c * 128:(c + 1) * 128], [b_sgn[k]], [psb[tb]])
                sc.op(A if (tcount[0] % 2) else V,
                      (lambda e, tv=tv, c4=c4, k=k: e.activation(
                          ogT[:, c4 * 4:c4 * 4 + 4, k * 128:(k + 1) * 128],
                          tv[:, 0:512].rearrange("p (c t) -> p c t", c=4), AF.Copy)) if (tcount[0] % 2) else
                      (lambda e, tv=tv, c4=c4, k=k: e.tensor_copy(
                          ogT[:, c4 * 4:c4 * 4 + 4, k * 128:(k + 1) * 128],
                          tv[:, 0:512].rearrange("p (c t) -> p c t", c=4))),
                      [psb[tb]], [b_ogT])
        sc.barrier()
        mergedT = sgn.rearrange("p k n -> p (k n)").rearrange("p (c t) -> p c t", c=16)
        b_mg = Buf("merged")
        wpo_v = wpo_d.rearrange("(c p) n -> p c n", p=128)
        wno_v = wno_d.rearrange("(c p) n -> p c n", p=128)
        wmg_v = wmg_d.rearrange("(c p) n -> p c n", p=128)
        pcount = [0]
        for ob in range(16):
            i = (ob // 2) % NW3
            o2 = ob % 2
            if o2 == 0:
                cols = slice(ob * 128, ob * 128 + 256)
                sc.dma(G, w3[i][:, 0:8, :], wpo_v[:, :, cols], writes=[b_w3[i]], slot="w3_%d" % i)
                sc.dma(G, w3[i][:, 8:24, :], wno_v[:, :, cols], writes=[b_w3[i]], slot="w3_%d" % i)
                sc.dma(G, w3[i][:, 24:40, :], wmg_v[:, :, cols], writes=[b_w3[i]], slot="w3_%d" % i)
                sc.dma(G, w3[i][:, 40:56, :], wmg_v[:, :, D + ob * 128:D + ob * 128 + 256], writes=[b_w3[i]],
                       slot="w3_%d" % i)
            w3s = w3[i][:, :, o2 * 128:(o2 + 1) * 128]
            for tg in range(2):
                tsl = slice(tg * 512, (tg + 1) * 512)
                pa, pg0, pbk, pg1 = [(pcount[0] * 4 + x) % 6 for x in range(4)]
                pcount[0] += 1
                for c in range(8):
                    mm(ps[pa], w3s[:, c, :], ypl[:, c, tsl], c == 0, c == 7, [b_w3[i], b_ypl], [psb[pa]])
                for c in range(16):
                    mm(ps[pg0], w3s[:, 24 + c, :], hTo[:, c, tsl], c == 0, c == 15, [b_w3[i], b_hTo], [psb[pg0]])
                for c in range(16):
                    mm(ps[pbk], w3s[:, 8 + c, :], ogT[:, c, tsl], c == 0, c == 15, [b_w3[i], b_ogT], [psb[pbk]])
                for c in range(16):
                    mm(ps[pg1], w3s[:, 40 + c, :], hTo[:, c, tsl], c == 0, c == 15, [b_w3[i], b_hTo], [psb[pg1]])
                sc.op(A, lambda e, pg0=pg0, ob=ob: e.activation(sg0, ps[pg0], AF.Sigmoid, bias=bmerge[:, ob:ob + 1]),
                      [psb[pg0], b_const], [b_sg0])
                sc.op(A, lambda e, pg1=pg1, ob=ob: e.activation(sg1, ps[pg1], AF.Sigmoid,
                                                                 bias=bmerge[:, 16 + ob:17 + ob]),
                      [psb[pg1], b_const], [b_sg1])
                sc.op(V, lambda e, pa=pa: e.tensor_tensor(m0, ps[pa], sg0, ALU.mult), [psb[pa], b_sg0], [b_m0])
                sc.op(V, lambda e, pbk=pbk: e.tensor_tensor(m1, ps[pbk], sg1, ALU.mult), [psb[pbk], b_sg1], [b_m1])
                sc.op(V, lambda e, ob=ob, tsl=tsl: e.tensor_tensor(mergedT[:, ob, tsl], m0, m1, ALU.add),
                      [b_m0, b_m1], [b_mg])
        sc.barrier()
        ph.reset()
        wout = ph.alloc([16, D], BF16)
        b_wout = [Buf("wout%d" % c_) for c_ in range(4)]
        fnw = ph.alloc([D], F32)
        xt = [ph.alloc([D], F32) for _ in range(2)]
        b_xt = [Buf("xt0"), Buf("xt1")]
        junk = ph.alloc([D], BF16)
        b_junk = Buf("junk")
        ssq = ph.alloc([2], F32)
        b_ssq = Buf("ssq")
        wout_v = wout_d.rearrange("(c p) n -> p c n", p=128)
        for cg in range(4):
            sc.dma(G, wout[:, :, cg * 512:(cg + 1) * 512], wout_v[:, :, cg * 512:(cg + 1) * 512], writes=[b_wout[cg]],
                   slot="wout%d" % cg)
        sc.dma(SY, fnw, fnw_d, writes=[b_const], slot="const")
        for k in range(NS):
            i = k % 2
            sc.dma(SY, xt[i], xtok_d[k * 128:(k + 1) * 128, :], writes=[b_xt[i]], slot="xt%d" % i)
            for cg in range(4):
                pb = (k * 4 + cg) % 6
                for c in range(16):
                    mm(ps[pb], mergedT[:, c, k * 128:(k + 1) * 128], wout[:, c, cg * 512:(cg + 1) * 512],
                       c == 0, c == 15, [b_mg, b_wout[cg]], [psb[pb]])
                sc.op(V, lambda e, i=i, pb=pb, cg=cg: e.tensor_tensor(xt[i][:, cg * 512:(cg + 1) * 512], ps[pb],
                                                                      xt[i][:, cg * 512:(cg + 1) * 512], ALU.add),
                      [psb[pb], b_xt[i]], [b_xt[i]])
            sc.op(V, lambda e, i=i: e.memset(ssq[:, i:i + 1], 0.0), [], [b_ssq])
            sc.op(A, lambda e, i=i: e.activation(junk, xt[i], AF.Square, accum_out=ssq[:, i:i + 1]),
                  [b_xt[i]], [b_junk, b_ssq])
            sc.op(A, lambda e, i=i: e.activation(ssq[:, i:i + 1], ssq[:, i:i + 1], AF.Sqrt, bias=EPS, scale=1.0 / D),
                  [b_ssq], [b_ssq])
            sc.op(V, lambda e, i=i: e.reciprocal(ssq[:, i:i + 1], ssq[:, i:i + 1]), [b_ssq], [b_ssq])
            sc.op(V, lambda e, i=i: e.scalar_tensor_tensor(xt[i], xt[i], ssq[:, i:i + 1], fnw, ALU.mult, ALU.mult),
                  [b_xt[i], b_ssq, b_const], [b_xt[i]])
            sc.dma(SY, out_d[k * 128:(k + 1) * 128, :], xt[i], reads=[b_xt[i]], slot="xt%d" % i)
        sc.barrier()
        sc.emit(block)
    return nc


def _host_inputs(inp):
    f32 = np.float32
    bf = ml_dtypes.bfloat16
    x = np.asarray(inp["x"], f32)[0]
    xpad = np.zeros((S + 32, D), f32)
    xpad[16:16 + S] = x
    half = 64
    inv = (np.float32(10000.0) ** (-np.arange(half, dtype=f32) / np.float32(half))).astype(f32)
    common = {
        "normw": np.ascontiguousarray(np.asarray(inp["norm_w"], f32)[0].reshape(16, 128).T),
        "w_in": np.ascontiguousarray(np.asarray(inp["w_in"], f32)[0]),
        "pool_mix": np.ascontiguousarray(np.asarray(inp["pool_mix"], f32)[0]),
        "pscale": np.ascontiguousarray(np.asarray(inp["pool_scale"], f32)[0].reshape(8, 128).T),
        "pe_kT": np.ascontiguousarray(np.asarray(inp["cmp_pe_k"], f32)[0].T),
        "pe_vT": np.ascontiguousarray(np.asarray(inp["cmp_pe_v"], f32)[0].T),
        "w1k": np.ascontiguousarray(np.asarray(inp["cmp_w1_k"], f32)[0]),
        "w1v": np.ascontiguousarray(np.asarray(inp["cmp_w1_v"], f32)[0]),
        "w2k": np.ascontiguousarray(np.asarray(inp["cmp_w2_k"], f32)[0]),
        "w2v": np.ascontiguousarray(np.asarray(inp["cmp_w2_v"], f32)[0]),
        "w_pool_out": np.ascontiguousarray(np.asarray(inp["w_pool_out"], f32)[0]),
        "w_nsa_out": np.ascontiguousarray(np.asarray(inp["w_nsa_out"], f32)[0]),
        "w_merge": np.ascontiguousarray(np.asarray(inp["w_merge"], f32)[0]),
        "bmerge": np.ascontiguousarray(np.asarray(inp["b_merge"], f32)[0].reshape(32, 128).T),
        "w_out": np.ascontiguousarray(np.asarray(inp["w_out"], f32)[0]),
        "fnw": np.ascontiguousarray(np.broadcast_to(np.asarray(inp["final_norm_w"], f32)[None, :], (128, D))),
        "ident": np.eye(128, dtype=f32).astype(bf),
    }
    emat = np.zeros((128, S), f32)
    emat[np.arange(S) // 64, np.arange(S)] = 1.0
    common["emat"] = emat.astype(bf)
    kk = np.arange(128)[:, None]
    qq = np.arange(128)[None, :]
    caus = np.where(kk <= qq, 0.0, NEGB).astype(f32)
    upper = np.where(kk > qq, 0.0, NEGB).astype(f32)
    full = np.zeros((128, 128), f32)
    none = np.full((128, 128), NEGB, f32)
    maps = []
    for c in range(NC):
        m = dict(common)
        blocks = [8 * k + c for k in range(NS)]
        xT = np.zeros((D, NT), f32)
        for k, i in enumerate(blocks):
            xT[:, k * TW:(k + 1) * TW] = xpad[128 * i:128 * i + TW].T
        m["xT"] = xT
        m["xtok"] = np.ascontiguousarray(np.concatenate([x[128 * i:128 * i + 128] for i in blocks], 0))
        cs = np.zeros((128, NS, 2, 64), f32)
        for k, i in enumerate(blocks):
            pos = (128 * i + np.arange(128)).astype(f32)
            ang = pos[:, None] * inv[None, :]
            cs[:, k, 0] = np.cos(ang)
            cs[:, k, 1] = np.sin(ang)
        m["ropecs"] = cs.reshape(128, -1)
        cc = np.zeros((128, 2, 64), f32)
        for h2 in range(2):
            for k, i in enumerate(blocks):
                n = 8 * i + np.arange(8)
                pos = (n * 16 + 31).astype(f32)
                ang = pos[:, None] * inv[None, :]
                r0 = h2 * 64 + k * 8
                cc[r0:r0 + 8, 0] = np.cos(ang)
                cc[r0:r0 + 8, 1] = np.sin(ang)
        m["ropecc"] = cc.reshape(128, -1)
        ic = np.zeros((128, 4, 16), f32)
        for g in range(4):
            w = 2 << g
            t = 128 * blocks[0] + np.arange(16)
            ic[:, g, :] = 1.0 / np.minimum(t + 1, w)
        m["invc"] = ic.reshape(128, -1)
        scm = np.zeros((128, 8, 4, 128), f32)
        for j in range(8):
            mk = full if j < c else (caus if j == c else none)
            scm[:, j] = mk[:, None, :]
        m["selcm"] = scm.reshape(128, -1).astype(bf)
        wmm = np.zeros((128, 7, 4, 128), f32)
        for j in range(5):
            t = c - 4 + j
            if t < 0:
                mk = none
            elif j == 0:
                mk = upper
            elif j == 4:
                mk = caus
            else:
                mk = full
            wmm[:, j] = mk[:, None, :]
        wmm[:, 5] = upper[:, None, :]
        wmm[:, 6] = caus[:, None, :]
        m["wm"] = wmm.reshape(128, -1).astype(bf)
        npr = np.arange(512)
        r_ = npr // 64
        k_ = (npr % 64) // 8
        nl_ = npr % 8
        nglob = 8 * (8 * k_ + r_) + nl_
        cend = nglob * 16 + 31
        cm = np.zeros((NS, 512, 128), f32)
        for k, i in enumerate(blocks):
            tpos = 128 * i + np.arange(128)
            valid = (cend[:, None] <= tpos[None, :]) & (nglob[:, None] < 511)
            cm[k] = np.where(valid, 0.0, NEGB)
        cm = cm.reshape(NS, 4, 128, 128)
        cm = np.broadcast_to(cm[:, :, :, None, :], (NS, 4, 128, 4, 128))
        m["cmpm"] = np.ascontiguousarray(cm.transpose(2, 0, 1, 3, 4)).reshape(128, -1).astype(bf)
        am = np.zeros((512, 128), f32)
        for j in range(128):
            for mm_ in range(4):
                for nn in range(2):
                    idx = 4 * j + mm_ - nn
                    if 0 <= idx < 511:
                        am[nglob == idx, j] += 1.0
        m["amat"] = np.ascontiguousarray(am.reshape(4, 128, 128).transpose(1, 0, 2)).reshape(128, -1).astype(bf)
        fm = np.zeros((128, NS, 2, 128), f32)
        jb = np.arange(128)[None, :]
        for k, i in enumerate(blocks):
            tpos = 128 * i + np.arange(128)
            jt = (tpos // 64)[:, None]
            forced = (jb == 0) | (jb == jt) | (jb == jt - 1)
            fut = jb > jt
            M = np.where(forced | fut, 0.0, 1.0)
            B = np.where(fut, -1e30, np.where(forced, 1e6, 0.0))
            fm[:, k, 0] = M
            fm[:, k, 1] = B
        m["fmm"] = fm.reshape(128, -1)
        maps.append(m)
    return maps


_NC_CACHE = {}


def kernel(**inputs):
    maps = _host_inputs(inputs)
    if "nc" not in _NC_CACHE:
        _NC_CACHE["nc"] = build()
    nc = _NC_CACHE["nc"]
    res = run_bass_kernel_spmd(nc, maps, core_ids=list(range(NC)))
    out = np.zeros((S, D), np.float32)
    for c in range(NC):
        o = np.asarray(res.results[c]["out"], np.float32)
        for k in range(NS):
            i = 8 * k + c
            out[128 * i:128 * i + 128] = o[k * 128:(k + 1) * 128]
    return out[None]
```
